# Optimizing a Trainium2 kernel written in Bass

```python
import math
import jax
import jax.numpy as jnp
from jax import lax
import numpy as np

D_MODEL = 1024
BATCH = 2
SEQ = 16384
DEPTH = 4

DA_HEAD_DIM = 128
DA_HEADS = D_MODEL // (2 * DA_HEAD_DIM)
SW_HEAD_DIM = 64
SW_HEADS = D_MODEL // SW_HEAD_DIM
SW_KV_HEADS = SW_HEADS // 8
WINDOW = 128
Q_BLOCK = 128
N_EXPERTS = 32
TOP_K = 4
D_FF = D_MODEL
SWIGLU_LIMIT = 7.0
SWIGLU_ALPHA = 1.702
MOE_BLOCK = 256
LN_EPS = 1e-5
DEEPNORM_ALPHA = (2.0 * DEPTH) ** 0.25
DEEPNORM_BETA = (8.0 * DEPTH) ** -0.25
NEG_INF = -1e30
DA_QK_COLS = DA_HEADS * 2 * DA_HEAD_DIM
DA_V_COLS = DA_HEADS * 2 * DA_HEAD_DIM
SW_Q_COLS = SW_HEADS * SW_HEAD_DIM
SW_KV_COLS = SW_KV_HEADS * SW_HEAD_DIM
D_IN = 2 * DA_QK_COLS + DA_V_COLS + SW_Q_COLS + 2 * SW_KV_COLS + 2 * D_MODEL

kernel_name = 'hybrid_diffattn_swa_moe_deepnorm'


def _alibi_slopes(n):
    return 2.0 ** (-8.0 * jnp.arange(1, n + 1, dtype=jnp.float32) / n)


def _layer_norm(x, g, b):
    xf = x.astype(jnp.float32)
    mu = jnp.mean(xf, axis=-1, keepdims=True)
    var = jnp.mean(jnp.square(xf - mu), axis=-1, keepdims=True)
    y = (xf - mu) * lax.rsqrt(var + LN_EPS) * g.astype(jnp.float32) + b.astype(jnp.float32)
    return y.astype(x.dtype)


def _diff_attention(q, k, v, lam, lam_init, subln_g):
    B, S = q.shape[0], q.shape[1]
    n_qb = S // Q_BLOCK
    slopes = _alibi_slopes(DA_HEADS)
    scale = DA_HEAD_DIM ** -0.5
    q_blocks = q.reshape(B, n_qb, Q_BLOCK, DA_HEADS, 2, DA_HEAD_DIM).transpose(1, 0, 2, 3, 4, 5)
    kpos = jnp.arange(S, dtype=jnp.int32)

    def one_block(args):
        qb, i = args
        s = jnp.einsum('bqhcd,bkhcd->bhcqk', qb, k, preferred_element_type=jnp.float32) * scale
        qpos = i * Q_BLOCK + jnp.arange(Q_BLOCK, dtype=jnp.int32)
        dist = qpos[:, None] - kpos[None, :]
        s = s - slopes[:, None, None, None] * dist.astype(jnp.float32)
        s = jnp.where(dist >= 0, s, NEG_INF)
        p = jax.nn.softmax(s, axis=-1)
        a = p[:, :, 0] - lam * p[:, :, 1]
        return jnp.einsum('bhqk,bkhe->bqhe', a.astype(v.dtype), v)

    o = lax.map(one_block, (q_blocks, jnp.arange(n_qb, dtype=jnp.int32)))
    o = o.transpose(1, 0, 2, 3, 4).reshape(B, S, DA_HEADS, 2 * DA_HEAD_DIM)
    of = o.astype(jnp.float32)
    of = of * lax.rsqrt(jnp.mean(jnp.square(of), axis=-1, keepdims=True) + LN_EPS)
    of = of * subln_g.astype(jnp.float32) * (1.0 - lam_init)
    return of.reshape(B, S, DA_HEADS * 2 * DA_HEAD_DIM).astype(v.dtype)


def _band(t, n_blocks):
    B = t.shape[0]
    tb = t.reshape(B, n_blocks, WINDOW, SW_KV_HEADS, SW_HEAD_DIM)
    prev = jnp.concatenate([jnp.zeros_like(tb[:, :1]), tb[:, :-1]], axis=1)
    return jnp.concatenate([prev, tb], axis=2)


def _sliding_window_attention(q, k, v, sinks):
    B, S = q.shape[0], q.shape[1]
    G = SW_HEADS // SW_KV_HEADS
    nb = S // WINDOW
    qb = q.reshape(B, nb, WINDOW, SW_KV_HEADS, G, SW_HEAD_DIM)
    kk = _band(k, nb)
    vv = _band(v, nb)
    s = jnp.einsum('bnqkgd,bnjkd->bnkgqj', qb, kk, preferred_element_type=jnp.float32) * (SW_HEAD_DIM ** -0.5)
    qi = jnp.arange(WINDOW, dtype=jnp.int32)
    kj = jnp.arange(2 * WINDOW, dtype=jnp.int32)
    dist = qi[:, None] + WINDOW - kj[None, :]
    key_global = jnp.arange(nb, dtype=jnp.int32)[:, None] * WINDOW + kj[None, :] - WINDOW
    valid = ((dist >= 0) & (dist < WINDOW))[None] & (key_global >= 0)[:, None, :]
    slopes = _alibi_slopes(SW_HEADS).reshape(SW_KV_HEADS, G)
    s = s - slopes[:, :, None, None] * dist.astype(jnp.float32)
    s = jnp.where(valid[:, None, None], s, NEG_INF)
    sink = jnp.broadcast_to(sinks.astype(jnp.float32).reshape(SW_KV_HEADS, G)[:, :, None, None], s.shape[:-1] + (1,))
    p = jax.nn.softmax(jnp.concatenate([s, sink], axis=-1), axis=-1)[..., :-1]
    o = jnp.einsum('bnkgqj,bnjkd->bnqkgd', p.astype(v.dtype), vv)
    return o.reshape(B, S, SW_HEADS * SW_HEAD_DIM)


def _hybrid_mixer(x, w_in, w_o, lambda_qk, subln_g, sinks, layer):
    B, S, _ = x.shape
    proj = x @ w_in
    offs = np.cumsum([DA_QK_COLS, DA_QK_COLS, DA_V_COLS, SW_Q_COLS, SW_KV_COLS, SW_KV_COLS, D_MODEL]).tolist()
    qa, ka, va, qb, kb, vb, ga, gb = jnp.split(proj, offs, axis=-1)
    qa = qa.reshape(B, S, DA_HEADS, 2, DA_HEAD_DIM)
    ka = ka.reshape(B, S, DA_HEADS, 2, DA_HEAD_DIM)
    va = va.reshape(B, S, DA_HEADS, 2 * DA_HEAD_DIM)
    lam_init = 0.8 - 0.6 * math.exp(-0.3 * layer)
    lq = lambda_qk.astype(jnp.float32)
    lam = jnp.exp(jnp.sum(lq[0] * lq[1])) - jnp.exp(jnp.sum(lq[2] * lq[3])) + lam_init
    o_a = _diff_attention(qa, ka, va, lam, lam_init, subln_g)
    o_b = _sliding_window_attention(
        qb.reshape(B, S, SW_HEADS, SW_HEAD_DIM),
        kb.reshape(B, S, SW_KV_HEADS, SW_HEAD_DIM),
        vb.reshape(B, S, SW_KV_HEADS, SW_HEAD_DIM),
        sinks)
    merged = jax.nn.sigmoid(ga) * o_a + jax.nn.sigmoid(gb) * o_b
    return merged @ w_o


def _moe(x, w_router, b_router, w_up, b_up, w_down, b_down):
    B, S, D = x.shape
    T = B * S
    TK = T * TOP_K
    xt = x.reshape(T, D)
    logits = jnp.dot(xt, w_router, preferred_element_type=jnp.float32) + b_router.astype(jnp.float32)
    top_val, top_idx = lax.top_k(logits, TOP_K)
    gates = jax.nn.softmax(top_val, axis=-1)
    flat_e = top_idx.reshape(TK).astype(jnp.int32)
    flat_tok = jnp.arange(TK, dtype=jnp.int32) // TOP_K
    flat_gate = gates.reshape(TK)
    order = jnp.argsort(flat_e)
    e_s = flat_e[order]
    tok_s = flat_tok[order]
    gate_s = flat_gate[order]
    counts = jnp.bincount(flat_e, length=N_EXPERTS).astype(jnp.int32)
    padded = (counts + MOE_BLOCK - 1) // MOE_BLOCK * MOE_BLOCK
    pad_end = jnp.cumsum(padded)
    pad_start = pad_end - padded
    start = jnp.cumsum(counts) - counts
    dest = pad_start[e_s] + jnp.arange(TK, dtype=jnp.int32) - start[e_s]
    n_blocks = (TK + N_EXPERTS * (MOE_BLOCK - 1) + MOE_BLOCK - 1) // MOE_BLOCK
    n_rows = n_blocks * MOE_BLOCK
    buf_tok = jnp.zeros((n_rows,), jnp.int32).at[dest].set(tok_s)
    buf_gate = jnp.zeros((n_rows,), jnp.float32).at[dest].set(gate_s)
    block_expert = jnp.minimum(
        jnp.searchsorted(pad_end, jnp.arange(n_blocks, dtype=jnp.int32) * MOE_BLOCK, side='right'),
        N_EXPERTS - 1).astype(jnp.int32)
    xb = xt[buf_tok].reshape(n_blocks, MOE_BLOCK, D)

    def expert_block(args):
        xblk, e = args
        h = xblk @ w_up[e] + b_up[e]
        gate = jnp.minimum(h[..., ::2], SWIGLU_LIMIT)
        up = jnp.clip(h[..., 1::2], -SWIGLU_LIMIT, SWIGLU_LIMIT)
        act = (up + 1.0) * (gate * jax.nn.sigmoid(SWIGLU_ALPHA * gate))
        return act @ w_down[e] + b_down[e]

    yb = lax.map(expert_block, (xb, block_expert)).reshape(n_rows, D)
    y = jnp.zeros((T, D), x.dtype).at[buf_tok].add(yb * buf_gate[:, None].astype(x.dtype))
    return y.reshape(B, S, D)


def setup_inputs(seed: int = 0) -> dict:
    key = jax.random.key(seed)
    ks = jax.random.split(key, 14)
    f32 = jnp.float32
    x = jax.random.normal(ks[0], (BATCH, SEQ, D_MODEL), f32)
    col_scale = jnp.concatenate([
        jnp.ones((2 * DA_QK_COLS,), f32),
        jnp.full((DA_V_COLS,), DEEPNORM_BETA, f32),
        jnp.ones((SW_Q_COLS + SW_KV_COLS,), f32),
        jnp.full((SW_KV_COLS,), DEEPNORM_BETA, f32),
        jnp.ones((2 * D_MODEL,), f32)])
    w_in = jax.random.normal(ks[1], (DEPTH, D_MODEL, D_IN), f32) * (D_MODEL ** -0.5) * col_scale
    w_o = jax.random.normal(ks[2], (DEPTH, D_MODEL, D_MODEL), f32) * (D_MODEL ** -0.5) * DEEPNORM_BETA
    lambda_qk = jax.random.normal(ks[3], (DEPTH, 4, DA_HEAD_DIM), f32) * 0.1
    subln_g = 1.0 + 0.02 * jax.random.normal(ks[4], (DEPTH, 2 * DA_HEAD_DIM), f32)
    sinks = 0.5 * jax.random.normal(ks[5], (DEPTH, SW_HEADS), f32)
    ln_g = 1.0 + 0.02 * jax.random.normal(ks[6], (DEPTH, 2, D_MODEL), f32)
    ln_b = 0.02 * jax.random.normal(ks[7], (DEPTH, 2, D_MODEL), f32)
    w_router = jax.random.normal(ks[8], (DEPTH, D_MODEL, N_EXPERTS), f32) * (D_MODEL ** -0.5)
    b_router = 0.01 * jax.random.normal(ks[9], (DEPTH, N_EXPERTS), f32)
    w_up = jax.random.normal(ks[10], (DEPTH, N_EXPERTS, D_MODEL, 2 * D_FF), f32) * (D_MODEL ** -0.5) * DEEPNORM_BETA
    b_up = 0.01 * jax.random.normal(ks[11], (DEPTH, N_EXPERTS, 2 * D_FF), f32)
    w_down = jax.random.normal(ks[12], (DEPTH, N_EXPERTS, D_FF, D_MODEL), f32) * (D_FF ** -0.5) * DEEPNORM_BETA
    b_down = 0.01 * jax.random.normal(ks[13], (DEPTH, N_EXPERTS, D_MODEL), f32)
    return {'x': x, 'w_in': w_in, 'w_o': w_o, 'lambda_qk': lambda_qk, 'subln_g': subln_g,
            'sinks': sinks, 'ln_g': ln_g, 'ln_b': ln_b, 'w_router': w_router, 'b_router': b_router,
            'w_up': w_up, 'b_up': b_up, 'w_down': w_down, 'b_down': b_down}


def reference(x, w_in, w_o, lambda_qk, subln_g, sinks, ln_g, ln_b, w_router, b_router, w_up, b_up, w_down, b_down):
    for l in range(DEPTH):
        mix = _hybrid_mixer(x, w_in[l], w_o[l], lambda_qk[l], subln_g[l], sinks[l], l)
        h = _layer_norm(DEEPNORM_ALPHA * x + mix, ln_g[l, 0], ln_b[l, 0])
        ffn = _moe(h, w_router[l], b_router[l], w_up[l], b_up[l], w_down[l], b_down[l])
        x = _layer_norm(DEEPNORM_ALPHA * h + ffn, ln_g[l, 1], ln_b[l, 1])
    return x
```

```python
import math
import numpy as np
import concourse.bass as bass
import concourse.mybir as mybir
from concourse.bass_utils import run_bass_kernel_spmd

F32 = mybir.dt.float32
BF16 = mybir.dt.bfloat16
U32 = mybir.dt.uint32
I32 = mybir.dt.int32
ALU = mybir.AluOpType
AF = mybir.ActivationFunctionType
AX = mybir.AxisListType

D_MODEL = 1024
BATCH = 2
SEQ = 16384
DEPTH = 4
N_EXPERTS = 32
TOP_K = 4
LN_EPS = 1e-5
DEEPNORM_ALPHA = (2.0 * DEPTH) ** 0.25
NEG_BIG = -30000.0

ENGS = ("pe", "act", "dve", "pool", "sp")


def _freeze(fn):
    import types
    if fn is None or fn.__closure__ is None:
        return fn
    cells = []
    for c in fn.__closure__:
        try:
            cells.append(types.CellType(c.cell_contents))
        except ValueError:
            cells.append(c)
    return types.FunctionType(fn.__code__, fn.__globals__, fn.__name__, fn.__defaults__, tuple(cells))


class KB:
    def __init__(self, nc, same_engine_sync=True):
        self.nc = nc
        self.same_engine_sync = same_engine_sync
        self.ops = {e: [] for e in ENGS}
        self.esem = {e: nc.alloc_semaphore("es_" + e) for e in ("pe", "act", "dve", "pool")}
        self.ecnt = {e: 0 for e in ("pe", "act", "dve", "pool")}
        self.dsem = {}
        self.last_w = {}
        self.readers = {}
        self.known = {e: {} for e in ENGS}
        self.n_dma_sems = 0

    def _need(self, eng, tok, waits):
        if tok is None:
            return
        sid, sem, val, teng = tok
        if teng == eng and eng == "pe":
            return
        if teng == eng and not self.same_engine_sync:
            return
        if self.known[eng].get(sid, 0) >= val:
            return
        for w in waits:
            if w[0] == sid:
                if w[2] < val:
                    w[2] = val
                return
        waits.append([sid, sem, val])

    def _deps(self, eng, reads, writes):
        waits = []
        for k in reads:
            self._need(eng, self.last_w.get(k), waits)
        for k in writes:
            self._need(eng, self.last_w.get(k), waits)
            for t in self.readers.get(k, ()):
                self._need(eng, t, waits)
        for sid, sem, val in waits:
            self.known[eng][sid] = val
        return waits

    def _commit(self, tok, reads, writes):
        for k in reads:
            lst = self.readers.setdefault(k, [])
            for i, t in enumerate(lst):
                if t[0] == tok[0]:
                    lst[i] = tok
                    break
            else:
                lst.append(tok)
        for k in writes:
            self.last_w[k] = tok
            self.readers[k] = []

    def op(self, eng, fn, reads=(), writes=()):
        fn = _freeze(fn)
        waits = self._deps(eng, reads, writes)
        self.ecnt[eng] += 1
        val = self.ecnt[eng]
        sem = self.esem[eng]
        tok = (id(sem), sem, val, eng)
        self.ops[eng].append((waits, fn, sem, 1))
        self._commit(tok, reads, writes)
        return tok

    def dma(self, eng, fn, reads=(), writes=(), semkey=None):
        fn = _freeze(fn)
        waits = self._deps(eng, reads, writes)
        if semkey is None:
            semkey = writes[0] if writes else reads[0]
        ent = self.dsem.get(semkey)
        if ent is None:
            ent = [self.nc.alloc_semaphore("ds%d" % self.n_dma_sems), 0]
            self.n_dma_sems += 1
            self.dsem[semkey] = ent
        ent[1] += 16
        sem, val = ent
        tok = (id(sem), sem, val, "dma")
        self.ops[eng].append((waits, fn, sem, 16))
        self._commit(tok, reads, writes)
        return tok

    def final_wait(self, eng, keys):
        waits = []
        for k in keys:
            self._need(eng, self.last_w.get(k), waits)
            for t in self.readers.get(k, ()):
                self._need(eng, t, waits)
        for sid, sem, val in waits:
            self.known[eng][sid] = val
        self.ops[eng].append((waits, None, None, 0))

    def emit(self):
        nc = self.nc
        ops = self.ops
        with nc.Block() as block:
            def run(e, engobj):
                for waits, fn, sem, inc in ops[e]:
                    for sid, wsem, val in waits:
                        engobj.wait_ge(wsem, val)
                    if fn is not None:
                        fn(engobj).then_inc(sem, inc)

            @block.tensor
            def _(eng):
                run("pe", eng)

            @block.scalar
            def _(eng):
                run("act", eng)

            @block.vector
            def _(eng):
                run("dve", eng)

            @block.gpsimd
            def _(eng):
                run("pool", eng)

            @block.sync
            def _(eng):
                run("sp", eng)


def _bf16_round(a):
    a = np.ascontiguousarray(a, dtype=np.float32)
    u = a.view(np.uint32).astype(np.uint64)
    r = ((u + 0x7FFF + ((u >> 16) & 1)) >> 16) << 16
    return r.astype(np.uint32).view(np.float32)


def attn_consts(head):
    kp = np.arange(128, dtype=np.float64)
    ident = np.eye(128, dtype=np.float32)
    maskA = np.zeros((128, 2, 2, 2, 128), np.float32)
    tri = (kp[:, None] > kp[None, :]).astype(np.float32) * NEG_BIG
    maskA[:, 0, :, 0, :] = tri[:, None, :]
    maskA[:, 1, :, 0, :] = NEG_BIG
    maskA[:, 1, :, 1, :] = tri[:, None, :]
    slope_a = 2.0 ** (-8.0 * (head + 1) / 4.0)
    m = np.arange(130, dtype=np.float64)
    biasA = (slope_a * (kp[:, None] + 128.0 * (m[None, :] - 128.0))).astype(np.float32)
    biasB = np.zeros((128, 2, 2, 4, 128), np.float32)
    ql = kp
    for g in range(4):
        slope = 2.0 ** (-8.0 * (4 * head + g + 1) / 16.0)
        dist_prev = ql[None, :] + 128.0 - kp[:, None]
        vprev = np.where(kp[:, None] > ql[None, :], -8.0 * slope * dist_prev, NEG_BIG)
        dist_cur = ql[None, :] - kp[:, None]
        vcur = np.where(kp[:, None] <= ql[None, :], -8.0 * slope * dist_cur, NEG_BIG)
        for t, v in enumerate((vprev, vcur)):
            v32 = v.astype(np.float32)
            hi = _bf16_round(v32)
            lo = (v32 - hi).astype(np.float32)
            biasB[:, t, 0, g, :] = hi
            biasB[:, t, 1, g, :] = lo
    return {
        "c_ident": ident,
        "c_maskA": maskA.reshape(128, 2 * 512),
        "c_biasA": biasA,
        "c_biasB": biasB.reshape(128, 4 * 512),
    }


def emit_attn(nc, kb, S, lam_init, io, pfx="a"):
    NSB = S // 256
    NCH = S // 128
    scaleA = 128.0 ** -0.5

    def sb(name, shape, dt):
        return nc.alloc_sbuf_tensor(pfx + name, shape, dt)

    def ps(name):
        return nc.alloc_psum_tensor(pfx + name, [128, 512], F32)

    K = lambda s: pfx + s

    ident = sb("ident", [128, 128], BF16)
    maskA = sb("maskA", [128, 2, 512], BF16)
    biasA = sb("biasA", [128, 130], F32)
    biasB = sb("biasB", [128, 2, 2, 512], BF16)
    wq = sb("wq", [128, 8, 256], BF16)
    wk = sb("wk", [128, 8, 256], BF16)
    wv = sb("wv", [128, 8, 256], BF16)
    wqb = sb("wqb", [128, 8, 256], BF16)
    wkb2 = sb("wkb2", [128, 8, 128], BF16)
    wvb = sb("wvb", [128, 8, 64], BF16)
    lq = sb("lq", [128, 512], F32)
    subg = sb("subg", [128, 256], F32)
    gsub = sb("gsub", [128, 256], F32)
    sinks = sb("sinks", [128, 4], F32)
    expsink = sb("expsink", [128, 4], F32)
    s12 = sb("s12", [128, 2], F32)
    e12 = sb("e12", [128, 2], F32)
    lamt = sb("lamt", [128, 1], F32)
    neglam = sb("neglam", [128, 1], F32)
    junk = sb("junk", [128, 256], F32)

    kT = sb("kT", [128, 2, S], BF16)
    va = sb("va", [128, NCH, 257], BF16)
    xt = [sb("xt%d" % i, [128, 8, 256], BF16) for i in range(2)]
    qT = sb("qT", [128, 2, 2, 128], BF16)
    qbz = sb("qbz", [128, 2, 4, 128], BF16)
    kbT = sb("kbT", [128, 4, 128], BF16)
    vb = sb("vb", [128, 4, 65], BF16)
    pt = [sb("pt%d" % i, [128, 512], BF16) for i in range(2)]
    t1 = sb("t1", [128, 256], F32)
    osb = sb("osb", [128, 256], F32)
    oa_sb = [sb("oa_sb%d" % i, [128, 256], F32) for i in range(2)]
    ob_sb = [sb("ob_sb%d" % i, [128, 256], F32) for i in range(2)]
    r0 = sb("r0", [128, 1], F32)
    r1 = sb("r1", [128, 1], F32)
    ss = sb("ss", [128, 1], F32)
    lnv = sb("lnv", [128, 1], F32)
    rstd = sb("rstd", [128, 1], F32)
    lt = sb("lt", [128, 4], F32)
    rl = sb("rl", [128, 4], F32)

    acc = [[ps("acc%d%d" % (c, s)) for s in range(2)] for c in range(2)]
    st = [ps("st%d" % i) for i in range(2)]
    ppb = [ps("pp0"), ps("pp1")]
    obps = ppb[1]

    op, dma = kb.op, kb.dma
    import os
    dbg = int(os.environ.get("ATT_DBG", "9"))

    def cast_load(dst, dst_key, src_ap):
        dma("pool", lambda e: e.dma_start(out=dst, in_=src_ap), writes=[dst_key])

    cast_load(ident[:], K("ident"), io["c_ident"])
    cast_load(maskA[:], K("maskA"), io["c_maskA"].rearrange("p (d n) -> p d n", d=2))
    dma("sp", lambda e: e.dma_start(out=biasA[:], in_=io["c_biasA"]), writes=[K("biasA")])
    cast_load(biasB[:], K("biasB"), io["c_biasB"].rearrange("p (t h n) -> p t h n", t=2, h=2))
    for wt, nm in ((wq, "wq"), (wk, "wk"), (wv, "wv"), (wqb, "wqb"), (wkb2, "wkb2"), (wvb, "wvb")):
        cast_load(wt[:], K(nm), io[nm].rearrange("(kc p) n -> p kc n", p=128))
    dma("sp", lambda e: e.dma_start(out=lq[:], in_=io["lamqk"].partition_broadcast(128)), writes=[K("lq")])
    dma("sp", lambda e: e.dma_start(out=subg[:], in_=io["subg"].partition_broadcast(128)), writes=[K("subg")])
    dma("sp", lambda e: e.dma_start(out=sinks[:], in_=io["sinks4"].partition_broadcast(128)), writes=[K("sinks")])

    for i in range(2):
        op("dve", lambda e, i=i: e.scalar_tensor_tensor(
            out=junk[:, 0:128], in0=lq[:, 256 * i:256 * i + 128], scalar=1.0,
            in1=lq[:, 256 * i + 128:256 * i + 256], op0=ALU.mult, op1=ALU.mult,
            accum_out=s12[:, i:i + 1]), reads=[K("lq")], writes=[K("junk"), K("s12")])
    op("act", lambda e: e.activation(out=e12[:], in_=s12[:], func=AF.Exp), reads=[K("s12")], writes=[K("e12")])
    op("dve", lambda e: e.tensor_tensor(out=lamt[:], in0=e12[:, 1:2], in1=e12[:, 0:1], op=ALU.subtract),
       reads=[K("e12")], writes=[K("lamt")])
    op("dve", lambda e: e.tensor_scalar(out=neglam[:], in0=lamt[:], scalar1=-float(lam_init), scalar2=None,
                                        op0=ALU.add), reads=[K("lamt")], writes=[K("neglam")])
    op("dve", lambda e: e.tensor_scalar(out=gsub[:], in0=subg[:], scalar1=float(1.0 - lam_init), scalar2=None,
                                        op0=ALU.mult), reads=[K("subg")], writes=[K("gsub")])
    op("act", lambda e: e.activation(out=expsink[:], in_=sinks[:], func=AF.Exp), reads=[K("sinks")],
       writes=[K("expsink")])
    op("dve", lambda e: e.memset(va[:, :, 256:257], 1.0), writes=[K("va_ones")])
    op("dve", lambda e: e.memset(vb[:, :, 64:65], 1.0), writes=[K("vb_ones")])
    op("dve", lambda e: e.memset(qbz[:], 0.0), writes=[K("qbz")])

    xT_v = io["xT"].rearrange("(kc p) s -> p kc s", p=128)

    def load_xt(I):
        b = I % 2
        dma("pool", lambda e: e.dma_start(out=xt[b][:], in_=xT_v[:, :, I * 256:(I + 1) * 256]),
            writes=[K("xt%d" % b)])

    state = {"pp": 0, "st": 0}

    def proj_group(mm_list, evac_dst, evac_keys, ncols, evacs=None):
        h = state["pp"] % 2
        state["pp"] += 1
        dstp = ppb[h][:, 0:ncols]
        n = len(mm_list)
        for i, (l, r, rk) in enumerate(mm_list):
            op("pe", lambda e, l=l, r=r, i=i: e.matmul(dstp, lhsT=l, rhs=r, start=(i == 0), stop=(i == n - 1)),
               reads=rk, writes=[K("pp%d" % h)])
        if evacs is None:
            evacs = [(evac_dst, dstp)]
        else:
            evacs = [(d_, f_(ppb[h])) for d_, f_ in evacs]
        for d_, s_ in evacs:
            op("dve", lambda e, d_=d_, s_=s_: e.tensor_copy(out=d_, in_=s_), reads=[K("pp%d" % h)], writes=evac_keys)

    def projections(I):
        b = I % 2
        xk = K("xt%d" % b)
        x = xt[b]
        for c in range(2):
            proj_group([(wq[:, kc, c * 128:(c + 1) * 128], x[:, kc, :], [K("wq"), xk]) for kc in range(8)],
                       qT[:, c, :, :].rearrange("p s q -> p (s q)"), [K("qT%d" % c)], 256)
        for c in range(2):
            proj_group([(wk[:, kc, c * 128:(c + 1) * 128], x[:, kc, :], [K("wk"), xk]) for kc in range(8)],
                       kT[:, c, I * 256:(I + 1) * 256], [K("kT%d_%d" % (c, I))], 256)
        for s in range(2):
            proj_group([(x[:, kc, s * 128:(s + 1) * 128], wv[:, kc, :], [K("wv"), xk]) for kc in range(8)],
                       va[:, 2 * I + s, 0:256], [K("va_%d" % (2 * I + s))], 256)
        for p in range(2):
            evs = []
            for half in range(2):
                g = 2 * p + half
                evs.append((qbz[half * 64:(half + 1) * 64, :, g, :],
                            lambda t, half=half: t[half * 64:(half + 1) * 64, 0:256].rearrange("p (s q) -> p s q", s=2)))
            proj_group([(wqb[:, kc, p * 128:(p + 1) * 128], x[:, kc, :], [K("wqb"), xk]) for kc in range(8)],
                       None, [K("qbz")], 256, evacs=evs)
        s0 = (2 * I) % 4
        proj_group([(wkb2[:, kc, :], x[:, kc, :], [K("wkb2"), xk]) for kc in range(8)],
                   kbT[:, s0:s0 + 2, :].rearrange("p s q -> p (s q)"), [K("kbT%d" % s0), K("kbT%d" % (s0 + 1))], 256)
        for s in range(2):
            proj_group([(x[:, kc, s * 128:(s + 1) * 128], wvb[:, kc, :], [K("wvb"), xk]) for kc in range(8)],
                       vb[:, s0 + s, 0:64], [K("vb%d" % (s0 + s))], 64)

    def diff_attn(I):
        nch = 2 * I + 2

        def qk(j):
            b = state["st"] % 2
            state["st"] += 1
            stv = st[b][:].rearrange("p (c s q) -> p c s q", c=2, s=2)
            d = j - 2 * I
            diag = d >= 0
            for c in range(2):
                op("pe", lambda e, c=c: e.matmul(stv[:, c, :, :], lhsT=kT[:, c, j * 128:(j + 1) * 128],
                                                 rhs=qT[:, c, :, :], start=(c == 0), stop=(c == 1 and not diag)),
                   reads=[K("kT%d_%d" % (c, j // 2)), K("qT%d" % c)], writes=[K("st%d" % b)])
            if diag:
                op("pe", lambda e: e.matmul(st[b][:], lhsT=ident[:], rhs=maskA[:, d, :], start=False, stop=True),
                   reads=[K("ident"), K("maskA")], writes=[K("st%d" % b)])
            return b

        def ex(j, b):
            stv = st[b][:].rearrange("p (c s q) -> p c s q", c=2, s=2)
            ptv = pt[b][:].rearrange("p (c s q) -> p c s q", c=2, s=2)
            for s in range(2):
                m = (j - 2 * I) - s + 128
                op("act", lambda e, s=s, m=m: e.activation(out=ptv[:, :, s, :], in_=stv[:, :, s, :], func=AF.Exp,
                                                           bias=biasA[:, m:m + 1], scale=scaleA),
                   reads=[K("st%d" % b), K("biasA")], writes=[K("pt%d_%d" % (b, s))])

        def av(j, b):
            ptv = pt[b][:].rearrange("p (c s q) -> p c s q", c=2, s=2)
            for c in range(2):
                for s in range(2):
                    op("pe", lambda e, c=c, s=s: e.matmul(acc[c][s][:, 0:257], lhsT=ptv[:, c, s, :], rhs=va[:, j, :],
                                                          start=(j == 0), stop=(j == nch - 1)),
                       reads=[K("pt%d_%d" % (b, s)), K("va_%d" % j), K("va_ones")], writes=[K("acc%d%d" % (c, s))])

        bufs = {0: qk(0)}
        for j in range(nch):
            if j + 1 < nch:
                bufs[j + 1] = qk(j + 1)
            ex(j, bufs[j])
            av(j, bufs[j])

    def diff_final(I):
        for s in range(2):
            diff_final_s(I, s)

    def diff_final_s(I, s):
        if True:
            a0, a1 = acc[0][s], acc[1][s]
            ob_ = oa_sb[s]
            op("dve", lambda e: e.reciprocal(out=r0[:], in_=a0[:, 256:257]), reads=[K("acc0%d" % s)], writes=[K("r0")])
            op("dve", lambda e: e.reciprocal(out=r1[:], in_=a1[:, 256:257]), reads=[K("acc1%d" % s)], writes=[K("r1")])
            op("dve", lambda e: e.tensor_tensor(out=r1[:], in0=r1[:], in1=neglam[:], op=ALU.mult),
               reads=[K("r1"), K("neglam")], writes=[K("r1")])
            op("dve", lambda e: e.tensor_scalar(out=t1[:], in0=a1[:, 0:256], scalar1=r1[:, 0:1], scalar2=None,
                                                op0=ALU.mult), reads=[K("acc1%d" % s), K("r1")], writes=[K("t1")])
            op("dve", lambda e: e.scalar_tensor_tensor(out=osb[:], in0=a0[:, 0:256], scalar=r0[:, 0:1], in1=t1[:],
                                                       op0=ALU.mult, op1=ALU.add),
               reads=[K("acc0%d" % s), K("r0"), K("t1")], writes=[K("osb")])
            op("dve", lambda e: e.scalar_tensor_tensor(out=junk[:], in0=osb[:], scalar=1.0, in1=osb[:],
                                                       op0=ALU.mult, op1=ALU.mult, accum_out=ss[:]),
               reads=[K("osb")], writes=[K("junk"), K("ss")])
            op("act", lambda e: e.activation(out=lnv[:], in_=ss[:], func=AF.Ln, bias=LN_EPS_TILE[0][:, 0:1],
                                             scale=1.0 / 256.0), reads=[K("ss"), K("epst")], writes=[K("lnv")])
            op("act", lambda e: e.activation(out=rstd[:], in_=lnv[:], func=AF.Exp, scale=-0.5),
               reads=[K("lnv")], writes=[K("rstd")])
            op("dve", lambda e, ob_=ob_: e.scalar_tensor_tensor(out=ob_[:], in0=osb[:], scalar=rstd[:, 0:1],
                                                                in1=gsub[:], op0=ALU.mult, op1=ALU.mult),
               reads=[K("osb"), K("rstd"), K("gsub")], writes=[K("oa_sb%d" % s)])
            r0_ = I * 256 + s * 128
            dma("sp", lambda e, ob_=ob_, r0_=r0_: e.dma_start(out=io["oa"][r0_:r0_ + 128, :], in_=ob_[:]),
                reads=[K("oa_sb%d" % s)], writes=[K("oa_out")], semkey=K("oa_sb%d" % s))

    def swa(I):
        for sblk in range(2):
            swa_block(I, sblk)

    def swa_block(I, sblk):
        if True:
            n = 2 * I + sblk
            slot = n % 4
            chunks = []
            if n > 0:
                chunks.append(((n - 1) % 4, 0))
            chunks.append((slot, 1))
            bl = []
            for (cs, t) in chunks:
                b = state["st"] % 2
                state["st"] += 1
                bl.append(b)
                op("pe", lambda e, cs=cs, b=b: e.matmul(
                    st[b][:].rearrange("p (g q) -> p g q", g=4), lhsT=kbT[:, cs, :],
                    rhs=qbz[:, sblk, :, :], start=True, stop=False),
                   reads=[K("kbT%d" % cs), K("qbz")], writes=[K("st%d" % b)])
                for hl in range(2):
                    op("pe", lambda e, t=t, hl=hl, b=b: e.matmul(st[b][:], lhsT=ident[:], rhs=biasB[:, t, hl, :],
                                                                start=False, stop=(hl == 1)),
                       reads=[K("ident"), K("biasB")], writes=[K("st%d" % b)])
                op("act", lambda e, b=b: e.activation(out=pt[b][:], in_=st[b][:], func=AF.Exp, scale=0.125),
                   reads=[K("st%d" % b)], writes=[K("pt%d_0" % b), K("pt%d_1" % b)])
            if dbg < 5:
                return
            for g in range(4):
                for ci, (cs, t) in enumerate(chunks):
                    b = bl[ci]
                    op("pe", lambda e, g=g, cs=cs, b=b, ci=ci: e.matmul(
                        obps[:, g * 65:(g + 1) * 65], lhsT=pt[b][:, g * 128:(g + 1) * 128], rhs=vb[:, cs, :],
                        start=(ci == 0), stop=(ci == len(chunks) - 1)),
                       reads=[K("pt%d_0" % b), K("pt%d_1" % b), K("vb%d" % cs), K("vb_ones")], writes=[K("pp1")])
            if dbg < 6:
                return
            obv = obps[:, 0:260].rearrange("p (g e) -> p g e", g=4)
            op("dve", lambda e: e.tensor_tensor(out=lt[:], in0=obv[:, :, 64], in1=expsink[:], op=ALU.add),
               reads=[K("pp1"), K("expsink")], writes=[K("lt")])
            op("dve", lambda e: e.reciprocal(out=rl[:], in_=lt[:]), reads=[K("lt")], writes=[K("rl")])
            ob_ = ob_sb[sblk]
            for g in range(4):
                op("dve", lambda e, g=g, ob_=ob_: e.tensor_scalar(out=ob_[:, g * 64:(g + 1) * 64], in0=obv[:, g, 0:64],
                                                                  scalar1=rl[:, g:g + 1], scalar2=None, op0=ALU.mult),
                   reads=[K("pp1"), K("rl")], writes=[K("ob_sb%d" % sblk)])
            dma("sp", lambda e, ob_=ob_, n=n: e.dma_start(out=io["ob"][n * 128:(n + 1) * 128, :], in_=ob_[:]),
                reads=[K("ob_sb%d" % sblk)], writes=[K("ob_out")], semkey=K("ob_sb%d" % sblk))

    epst = sb("epst", [128, 1], F32)
    LN_EPS_TILE = [epst]
    op("dve", lambda e: e.memset(epst[:], LN_EPS), writes=[K("epst")])

    if dbg >= 1:
        load_xt(0)
    for I in range(NSB):
        if dbg >= 1 and I + 1 < NSB:
            load_xt(I + 1)
        if dbg >= 2:
            projections(I)
        if dbg >= 3:
            diff_attn(I)
        if dbg >= 4:
            swa(I)
        if dbg >= 7:
            diff_final(I)
    outs = []
    if dbg >= 6:
        outs.append(K("ob_out"))
    if dbg >= 7:
        outs.append(K("oa_out"))
    if dbg < 6:
        outs = [K("gsub"), K("expsink"), K("neglam"), K("va_ones"), K("vb_ones"), K("xt0"), K("xt1"), K("biasA"), K("biasB"), K("maskA"), K("ident"), K("wvb"), K("wkb2")]
    return outs


def attn_decl(nc, S, pfx=""):
    io = {}
    def di(name, shape):
        io[name] = nc.dram_tensor(pfx + name, shape, F32, kind="ExternalInput").ap()
    di("xT", [1024, S])
    for nm in ("wq", "wk", "wv", "wqb"):
        di(nm, [1024, 256])
    di("wkb2", [1024, 128])
    di("wvb", [1024, 64])
    di("lamqk", [1, 512])
    di("subg", [1, 256])
    di("sinks4", [1, 4])
    di("c_ident", [128, 128])
    di("c_maskA", [128, 1024])
    di("c_biasA", [128, 130])
    di("c_biasB", [128, 2048])
    io["oa"] = nc.dram_tensor(pfx + "oa", [S, 256], F32, kind="ExternalOutput").ap()
    io["ob"] = nc.dram_tensor(pfx + "ob", [S, 256], F32, kind="ExternalOutput").ap()
    return io


def attn_inputs(layer, head, xT_b, w_in, lambda_qk, subln_g, sinks):
    W = w_in[layer]
    h = head
    kvh = h // 2
    wkb = W[:, 4096 + kvh * 64:4096 + (kvh + 1) * 64]
    m = {
        "xT": xT_b,
        "wq": np.ascontiguousarray(W[:, h * 256:(h + 1) * 256]),
        "wk": np.ascontiguousarray(W[:, 1024 + h * 256:1024 + (h + 1) * 256]),
        "wv": np.ascontiguousarray(W[:, 2048 + h * 256:2048 + (h + 1) * 256]),
        "wqb": np.ascontiguousarray(W[:, 3072 + h * 256:3072 + (h + 1) * 256]),
        "wkb2": np.ascontiguousarray(np.concatenate([wkb, wkb], axis=1)),
        "wvb": np.ascontiguousarray(W[:, 4224 + kvh * 64:4224 + (kvh + 1) * 64]),
        "lamqk": np.ascontiguousarray(lambda_qk[layer].reshape(1, 512)),
        "subg": np.ascontiguousarray(subln_g[layer].reshape(1, 256)),
        "sinks4": np.ascontiguousarray(sinks[layer, 4 * h:4 * h + 4].reshape(1, 4)),
    }
    m.update(attn_consts(h))
    return m


def lam_init_of(layer):
    return 0.8 - 0.6 * math.exp(-0.3 * layer)


def build_attn_program(S, layer):
    nc = bass.Bass("TRN2", target_bir_lowering=False)
    io = attn_decl(nc, S)
    kb = KB(nc)
    outs = emit_attn(nc, kb, S, lam_init_of(layer), io)
    kb.final_wait("sp", outs)
    kb.emit()
    return nc


def kb_barrier(kb):
    toks = []
    for e in ("pe", "act", "dve", "pool"):
        if kb.ecnt[e] > 0:
            toks.append((id(kb.esem[e]), kb.esem[e], kb.ecnt[e], e))
    for key, (sem, val) in kb.dsem.items():
        if val > 0:
            toks.append((id(sem), sem, val, "dma"))
    for eng in ENGS:
        waits = []
        for sid, sem, val, teng in toks:
            if teng == eng:
                continue
            if kb.known[eng].get(sid, 0) >= val:
                continue
            waits.append([sid, sem, val])
            kb.known[eng][sid] = val
        kb.ops[eng].append((waits, None, None, 0))


def tok_consts(C):
    t = np.arange(128)
    return {
        "c_ident": np.eye(128, dtype=np.float32),
        "c_upper": (t[:, None] < t[None, :]).astype(np.float32),
        "c_ecap": np.broadcast_to((np.arange(32, dtype=np.float32) * C)[None, :], (128, 32)).copy(),
    }


def emit_tok(nc, kb, T, C, io, pfx="t"):
    import contextlib
    NT = T // 128
    CT = C // 128
    NH = (C + 511) // 512
    CH = C // NH
    NSLOT = 32 * C
    alpha = float(DEEPNORM_ALPHA)
    op, dma = kb.op, kb.dma
    K = lambda s: pfx + s
    psb = [nc.alloc_psum_tensor(pfx + "ps%d" % i, [128, 512], F32) for i in range(8)]
    pstate = {"i": 0}

    def pbank():
        i = pstate["i"] % 8
        pstate["i"] += 1
        return psb[i], K("ps%d" % i)

    def sbp(name, shape, dt):
        return nc.alloc_sbuf_tensor(pfx + name, shape, dt)

    identbf = sbp("identbf", [128, 128], BF16)
    ident32 = sbp("ident32", [128, 128], F32)
    upper = sbp("upper", [128, 128], BF16)
    onesbf = sbp("onesbf", [128, 128], BF16)
    ecap = sbp("ecap", [128, 32], F32)
    lng = sbp("lng", [128, 2, 1024], F32)
    lnb = sbp("lnb", [128, 2, 1024], F32)
    brt = sbp("brt", [128, 32], F32)
    wr32 = sbp("wr32", [128, 8, 32], F32)
    base = sbp("base", [128, 32], F32)
    desti = sbp("desti", [128, NT, 4], I32)
    gkall = sbp("gkall", [128, NT, 4], F32)
    epst = sbp("epst", [128, 1], F32)
    bgT = sbp("bgT", [128, 256], F32)
    buT = sbp("buT", [128, 256], F32)

    dma("pool", lambda e: e.dma_start(out=identbf[:], in_=io["c_ident"]), writes=[K("identbf")])
    dma("sp", lambda e: e.dma_start(out=ident32[:], in_=io["c_ident"]), writes=[K("ident32")])
    dma("pool", lambda e: e.dma_start(out=upper[:], in_=io["c_upper"]), writes=[K("upper")])
    dma("sp", lambda e: e.dma_start(out=ecap[:], in_=io["c_ecap"]), writes=[K("ecap")])
    for j in range(2):
        dma("sp", lambda e, j=j: e.dma_start(out=lng[:, j, :], in_=io["ln_g"][j:j + 1, :].partition_broadcast(128)),
            writes=[K("lng%d" % j)])
        dma("sp", lambda e, j=j: e.dma_start(out=lnb[:, j, :], in_=io["ln_b"][j:j + 1, :].partition_broadcast(128)),
            writes=[K("lnb%d" % j)])
    dma("sp", lambda e: e.dma_start(out=brt[:], in_=io["b_router"].partition_broadcast(128)), writes=[K("brt")])
    dma("sp", lambda e: e.dma_start(out=wr32[:], in_=io["w_router"].rearrange("(kc p) n -> p kc n", p=128)),
        writes=[K("wr32")])
    op("dve", lambda e: e.memset(onesbf[:], 1.0), writes=[K("onesbf")])
    op("dve", lambda e: e.memset(base[:], 0.0), writes=[K("base")])
    op("dve", lambda e: e.memset(epst[:], LN_EPS), writes=[K("epst")])

    def layer_norm(pre, sums, j, out, tmp, tagk, small):
        negmean, ssq, lnv, rstd, nb = small
        kp, ko, kt = K(tagk + "pre"), K(tagk + "out"), K(tagk + "tmp")
        ks = K(tagk + "small")
        op("dve", lambda e: e.tensor_tensor(out=negmean[:], in0=sums[:, 0:1], in1=sums[:, 1:2], op=ALU.add),
           reads=[K(tagk + "sums")], writes=[ks + "nm"])
        op("dve", lambda e: e.tensor_scalar(out=negmean[:], in0=negmean[:], scalar1=-1.0 / 1024.0, scalar2=None,
                                            op0=ALU.mult), reads=[ks + "nm"], writes=[ks + "nm"])
        op("act", lambda e: e.activation(out=tmp[:], in_=pre[:], func=AF.Square, bias=negmean[:, 0:1], scale=1.0,
                                         accum_out=ssq[:]), reads=[kp, ks + "nm"], writes=[kt, ks + "ssq"])
        op("act", lambda e: e.activation(out=lnv[:], in_=ssq[:], func=AF.Ln, bias=epst[:, 0:1], scale=1.0 / 1024.0),
           reads=[ks + "ssq", K("epst")], writes=[ks + "lnv"])
        op("act", lambda e: e.activation(out=rstd[:], in_=lnv[:], func=AF.Exp, scale=-0.5),
           reads=[ks + "lnv"], writes=[ks + "rstd"])
        op("dve", lambda e: e.tensor_tensor(out=nb[:], in0=negmean[:], in1=rstd[:], op=ALU.mult),
           reads=[ks + "nm", ks + "rstd"], writes=[ks + "nb"])
        op("act", lambda e: e.activation(out=tmp[:], in_=pre[:], func=AF.Identity, bias=nb[:, 0:1], scale=rstd[:, 0:1]),
           reads=[kp, ks + "nb", ks + "rstd"], writes=[kt])
        op("dve", lambda e: e.tensor_tensor(out=tmp[:], in0=tmp[:], in1=lng[:, j, :], op=ALU.mult),
           reads=[kt, K("lng%d" % j)], writes=[kt])
        op("dve", lambda e: e.tensor_tensor(out=out[:], in0=tmp[:], in1=lnb[:, j, :], op=ALU.add),
           reads=[kt, K("lnb%d" % j)], writes=[ko])

    with contextlib.ExitStack() as es:
        def sb(name, shape, dt):
            return es.enter_context(nc.sbuf_tensor(pfx + name, shape, dt))
        wg = sb("wg", [128, 8, 2048], BF16)
        wo = sb("wo", [128, 8, 1024], BF16)
        xt_ = [sb("x%d" % i, [128, 1024], F32) for i in range(2)]
        oat = [sb("oa%d" % i, [128, 1024], F32) for i in range(2)]
        obt = [sb("ob%d" % i, [128, 1024], F32) for i in range(2)]
        xTt = [sb("xT%d" % i, [128, 8, 128], BF16) for i in range(2)]
        sig = sb("sig", [128, 2048], F32)
        m1 = sb("m1", [128, 1024], F32)
        m2 = sb("m2", [128, 1024], F32)
        mg = sb("mg", [128, 1024], BF16)
        mT = sb("mT", [128, 8, 128], BF16)
        pre = sb("pre", [128, 1024], F32)
        tmp = sb("tmp", [128, 1024], F32)
        hh = [sb("h%d" % i, [128, 1024], F32) for i in range(2)]
        hbf = [sb("hbf%d" % i, [128, 1024], BF16) for i in range(2)]
        hT32 = sb("hT32", [128, 8, 128], F32)
        sums = sb("sums", [128, 2], F32)
        small = [sb("sm%d" % i, [128, 1], F32) for i in range(5)]
        logit = sb("logit", [128, 32], F32)
        m8 = sb("m8", [128, 8], F32)
        negm0 = sb("negm0", [128, 1], F32)
        exl = sb("exl", [128, 32], F32)
        mask = sb("mask", [128, 32], F32)
        maskbf = sb("maskbf", [128, 32], BF16)
        gun = sb("gun", [128, 32], F32)
        den = sb("den", [128, 1], F32)
        gd = sb("gd", [128, 32], F32)
        slotf = sb("slotf", [128, 32], F32)
        oh = sb("oh", [128, 32], F32)
        junk32 = sb("junk32", [128, 32], F32)
        destf = sb("destf", [128, 4], F32)

        dma("pool", lambda e: e.dma_start(out=wg[:, 0:4], in_=io["wg"].rearrange("(kc p) n -> p kc n", p=128)[:, 0:4]),
            writes=[K("wg_a")])
        dma("pool", lambda e: e.dma_start(out=wg[:, 4:8], in_=io["wg"].rearrange("(kc p) n -> p kc n", p=128)[:, 4:8]),
            writes=[K("wg_b")])
        dma("pool", lambda e: e.dma_start(out=wo[:], in_=io["w_o"].rearrange("(kc p) n -> p kc n", p=128)),
            writes=[K("wo")])
        xT_v = io["xT"].rearrange("(kc p) t -> p kc t", p=128)
        zt = sb("zt", [128, CT, 1024], BF16)
        op("pool", lambda e: e.memset(zt[:], 0.0), writes=[K("zt")])
        for ex in range(N_EXPERTS):
            dma("sp", lambda e: e.dma_start(out=io["xbuf"][ex * C:(ex + 1) * C, :].rearrange("(t p) d -> p t d", p=128),
                                            in_=zt[:]), reads=[K("zt")], writes=[K("xz%d" % ex)], semkey=K("xz"))

        def loads(i):
            b = i % 2
            r = slice(i * 128, (i + 1) * 128)
            dma("pool", lambda e: e.dma_start(out=xTt[b][:], in_=xT_v[:, :, r]), writes=[K("xT%d" % b)])
            dma("sp", lambda e: e.dma_start(out=xt_[b][:], in_=io["x"][r, :]), writes=[K("x%d" % b)])
            dma("sp", lambda e: e.dma_start(out=oat[b][:], in_=io["oa"][r, :]), writes=[K("oa%d" % b)])
            dma("sp", lambda e: e.dma_start(out=obt[b][:], in_=io["ob"][r, :]), writes=[K("ob%d" % b)])

        def tile1(i):
            b = i % 2
            for cb in range(4):
                bank, bk = pbank()
                for kc in range(8):
                    op("pe", lambda e, kc=kc, bank=bank: e.matmul(bank[:], lhsT=xTt[b][:, kc, :],
                                                                 rhs=wg[:, kc, cb * 512:(cb + 1) * 512],
                                                                 start=(kc == 0), stop=(kc == 7)),
                       reads=[K("xT%d" % b), K("wg_a"), K("wg_b")], writes=[bk])
                op("act", lambda e, bank=bank: e.activation(out=sig[:, cb * 512:(cb + 1) * 512], in_=bank[:],
                                                            func=AF.Sigmoid), reads=[bk], writes=[K("sig%d" % cb)])
            op("dve", lambda e: e.tensor_tensor(out=m1[:], in0=sig[:, 0:1024], in1=oat[b][:], op=ALU.mult),
               reads=[K("sig0"), K("sig1"), K("oa%d" % b)], writes=[K("m1")])
            op("dve", lambda e: e.tensor_tensor(out=m2[:], in0=sig[:, 1024:2048], in1=obt[b][:], op=ALU.mult),
               reads=[K("sig2"), K("sig3"), K("ob%d" % b)], writes=[K("m2")])
            op("dve", lambda e: e.tensor_tensor(out=mg[:], in0=m1[:], in1=m2[:], op=ALU.add),
               reads=[K("m1"), K("m2")], writes=[K("mg")])
            bank, bk = pbank()
            bankbf = bank[:].bitcast(BF16)
            for kc in range(8):
                op("pe", lambda e, kc=kc: e.transpose(out=bankbf[:, kc * 128:(kc + 1) * 128],
                                                      in_=mg[:, kc * 128:(kc + 1) * 128], identity=identbf[:]),
                   reads=[K("mg"), K("identbf")], writes=[bk])
            op("dve", lambda e: e.tensor_copy(out=mT[:].rearrange("p k t -> p (k t)"), in_=bankbf),
               reads=[bk], writes=[K("mT")])
            for dh in range(2):
                bank, bk = pbank()
                for kc in range(8):
                    op("pe", lambda e, kc=kc, bank=bank: e.matmul(bank[:], lhsT=mT[:, kc, :],
                                                                 rhs=wo[:, kc, dh * 512:(dh + 1) * 512],
                                                                 start=(kc == 0), stop=(kc == 7)),
                       reads=[K("mT"), K("wo")], writes=[bk])
                op("dve", lambda e, bank=bank: e.scalar_tensor_tensor(
                    out=pre[:, dh * 512:(dh + 1) * 512], in0=xt_[b][:, dh * 512:(dh + 1) * 512], scalar=alpha,
                    in1=bank[:], op0=ALU.mult, op1=ALU.add, accum_out=sums[:, dh:dh + 1]),
                   reads=[bk, K("x%d" % b)], writes=[K("l1pre"), K("l1sums")])
            h = hh[b]
            layer_norm(pre, sums, 0, h, tmp, "l1", small)
            dma("sp", lambda e: e.dma_start(out=io["hbuf"][i * 128:(i + 1) * 128, :], in_=h[:]),
                reads=[K("l1out")], writes=[K("hbuf%d" % i)], semkey=K("hst%d" % b))
            op("act", lambda e: e.activation(out=hbf[b][:], in_=h[:], func=AF.Copy), reads=[K("l1out")],
               writes=[K("hbf%d" % b)])
            for half in range(2):
                bank, bk = pbank()
                for q in range(4):
                    kc = half * 4 + q
                    op("pe", lambda e, kc=kc, q=q, bank=bank: e.transpose(out=bank[:, q * 128:(q + 1) * 128],
                                                                         in_=h[:, kc * 128:(kc + 1) * 128],
                                                                         identity=ident32[:]),
                       reads=[K("l1out"), K("ident32")], writes=[bk])
                op("dve", lambda e, bank=bank: e.tensor_copy(
                    out=hT32[:, half * 4:(half + 1) * 4, :].rearrange("p k t -> p (k t)"), in_=bank[:]),
                   reads=[bk], writes=[K("hT32_%d" % half)])
            bank, bk = pbank()
            for kc in range(8):
                op("pe", lambda e, kc=kc, bank=bank: e.matmul(bank[:, 0:32], lhsT=hT32[:, kc, :], rhs=wr32[:, kc, :],
                                                             start=(kc == 0), stop=(kc == 7)),
                   reads=[K("hT32_0"), K("hT32_1"), K("wr32")], writes=[bk])
            op("dve", lambda e, bank=bank: e.tensor_tensor(out=logit[:], in0=bank[:, 0:32], in1=brt[:], op=ALU.add),
               reads=[bk, K("brt")], writes=[K("logit")])
            op("dve", lambda e: e.max(out=m8[:], in_=logit[:]), reads=[K("logit")], writes=[K("m8")])
            op("dve", lambda e: e.tensor_scalar(out=negm0[:], in0=m8[:, 0:1], scalar1=-1.0, scalar2=None, op0=ALU.mult),
               reads=[K("m8")], writes=[K("negm0")])
            op("act", lambda e: e.activation(out=exl[:], in_=logit[:], func=AF.Exp, bias=negm0[:, 0:1], scale=1.0),
               reads=[K("logit"), K("negm0")], writes=[K("exl")])
            op("dve", lambda e: e.tensor_scalar(out=mask[:], in0=logit[:], scalar1=m8[:, 3:4], scalar2=None,
                                                op0=ALU.is_ge), reads=[K("logit"), K("m8")], writes=[K("mask")])
            op("dve", lambda e: e.scalar_tensor_tensor(out=gun[:], in0=exl[:], scalar=1.0, in1=mask[:], op0=ALU.mult,
                                                       op1=ALU.mult, accum_out=den[:]),
               reads=[K("exl"), K("mask")], writes=[K("gun"), K("den")])
            op("dve", lambda e: e.reciprocal(out=den[:], in_=den[:]), reads=[K("den")], writes=[K("den")])
            op("dve", lambda e: e.tensor_scalar(out=gd[:], in0=gun[:], scalar1=den[:, 0:1], scalar2=None, op0=ALU.mult),
               reads=[K("gun"), K("den")], writes=[K("gd")])
            op("dve", lambda e: e.tensor_copy(out=maskbf[:], in_=mask[:]), reads=[K("mask")], writes=[K("maskbf")])
            bank, bk = pbank()
            op("pe", lambda e, bank=bank: e.matmul(bank[:, 0:32], lhsT=upper[:], rhs=maskbf[:], start=True, stop=True),
               reads=[K("upper"), K("maskbf")], writes=[bk])
            op("pe", lambda e, bank=bank: e.matmul(bank[:, 32:64], lhsT=onesbf[:], rhs=maskbf[:], start=True, stop=True),
               reads=[K("onesbf"), K("maskbf")], writes=[bk])
            op("dve", lambda e, bank=bank: e.tensor_tensor(out=slotf[:], in0=bank[:, 0:32], in1=base[:], op=ALU.add),
               reads=[bk, K("base")], writes=[K("slotf")])
            op("dve", lambda e: e.scalar_tensor_tensor(out=slotf[:], in0=slotf[:], scalar=float(C - 1), in1=ecap[:],
                                                       op0=ALU.min, op1=ALU.add),
               reads=[K("slotf"), K("ecap")], writes=[K("slotf")])
            op("dve", lambda e, bank=bank: e.tensor_tensor(out=base[:], in0=bank[:, 32:64], in1=base[:], op=ALU.add),
               reads=[bk, K("base")], writes=[K("base")])
            for k in range(4):
                op("dve", lambda e, k=k: e.tensor_scalar(out=oh[:], in0=logit[:], scalar1=m8[:, k:k + 1], scalar2=None,
                                                         op0=ALU.is_equal), reads=[K("logit"), K("m8")], writes=[K("oh")])
                op("dve", lambda e, k=k: e.scalar_tensor_tensor(out=junk32[:], in0=oh[:], scalar=1.0, in1=slotf[:],
                                                                op0=ALU.mult, op1=ALU.mult, accum_out=destf[:, k:k + 1]),
                   reads=[K("oh"), K("slotf")], writes=[K("junk32"), K("destf")])
                op("dve", lambda e, k=k: e.scalar_tensor_tensor(out=junk32[:], in0=oh[:], scalar=1.0, in1=gd[:],
                                                                op0=ALU.mult, op1=ALU.mult,
                                                                accum_out=gkall[:, i, k:k + 1]),
                   reads=[K("oh"), K("gd")], writes=[K("junk32"), K("gk%d" % i)])
            op("dve", lambda e: e.tensor_copy(out=desti[:, i, :], in_=destf[:]), reads=[K("destf")],
               writes=[K("desti%d" % i)])
            for k in range(4):
                dma("pool", lambda e, k=k: e.indirect_dma_start(
                    out=io["xbuf"][:, :], out_offset=bass.IndirectOffsetOnAxis(ap=desti[:, i, k:k + 1], axis=0),
                    in_=hbf[b][:, :], in_offset=None),
                    reads=[K("hbf%d" % b), K("desti%d" % i), K("xz%d" % (N_EXPERTS - 1))],
                    writes=[K("xbuf_%d_%d" % (i, k))], semkey=K("hsc%d" % b))

        loads(0)
        for i in range(NT):
            if i + 1 < NT:
                loads(i + 1)
            tile1(i)
        kb_barrier(kb)

    with contextlib.ExitStack() as es:
        def sb(name, shape, dt):
            return es.enter_context(nc.sbuf_tensor(pfx + name, shape, dt))
        wup = [sb("wup%d" % i, [128, 8, 2048], BF16) for i in range(2)]
        wdn = [sb("wdn%d" % i, [128, 8, 1024], BF16) for i in range(2)]
        bdn = [sb("bdn%d" % i, [128, 1024], F32) for i in range(2)]
        xrows = [sb("xrows%d" % i, [128, CT, 1024], BF16) for i in range(2)]
        xTe = sb("xTe", [128, 8, C], BF16)
        actT = sb("actT", [128, 8, C], BF16)
        gsb = [sb("gsb%d" % i, [128, CH], F32) for i in range(2)]
        sgb = [sb("sgb%d" % i, [128, CH], F32) for i in range(2)]
        usb = [sb("usb%d" % i, [128, CH], F32) for i in range(2)]
        ysb = [sb("ysb%d" % i, [128, 1024], F32) for i in range(2)]
        bupr = [sb("bupr%d" % i, [128, 256], F32) for i in range(2)]

        bup_v = io["b_up"].rearrange("e (fo n) -> (e fo) n", n=256)
        for j in range(2):
            dma("sp", lambda e, j=j: e.dma_start(out=bupr[j][:], in_=bup_v[j * 128:(j + 1) * 128, :]),
                writes=[K("bupr%d" % j)])
        for gu, dst in ((0, bgT), (1, buT)):
            bank, bk = pbank()
            for j in range(2):
                src = bupr[j][:].rearrange("p (f two) -> p f two", two=2)[:, :, gu]
                op("pe", lambda e, j=j, src=src, bank=bank: e.transpose(out=bank[:, j * 128:(j + 1) * 128], in_=src,
                                                                       identity=ident32[:]),
                   reads=[K("bupr%d" % j), K("ident32")], writes=[bk])
            op("dve", lambda e, bank=bank, dst=dst: e.tensor_copy(out=dst[:], in_=bank[:, 0:256]), reads=[bk],
               writes=[K("bgu%d" % gu)])

        def wloads(ex):
            wb = ex % 2
            upv = io["w_up"][ex].rearrange("(kc p) n -> p kc n", p=128)
            for q in range(4):
                dma("pool", lambda e, q=q: e.dma_start(out=wup[wb][:, 2 * q:2 * q + 2], in_=upv[:, 2 * q:2 * q + 2]),
                    writes=[K("wup%d_%d" % (wb, q))])
            dnv = io["w_down"][ex].rearrange("(kc p) n -> p kc n", p=128)
            for q in range(2):
                dma("pool", lambda e, q=q: e.dma_start(out=wdn[wb][:, 4 * q:4 * q + 4], in_=dnv[:, 4 * q:4 * q + 4]),
                    writes=[K("wdn%d_%d" % (wb, q))])
            dma("sp", lambda e: e.dma_start(out=bdn[wb][:], in_=io["b_down"][ex:ex + 1, :].partition_broadcast(128)),
                writes=[K("bdn%d" % wb)])
            dma("sp", lambda e: e.dma_start(out=xrows[wb][:],
                                            in_=io["xbuf"][ex * C:(ex + 1) * C, :].rearrange("(t p) d -> p t d", p=128)),
                writes=[K("xrows%d" % wb)])

        tstate = {"i": 0}

        def expert(ex):
            wb = ex % 2
            wupv = wup[wb][:].rearrange("p k (f two) -> p k f two", two=2)
            for t in range(CT):
                bank, bk = pbank()
                bankbf = bank[:].bitcast(BF16)
                for kc in range(8):
                    op("pe", lambda e, kc=kc, t=t, bankbf=bankbf: e.transpose(
                        out=bankbf[:, kc * 128:(kc + 1) * 128], in_=xrows[wb][:, t, kc * 128:(kc + 1) * 128],
                        identity=identbf[:]), reads=[K("xrows%d" % wb), K("identbf")], writes=[bk])
                op("dve", lambda e, t=t, bankbf=bankbf: e.tensor_copy(
                    out=xTe[:, :, t * 128:(t + 1) * 128], in_=bankbf.rearrange("p (k q) -> p k q", k=8)),
                   reads=[bk], writes=[K("xTe%d" % t)])
            xkeys = [K("xTe%d" % t) for t in range(CT)]
            for hf in range(NH):
                cs = slice(hf * CH, (hf + 1) * CH)
                for fo in range(8):
                    banks = []
                    for gu in range(2):
                        bank, bk = pbank()
                        banks.append((bank, bk))
                        for kc in range(8):
                            op("pe", lambda e, kc=kc, gu=gu, bank=bank: e.matmul(
                                bank[:, 0:CH], lhsT=wupv[:, kc, fo * 128:(fo + 1) * 128, gu], rhs=xTe[:, kc, cs],
                                start=(kc == 0), stop=(kc == 7)),
                               reads=xkeys + [K("wup%d_%d" % (wb, kc // 2))], writes=[bk])
                    tb = tstate["i"] % 2
                    tstate["i"] += 1
                    g_, s_, u_ = gsb[tb], sgb[tb], usb[tb]
                    col = ex * 8 + fo
                    (bg_, bgk), (bu_, buk) = banks
                    op("dve", lambda e, g_=g_, bg_=bg_: e.tensor_scalar(out=g_[:], in0=bg_[:, 0:CH],
                                                                        scalar1=bgT[:, col:col + 1], scalar2=7.0,
                                                                        op0=ALU.add, op1=ALU.min),
                       reads=[bgk, K("bgu0")], writes=[K("g%d" % tb)])
                    op("act", lambda e, g_=g_, s_=s_: e.activation(out=s_[:], in_=g_[:], func=AF.Sigmoid, scale=1.702),
                       reads=[K("g%d" % tb)], writes=[K("s%d" % tb)])
                    op("dve", lambda e, u_=u_, bu_=bu_: e.tensor_scalar(out=u_[:], in0=bu_[:, 0:CH],
                                                                        scalar1=buT[:, col:col + 1], scalar2=7.0,
                                                                        op0=ALU.add, op1=ALU.min),
                       reads=[buk, K("bgu1")], writes=[K("u%d" % tb)])
                    op("dve", lambda e, u_=u_: e.tensor_scalar(out=u_[:], in0=u_[:], scalar1=-7.0, scalar2=1.0,
                                                               op0=ALU.max, op1=ALU.add),
                       reads=[K("u%d" % tb)], writes=[K("u%d" % tb)])
                    op("dve", lambda e, g_=g_, s_=s_: e.tensor_tensor(out=g_[:], in0=g_[:], in1=s_[:], op=ALU.mult),
                       reads=[K("g%d" % tb), K("s%d" % tb)], writes=[K("g%d" % tb)])
                    op("dve", lambda e, g_=g_, u_=u_: e.tensor_tensor(out=actT[:, fo, cs], in0=g_[:], in1=u_[:],
                                                                      op=ALU.mult),
                       reads=[K("g%d" % tb), K("u%d" % tb)], writes=[K("actT%d_%d" % (hf, fo))])
            akeys = [K("actT%d_%d" % (hf, fo)) for hf in range(NH) for fo in range(8)]
            for t in range(CT):
                yb = (ex * CT + t) % 2
                y_ = ysb[yb]
                for dh in range(2):
                    bank, bk = pbank()
                    for fo in range(8):
                        op("pe", lambda e, fo=fo, bank=bank: e.matmul(
                            bank[:], lhsT=actT[:, fo, t * 128:(t + 1) * 128], rhs=wdn[wb][:, fo, dh * 512:(dh + 1) * 512],
                            start=(fo == 0), stop=(fo == 7)),
                           reads=akeys + [K("wdn%d_%d" % (wb, fo // 4))], writes=[bk])
                    op("dve", lambda e, bank=bank, y_=y_: e.tensor_tensor(
                        out=y_[:, dh * 512:(dh + 1) * 512], in0=bank[:], in1=bdn[wb][:, dh * 512:(dh + 1) * 512],
                        op=ALU.add), reads=[bk, K("bdn%d" % wb)], writes=[K("ysb%d_%d" % (yb, dh))])
                r0 = ex * C + t * 128
                dma("sp", lambda e, y_=y_, r0=r0: e.dma_start(out=io["ybuf"][r0:r0 + 128, :], in_=y_[:]),
                    reads=[K("ysb%d_0" % yb), K("ysb%d_1" % yb)], writes=[K("ybuf_%d" % r0)], semkey=K("yst%d" % yb))

        wloads(0)
        for ex in range(N_EXPERTS):
            if ex + 1 < N_EXPERTS:
                wloads(ex + 1)
            expert(ex)
        kb_barrier(kb)

    with contextlib.ExitStack() as es:
        def sb(name, shape, dt):
            return es.enter_context(nc.sbuf_tensor(pfx + name, shape, dt))
        hb = [sb("hb%d" % i, [128, 1024], F32) for i in range(2)]
        yk = [[sb("yk%d_%d" % (i, k), [128, 1024], F32) for k in range(4)] for i in range(2)]
        ff = sb("ff", [128, 1024], F32)
        pre = sb("pre2", [128, 1024], F32)
        tmp = sb("tmp2", [128, 1024], F32)
        outt = [sb("outt%d" % i, [128, 1024], F32) for i in range(2)]
        sums = sb("sums2", [128, 2], F32)
        small = [sb("sm2_%d" % i, [128, 1], F32) for i in range(5)]

        def loads3(i):
            b = i % 2
            dma("sp", lambda e: e.dma_start(out=hb[b][:], in_=io["hbuf"][i * 128:(i + 1) * 128, :]), writes=[K("hb%d" % b)])
            for k in range(4):
                dma("pool", lambda e, k=k: e.indirect_dma_start(
                    out=yk[b][k][:, :], out_offset=None, in_=io["ybuf"][:, :],
                    in_offset=bass.IndirectOffsetOnAxis(ap=desti[:, i, k:k + 1], axis=0)),
                    reads=[K("desti%d" % i)], writes=[K("yk%d_%d" % (b, k))])

        def tile3(i):
            b = i % 2
            for k in range(4):
                if k == 0:
                    op("dve", lambda e: e.tensor_scalar(out=ff[:], in0=yk[b][0][:], scalar1=gkall[:, i, 0:1], scalar2=None,
                                                        op0=ALU.mult), reads=[K("yk%d_0" % b), K("gk%d" % i)],
                       writes=[K("ff")])
                else:
                    op("dve", lambda e, k=k: e.scalar_tensor_tensor(out=ff[:], in0=yk[b][k][:], scalar=gkall[:, i, k:k + 1],
                                                                    in1=ff[:], op0=ALU.mult, op1=ALU.add),
                       reads=[K("yk%d_%d" % (b, k)), K("gk%d" % i), K("ff")], writes=[K("ff")])
            for dh in range(2):
                op("dve", lambda e, dh=dh: e.scalar_tensor_tensor(
                    out=pre[:, dh * 512:(dh + 1) * 512], in0=hb[b][:, dh * 512:(dh + 1) * 512], scalar=alpha,
                    in1=ff[:, dh * 512:(dh + 1) * 512], op0=ALU.mult, op1=ALU.add, accum_out=sums[:, dh:dh + 1]),
                   reads=[K("hb%d" % b), K("ff")], writes=[K("l2pre"), K("l2sums")])
            o_ = outt[b]
            layer_norm(pre, sums, 1, o_, tmp, "l2", small)
            dma("sp", lambda e: e.dma_start(out=io["xo"][i * 128:(i + 1) * 128, :], in_=o_[:]),
                reads=[K("l2out")], writes=[K("xo_out")], semkey=K("ost%d" % b))

        loads3(0)
        for i in range(NT):
            if i + 1 < NT:
                loads3(i + 1)
            tile3(i)
    return [K("xo_out")]


def tok_decl(nc, T, C, pfx=""):
    io = {}
    def di(name, shape):
        io[name] = nc.dram_tensor(pfx + name, shape, F32, kind="ExternalInput").ap()
    di("x", [T, 1024]); di("xT", [1024, T]); di("oa", [T, 1024]); di("ob", [T, 1024])
    di("wg", [1024, 2048]); di("w_o", [1024, 1024]); di("ln_g", [2, 1024]); di("ln_b", [2, 1024])
    di("w_router", [1024, 32]); di("b_router", [1, 32])
    di("w_up", [32, 1024, 2048]); di("b_up", [32, 2048]); di("w_down", [32, 1024, 1024]); di("b_down", [32, 1024])
    di("c_ident", [128, 128]); di("c_upper", [128, 128]); di("c_ecap", [128, 32])
    io["xo"] = nc.dram_tensor(pfx + "xo", [T, 1024], F32, kind="ExternalOutput").ap()
    io["hbuf"] = nc.dram_tensor(pfx + "hbuf", [T, 1024], F32, kind="Internal").ap()
    io["xbuf"] = nc.dram_tensor(pfx + "xbuf", [32 * C, 1024], BF16, kind="Internal").ap()
    io["ybuf"] = nc.dram_tensor(pfx + "ybuf", [32 * C, 1024], F32, kind="Internal").ap()
    return io


def build_tok_program(T, C):
    nc = bass.Bass("TRN2", target_bir_lowering=False)
    io = tok_decl(nc, T, C)
    kb = KB(nc)
    outs = emit_tok(nc, kb, T, C, io)
    kb.final_wait("sp", outs)
    kb.emit()
    return nc


def tok_inputs(layer, x_c, xT_c, oa_c, ob_c, inp, C):
    m = {
        "x": x_c, "xT": xT_c, "oa": oa_c, "ob": ob_c,
        "wg": np.ascontiguousarray(inp["w_in"][layer][:, 4352:6400]),
        "w_o": inp["w_o"][layer], "ln_g": inp["ln_g"][layer], "ln_b": inp["ln_b"][layer],
        "w_router": inp["w_router"][layer], "b_router": np.ascontiguousarray(inp["b_router"][layer].reshape(1, 32)),
        "w_up": inp["w_up"][layer], "b_up": inp["b_up"][layer], "w_down": inp["w_down"][layer],
        "b_down": inp["b_down"][layer],
    }
    m.update(tok_consts(C))
    return m


CAP = 768
_PROG_CACHE = {}


def _attn_prog(layer):
    key = ("attn", layer)
    if key not in _PROG_CACHE:
        _PROG_CACHE[key] = build_attn_program(SEQ, layer)
    return _PROG_CACHE[key]


def _tok_prog():
    key = ("tok",)
    if key not in _PROG_CACHE:
        _PROG_CACHE[key] = build_tok_program(BATCH * SEQ // 8, CAP)
    return _PROG_CACHE[key]


def kernel(x, w_in, w_o, lambda_qk, subln_g, sinks, ln_g, ln_b, w_router, b_router, w_up, b_up, w_down, b_down):
    f32 = lambda a: np.ascontiguousarray(np.asarray(a, dtype=np.float32))
    inp = {"w_in": f32(w_in), "w_o": f32(w_o), "ln_g": f32(ln_g), "ln_b": f32(ln_b), "w_router": f32(w_router),
           "b_router": f32(b_router), "w_up": f32(w_up), "b_up": f32(b_up), "w_down": f32(w_down),
           "b_down": f32(b_down)}
    lambda_qk, subln_g, sinks = f32(lambda_qk), f32(subln_g), f32(sinks)
    xcur = f32(x)
    T = BATCH * SEQ // 8
    for layer in range(DEPTH):
        xTs = [np.ascontiguousarray(xcur[b].T) for b in range(BATCH)]
        in_maps = [attn_inputs(layer, c % 4, xTs[c // 4], inp["w_in"], lambda_qk, subln_g, sinks) for c in range(8)]
        res = run_bass_kernel_spmd(_attn_prog(layer), in_maps, core_ids=list(range(8)))
        oa = np.empty((BATCH, SEQ, D_MODEL), np.float32)
        ob = np.empty((BATCH, SEQ, D_MODEL), np.float32)
        for c in range(8):
            b, h = c // 4, c % 4
            oa[b, :, h * 256:(h + 1) * 256] = res.results[c]["oa"]
            ob[b, :, h * 256:(h + 1) * 256] = res.results[c]["ob"]
        del res, in_maps, xTs
        xf = xcur.reshape(-1, D_MODEL)
        oaf = oa.reshape(-1, D_MODEL)
        obf = ob.reshape(-1, D_MODEL)
        in_maps = []
        for c in range(8):
            sl = slice(c * T, (c + 1) * T)
            in_maps.append(tok_inputs(layer, np.ascontiguousarray(xf[sl]), np.ascontiguousarray(xf[sl].T),
                                      np.ascontiguousarray(oaf[sl]), np.ascontiguousarray(obf[sl]), inp, CAP))
        res = run_bass_kernel_spmd(_tok_prog(), in_maps, core_ids=list(range(8)))
        xcur = np.concatenate([res.results[c]["xo"] for c in range(8)], axis=0).reshape(BATCH, SEQ, D_MODEL)
        del res, in_maps
    return xcur
```

```python
import math
import numpy as np
import concourse.bass as bass
import concourse.mybir as mybir
from concourse.bass_utils import run_bass_kernel_spmd

F32 = mybir.dt.float32
BF16 = mybir.dt.bfloat16
U32 = mybir.dt.uint32
I32 = mybir.dt.int32
ALU = mybir.AluOpType
AF = mybir.ActivationFunctionType
AX = mybir.AxisListType

D_MODEL = 1024
BATCH = 2
SEQ = 16384
DEPTH = 4
N_EXPERTS = 32
TOP_K = 4
LN_EPS = 1e-5
DEEPNORM_ALPHA = (2.0 * DEPTH) ** 0.25
NEG_BIG = -30000.0

ENGS = ("pe", "act", "dve", "pool", "sp")


def _freeze(fn):
    import types
    if fn is None or fn.__closure__ is None:
        return fn
    cells = []
    for c in fn.__closure__:
        try:
            cells.append(types.CellType(c.cell_contents))
        except ValueError:
            cells.append(c)
    return types.FunctionType(fn.__code__, fn.__globals__, fn.__name__, fn.__defaults__, tuple(cells))


class KB:
    def __init__(self, nc, same_engine_sync=True):
        self.nc = nc
        self.same_engine_sync = same_engine_sync
        self.ops = {e: [] for e in ENGS}
        self.esem = {e: nc.alloc_semaphore("es_" + e) for e in ("pe", "act", "dve", "pool")}
        self.ecnt = {e: 0 for e in ("pe", "act", "dve", "pool")}
        self.dsem = {}
        self.last_w = {}
        self.readers = {}
        self.known = {e: {} for e in ENGS}
        self.n_dma_sems = 0

    def _need(self, eng, tok, waits):
        if tok is None:
            return
        sid, sem, val, teng = tok
        if teng == eng and eng == "pe":
            return
        if teng == eng and not self.same_engine_sync:
            return
        if self.known[eng].get(sid, 0) >= val:
            return
        for w in waits:
            if w[0] == sid:
                if w[2] < val:
                    w[2] = val
                return
        waits.append([sid, sem, val])

    def _deps(self, eng, reads, writes):
        waits = []
        for k in reads:
            self._need(eng, self.last_w.get(k), waits)
        for k in writes:
            self._need(eng, self.last_w.get(k), waits)
            for t in self.readers.get(k, ()):
                self._need(eng, t, waits)
        for sid, sem, val in waits:
            self.known[eng][sid] = val
        return waits

    def _commit(self, tok, reads, writes):
        for k in reads:
            lst = self.readers.setdefault(k, [])
            for i, t in enumerate(lst):
                if t[0] == tok[0]:
                    lst[i] = tok
                    break
            else:
                lst.append(tok)
        for k in writes:
            self.last_w[k] = tok
            self.readers[k] = []

    def op(self, eng, fn, reads=(), writes=()):
        fn = _freeze(fn)
        waits = self._deps(eng, reads, writes)
        self.ecnt[eng] += 1
        val = self.ecnt[eng]
        sem = self.esem[eng]
        tok = (id(sem), sem, val, eng)
        self.ops[eng].append((waits, fn, sem, 1))
        self._commit(tok, reads, writes)
        return tok

    def dma(self, eng, fn, reads=(), writes=(), semkey=None):
        fn = _freeze(fn)
        waits = self._deps(eng, reads, writes)
        if semkey is None:
            semkey = writes[0] if writes else reads[0]
        ent = self.dsem.get(semkey)
        if ent is None:
            ent = [self.nc.alloc_semaphore("ds%d" % self.n_dma_sems), 0]
            self.n_dma_sems += 1
            self.dsem[semkey] = ent
        ent[1] += 16
        sem, val = ent
        tok = (id(sem), sem, val, "dma")
        self.ops[eng].append((waits, fn, sem, 16))
        self._commit(tok, reads, writes)
        return tok

    def final_wait(self, eng, keys):
        waits = []
        for k in keys:
            self._need(eng, self.last_w.get(k), waits)
            for t in self.readers.get(k, ()):
                self._need(eng, t, waits)
        for sid, sem, val in waits:
            self.known[eng][sid] = val
        self.ops[eng].append((waits, None, None, 0))

    def emit(self):
        nc = self.nc
        ops = self.ops
        with nc.Block() as block:
            def run(e, engobj):
                for waits, fn, sem, inc in ops[e]:
                    for sid, wsem, val in waits:
                        engobj.wait_ge(wsem, val)
                    if fn is not None:
                        fn(engobj).then_inc(sem, inc)

            @block.tensor
            def _(eng):
                run("pe", eng)

            @block.scalar
            def _(eng):
                run("act", eng)

            @block.vector
            def _(eng):
                run("dve", eng)

            @block.gpsimd
            def _(eng):
                run("pool", eng)

            @block.sync
            def _(eng):
                run("sp", eng)


def _bf16_round(a):
    a = np.ascontiguousarray(a, dtype=np.float32)
    u = a.view(np.uint32).astype(np.uint64)
    r = ((u + 0x7FFF + ((u >> 16) & 1)) >> 16) << 16
    return r.astype(np.uint32).view(np.float32)


def attn_consts(head):
    kp = np.arange(128, dtype=np.float64)
    ident = np.eye(128, dtype=np.float32)
    maskA = np.zeros((128, 2, 2, 2, 128), np.float32)
    tri = (kp[:, None] > kp[None, :]).astype(np.float32) * NEG_BIG
    maskA[:, 0, :, 0, :] = tri[:, None, :]
    maskA[:, 1, :, 0, :] = NEG_BIG
    maskA[:, 1, :, 1, :] = tri[:, None, :]
    slope_a = 2.0 ** (-8.0 * (head + 1) / 4.0)
    m = np.arange(130, dtype=np.float64)
    biasA = (slope_a * (kp[:, None] + 128.0 * (m[None, :] - 128.0))).astype(np.float32)
    biasB = np.zeros((128, 2, 2, 4, 128), np.float32)
    ql = kp
    for g in range(4):
        slope = 2.0 ** (-8.0 * (4 * head + g + 1) / 16.0)
        dist_prev = ql[None, :] + 128.0 - kp[:, None]
        vprev = np.where(kp[:, None] > ql[None, :], -8.0 * slope * dist_prev, NEG_BIG)
        dist_cur = ql[None, :] - kp[:, None]
        vcur = np.where(kp[:, None] <= ql[None, :], -8.0 * slope * dist_cur, NEG_BIG)
        for t, v in enumerate((vprev, vcur)):
            v32 = v.astype(np.float32)
            hi = _bf16_round(v32)
            lo = (v32 - hi).astype(np.float32)
            biasB[:, t, 0, g, :] = hi
            biasB[:, t, 1, g, :] = lo
    return {
        "c_ident": ident,
        "c_maskA": maskA.reshape(128, 2 * 512),
        "c_biasA": biasA,
        "c_biasB": biasB.reshape(128, 4 * 512),
    }


def emit_attn(nc, kb, S, lam_init, io, pfx="a"):
    NSB = S // 256
    NCH = S // 128
    scaleA = 128.0 ** -0.5

    def sb(name, shape, dt):
        return nc.alloc_sbuf_tensor(pfx + name, shape, dt)

    def ps(name):
        return nc.alloc_psum_tensor(pfx + name, [128, 512], F32)

    K = lambda s: pfx + s

    ident = sb("ident", [128, 128], BF16)
    maskA = sb("maskA", [128, 2, 512], BF16)
    biasA = sb("biasA", [128, 130], F32)
    biasB = sb("biasB", [128, 2, 2, 512], BF16)
    wq = sb("wq", [128, 8, 256], BF16)
    wk = sb("wk", [128, 8, 256], BF16)
    wv = sb("wv", [128, 8, 256], BF16)
    wqb = sb("wqb", [128, 8, 256], BF16)
    wkb2 = sb("wkb2", [128, 8, 128], BF16)
    wvb = sb("wvb", [128, 8, 64], BF16)
    lq = sb("lq", [128, 512], F32)
    subg = sb("subg", [128, 256], F32)
    gsub = sb("gsub", [128, 256], F32)
    sinks = sb("sinks", [128, 4], F32)
    expsink = sb("expsink", [128, 4], F32)
    s12 = sb("s12", [128, 2], F32)
    e12 = sb("e12", [128, 2], F32)
    lamt = sb("lamt", [128, 1], F32)
    neglam = sb("neglam", [128, 1], F32)
    junk = sb("junk", [128, 256], F32)

    kT = sb("kT", [128, 2, S], BF16)
    va = sb("va", [128, NCH, 257], BF16)
    xt = [sb("xt%d" % i, [128, 8, 256], BF16) for i in range(2)]
    qT = sb("qT", [128, 2, 2, 128], BF16)
    qbz = sb("qbz", [128, 2, 4, 128], BF16)
    kbT = sb("kbT", [128, 4, 128], BF16)
    vb = sb("vb", [128, 4, 65], BF16)
    pt = [sb("pt%d" % i, [128, 512], BF16) for i in range(4)]
    t1 = sb("t1", [128, 256], F32)
    osb = sb("osb", [128, 256], F32)
    oa_sb = [sb("oa_sb%d" % i, [128, 256], F32) for i in range(2)]
    ob_sb = [sb("ob_sb%d" % i, [128, 256], F32) for i in range(2)]
    r0 = sb("r0", [128, 1], F32)
    r1 = sb("r1", [128, 1], F32)
    ss = sb("ss", [128, 1], F32)
    lnv = sb("lnv", [128, 1], F32)
    rstd = sb("rstd", [128, 1], F32)
    lt = sb("lt", [128, 4], F32)
    rl = sb("rl", [128, 4], F32)

    acc = [[ps("acc%d%d" % (c, s)) for s in range(2)] for c in range(2)]
    gbank = [ps("g%d" % i) for i in range(4)]
    gstate = {"i": 0}

    def gnext():
        i = gstate["i"] % 4
        gstate["i"] += 1
        return gbank[i], K("g%d" % i), i

    op, dma = kb.op, kb.dma
    import os
    dbg = int(os.environ.get("ATT_DBG", "9"))

    def cast_load(dst, dst_key, src_ap):
        dma("pool", lambda e: e.dma_start(out=dst, in_=src_ap), writes=[dst_key])

    cast_load(ident[:], K("ident"), io["c_ident"])
    cast_load(maskA[:], K("maskA"), io["c_maskA"].rearrange("p (d n) -> p d n", d=2))
    dma("sp", lambda e: e.dma_start(out=biasA[:], in_=io["c_biasA"]), writes=[K("biasA")])
    cast_load(biasB[:], K("biasB"), io["c_biasB"].rearrange("p (t h n) -> p t h n", t=2, h=2))
    for wt, nm in ((wq, "wq"), (wk, "wk"), (wv, "wv"), (wqb, "wqb"), (wkb2, "wkb2"), (wvb, "wvb")):
        cast_load(wt[:], K(nm), io[nm].rearrange("(kc p) n -> p kc n", p=128))
    dma("sp", lambda e: e.dma_start(out=lq[:], in_=io["lamqk"].partition_broadcast(128)), writes=[K("lq")])
    dma("sp", lambda e: e.dma_start(out=subg[:], in_=io["subg"].partition_broadcast(128)), writes=[K("subg")])
    dma("sp", lambda e: e.dma_start(out=sinks[:], in_=io["sinks4"].partition_broadcast(128)), writes=[K("sinks")])

    for i in range(2):
        op("dve", lambda e, i=i: e.scalar_tensor_tensor(
            out=junk[:, 0:128], in0=lq[:, 256 * i:256 * i + 128], scalar=1.0,
            in1=lq[:, 256 * i + 128:256 * i + 256], op0=ALU.mult, op1=ALU.mult,
            accum_out=s12[:, i:i + 1]), reads=[K("lq")], writes=[K("junk"), K("s12")])
    op("act", lambda e: e.activation(out=e12[:], in_=s12[:], func=AF.Exp), reads=[K("s12")], writes=[K("e12")])
    op("dve", lambda e: e.tensor_tensor(out=lamt[:], in0=e12[:, 1:2], in1=e12[:, 0:1], op=ALU.subtract),
       reads=[K("e12")], writes=[K("lamt")])
    op("dve", lambda e: e.tensor_scalar(out=neglam[:], in0=lamt[:], scalar1=-float(lam_init), scalar2=None,
                                        op0=ALU.add), reads=[K("lamt")], writes=[K("neglam")])
    op("dve", lambda e: e.tensor_scalar(out=gsub[:], in0=subg[:], scalar1=float(1.0 - lam_init), scalar2=None,
                                        op0=ALU.mult), reads=[K("subg")], writes=[K("gsub")])
    op("act", lambda e: e.activation(out=expsink[:], in_=sinks[:], func=AF.Exp), reads=[K("sinks")],
       writes=[K("expsink")])
    op("dve", lambda e: e.memset(va[:, :, 256:257], 1.0), writes=[K("va_ones")])
    op("dve", lambda e: e.memset(vb[:, :, 64:65], 1.0), writes=[K("vb_ones")])
    op("dve", lambda e: e.memset(qbz[:], 0.0), writes=[K("qbz")])

    xT_v = io["xT"].rearrange("(kc p) s -> p kc s", p=128)

    def load_xt(I):
        b = I % 2
        dma("pool", lambda e: e.dma_start(out=xt[b][:], in_=xT_v[:, :, I * 256:(I + 1) * 256]),
            writes=[K("xt%d" % b)])

    state = {"pp": 0, "st": 0}

    def proj_group(mm_list, evac_dst, evac_keys, ncols, evacs=None):
        bank, bk, _ = gnext()
        dstp = bank[:, 0:ncols]
        n = len(mm_list)
        for i, (l, r, rk) in enumerate(mm_list):
            op("pe", lambda e, l=l, r=r, i=i: e.matmul(dstp, lhsT=l, rhs=r, start=(i == 0), stop=(i == n - 1)),
               reads=rk, writes=[bk])
        if evacs is None:
            evacs = [(evac_dst, dstp)]
        else:
            evacs = [(d_, f_(bank)) for d_, f_ in evacs]
        for d_, s_ in evacs:
            op("dve", lambda e, d_=d_, s_=s_: e.tensor_copy(out=d_, in_=s_), reads=[bk], writes=evac_keys)

    def projections(I):
        b = I % 2
        xk = K("xt%d" % b)
        x = xt[b]
        for c in range(2):
            proj_group([(wq[:, kc, c * 128:(c + 1) * 128], x[:, kc, :], [K("wq"), xk]) for kc in range(8)],
                       qT[:, c, :, :].rearrange("p s q -> p (s q)"), [K("qT%d" % c)], 256)
        for c in range(2):
            proj_group([(wk[:, kc, c * 128:(c + 1) * 128], x[:, kc, :], [K("wk"), xk]) for kc in range(8)],
                       kT[:, c, I * 256:(I + 1) * 256], [K("kT%d_%d" % (c, I))], 256)
        for s in range(2):
            proj_group([(x[:, kc, s * 128:(s + 1) * 128], wv[:, kc, :], [K("wv"), xk]) for kc in range(8)],
                       va[:, 2 * I + s, 0:256], [K("va_%d" % (2 * I + s))], 256)
        for p in range(2):
            evs = []
            for half in range(2):
                g = 2 * p + half
                evs.append((qbz[half * 64:(half + 1) * 64, :, g, :],
                            lambda t, half=half: t[half * 64:(half + 1) * 64, 0:256].rearrange("p (s q) -> p s q", s=2)))
            proj_group([(wqb[:, kc, p * 128:(p + 1) * 128], x[:, kc, :], [K("wqb"), xk]) for kc in range(8)],
                       None, [K("qbz")], 256, evacs=evs)
        s0 = (2 * I) % 4
        proj_group([(wkb2[:, kc, :], x[:, kc, :], [K("wkb2"), xk]) for kc in range(8)],
                   kbT[:, s0:s0 + 2, :].rearrange("p s q -> p (s q)"), [K("kbT%d" % s0), K("kbT%d" % (s0 + 1))], 256)
        for s in range(2):
            proj_group([(x[:, kc, s * 128:(s + 1) * 128], wvb[:, kc, :], [K("wvb"), xk]) for kc in range(8)],
                       vb[:, s0 + s, 0:64], [K("vb%d" % (s0 + s))], 64)

    def diff_attn(I):
        nch = 2 * I + 2

        def qk(j):
            bank, bk, b = gnext()
            stv = bank[:].rearrange("p (c s q) -> p c s q", c=2, s=2)
            d = j - 2 * I
            diag = d >= 0
            for c in range(2):
                op("pe", lambda e, c=c: e.matmul(stv[:, c, :, :], lhsT=kT[:, c, j * 128:(j + 1) * 128],
                                                 rhs=qT[:, c, :, :], start=(c == 0), stop=(c == 1 and not diag)),
                   reads=[K("kT%d_%d" % (c, j // 2)), K("qT%d" % c)], writes=[bk])
            if diag:
                op("pe", lambda e: e.matmul(bank[:], lhsT=ident[:], rhs=maskA[:, d, :], start=False, stop=True),
                   reads=[K("ident"), K("maskA")], writes=[bk])
            return bank, bk, b

        def ex(j, bb):
            bank, bk, b = bb
            m = (j - 2 * I) + 128
            op("act", lambda e: e.activation(out=pt[b][:], in_=bank[:], func=AF.Exp, bias=biasA[:, m:m + 1],
                                             scale=scaleA),
               reads=[bk, K("biasA")], writes=[K("pt%d_0" % b), K("pt%d_1" % b)])

        def av(j, bb):
            bank, bk, b = bb
            ptv = pt[b][:].rearrange("p (c s q) -> p c s q", c=2, s=2)
            for c in range(2):
                for s in range(2):
                    op("pe", lambda e, c=c, s=s: e.matmul(acc[c][s][:, 0:257], lhsT=ptv[:, c, s, :], rhs=va[:, j, :],
                                                          start=(j == 0), stop=(j == nch - 1)),
                       reads=[K("pt%d_%d" % (b, s)), K("va_%d" % j), K("va_ones")], writes=[K("acc%d%d" % (c, s))])

        LA = 2
        bufs = {}
        for j in range(min(LA, nch)):
            bufs[j] = qk(j)
        for j in range(nch):
            if j + LA < nch:
                bufs[j + LA] = qk(j + LA)
            ex(j, bufs[j])
            av(j, bufs[j])
            del bufs[j]

    def diff_final(I):
        for s in range(2):
            diff_final_s(I, s)

    def diff_final_s(I, s):
        if True:
            a0, a1 = acc[0][s], acc[1][s]
            ob_ = oa_sb[s]
            op("dve", lambda e: e.reciprocal(out=r0[:], in_=a0[:, 256:257]), reads=[K("acc0%d" % s)], writes=[K("r0")])
            op("dve", lambda e: e.reciprocal(out=r1[:], in_=a1[:, 256:257]), reads=[K("acc1%d" % s)], writes=[K("r1")])
            op("dve", lambda e: e.tensor_tensor(out=r1[:], in0=r1[:], in1=neglam[:], op=ALU.mult),
               reads=[K("r1"), K("neglam")], writes=[K("r1")])
            op("dve", lambda e: e.tensor_scalar(out=t1[:], in0=a1[:, 0:256], scalar1=r1[:, 0:1], scalar2=None,
                                                op0=ALU.mult), reads=[K("acc1%d" % s), K("r1")], writes=[K("t1")])
            op("dve", lambda e: e.scalar_tensor_tensor(out=osb[:], in0=a0[:, 0:256], scalar=r0[:, 0:1], in1=t1[:],
                                                       op0=ALU.mult, op1=ALU.add),
               reads=[K("acc0%d" % s), K("r0"), K("t1")], writes=[K("osb")])
            op("dve", lambda e: e.scalar_tensor_tensor(out=junk[:], in0=osb[:], scalar=1.0, in1=osb[:],
                                                       op0=ALU.mult, op1=ALU.mult, accum_out=ss[:]),
               reads=[K("osb")], writes=[K("junk"), K("ss")])
            op("act", lambda e: e.activation(out=lnv[:], in_=ss[:], func=AF.Ln, bias=LN_EPS_TILE[0][:, 0:1],
                                             scale=1.0 / 256.0), reads=[K("ss"), K("epst")], writes=[K("lnv")])
            op("act", lambda e: e.activation(out=rstd[:], in_=lnv[:], func=AF.Exp, scale=-0.5),
               reads=[K("lnv")], writes=[K("rstd")])
            op("dve", lambda e, ob_=ob_: e.scalar_tensor_tensor(out=ob_[:], in0=osb[:], scalar=rstd[:, 0:1],
                                                                in1=gsub[:], op0=ALU.mult, op1=ALU.mult),
               reads=[K("osb"), K("rstd"), K("gsub")], writes=[K("oa_sb%d" % s)])
            r0_ = I * 256 + s * 128
            dma("sp", lambda e, ob_=ob_, r0_=r0_: e.dma_start(out=io["oa"][r0_:r0_ + 128, :], in_=ob_[:]),
                reads=[K("oa_sb%d" % s)], writes=[K("oa_out")], semkey=K("oa_sb%d" % s))

    def swa(I):
        for sblk in range(2):
            swa_block(I, sblk)

    def swa_block(I, sblk):
        if True:
            n = 2 * I + sblk
            slot = n % 4
            chunks = []
            if n > 0:
                chunks.append(((n - 1) % 4, 0))
            chunks.append((slot, 1))
            bl = []
            for (cs, t) in chunks:
                bank, bk, b = gnext()
                bl.append((bank, bk, b))
                op("pe", lambda e, cs=cs, bank=bank: e.matmul(
                    bank[:].rearrange("p (g q) -> p g q", g=4), lhsT=kbT[:, cs, :],
                    rhs=qbz[:, sblk, :, :], start=True, stop=False),
                   reads=[K("kbT%d" % cs), K("qbz")], writes=[bk])
                for hl in range(2):
                    op("pe", lambda e, t=t, hl=hl, bank=bank: e.matmul(bank[:], lhsT=ident[:], rhs=biasB[:, t, hl, :],
                                                                      start=False, stop=(hl == 1)),
                       reads=[K("ident"), K("biasB")], writes=[bk])
                op("act", lambda e, b=b, bank=bank: e.activation(out=pt[b][:], in_=bank[:], func=AF.Exp, scale=0.125),
                   reads=[bk], writes=[K("pt%d_0" % b), K("pt%d_1" % b)])
            obps, obk, _ = gnext()
            nchk = len(chunks)
            for g in range(4):
                for ci, (cs, t) in enumerate(chunks):
                    b = bl[ci][2]
                    op("pe", lambda e, g=g, cs=cs, b=b, ci=ci: e.matmul(
                        obps[:, g * 65:(g + 1) * 65], lhsT=pt[b][:, g * 128:(g + 1) * 128], rhs=vb[:, cs, :],
                        start=(ci == 0), stop=(ci == nchk - 1)),
                       reads=[K("pt%d_0" % b), K("pt%d_1" % b), K("vb%d" % cs), K("vb_ones")], writes=[obk])
            obv = obps[:, 0:260].rearrange("p (g e) -> p g e", g=4)
            op("dve", lambda e: e.tensor_tensor(out=lt[:], in0=obv[:, :, 64], in1=expsink[:], op=ALU.add),
               reads=[obk, K("expsink")], writes=[K("lt")])
            op("dve", lambda e: e.reciprocal(out=rl[:], in_=lt[:]), reads=[K("lt")], writes=[K("rl")])
            ob_ = ob_sb[sblk]
            for g in range(4):
                op("dve", lambda e, g=g, ob_=ob_: e.tensor_scalar(out=ob_[:, g * 64:(g + 1) * 64], in0=obv[:, g, 0:64],
                                                                  scalar1=rl[:, g:g + 1], scalar2=None, op0=ALU.mult),
                   reads=[obk, K("rl")], writes=[K("ob_sb%d" % sblk)])
            dma("sp", lambda e, ob_=ob_, n=n: e.dma_start(out=io["ob"][n * 128:(n + 1) * 128, :], in_=ob_[:]),
                reads=[K("ob_sb%d" % sblk)], writes=[K("ob_out")], semkey=K("ob_sb%d" % sblk))

    epst = sb("epst", [128, 1], F32)
    LN_EPS_TILE = [epst]
    op("dve", lambda e: e.memset(epst[:], LN_EPS), writes=[K("epst")])

    load_xt(0)
    if NSB > 1:
        load_xt(1)
    projections(0)
    for I in range(NSB):
        diff_attn(I)
        swa(I)
        if I + 1 < NSB:
            projections(I + 1)
            if I + 2 < NSB:
                load_xt(I + 2)
        diff_final(I)
    return [K("ob_out"), K("oa_out")]


def attn_decl(nc, S, pfx=""):
    io = {}
    def di(name, shape):
        io[name] = nc.dram_tensor(pfx + name, shape, F32, kind="ExternalInput").ap()
    di("xT", [1024, S])
    for nm in ("wq", "wk", "wv", "wqb"):
        di(nm, [1024, 256])
    di("wkb2", [1024, 128])
    di("wvb", [1024, 64])
    di("lamqk", [1, 512])
    di("subg", [1, 256])
    di("sinks4", [1, 4])
    di("c_ident", [128, 128])
    di("c_maskA", [128, 1024])
    di("c_biasA", [128, 130])
    di("c_biasB", [128, 2048])
    io["oa"] = nc.dram_tensor(pfx + "oa", [S, 256], F32, kind="ExternalOutput").ap()
    io["ob"] = nc.dram_tensor(pfx + "ob", [S, 256], F32, kind="ExternalOutput").ap()
    return io


def attn_inputs(layer, head, xT_b, w_in, lambda_qk, subln_g, sinks):
    W = w_in[layer]
    h = head
    kvh = h // 2
    wkb = W[:, 4096 + kvh * 64:4096 + (kvh + 1) * 64]
    m = {
        "xT": xT_b,
        "wq": np.ascontiguousarray(W[:, h * 256:(h + 1) * 256]),
        "wk": np.ascontiguousarray(W[:, 1024 + h * 256:1024 + (h + 1) * 256]),
        "wv": np.ascontiguousarray(W[:, 2048 + h * 256:2048 + (h + 1) * 256]),
        "wqb": np.ascontiguousarray(W[:, 3072 + h * 256:3072 + (h + 1) * 256]),
        "wkb2": np.ascontiguousarray(np.concatenate([wkb, wkb], axis=1)),
        "wvb": np.ascontiguousarray(W[:, 4224 + kvh * 64:4224 + (kvh + 1) * 64]),
        "lamqk": np.ascontiguousarray(lambda_qk[layer].reshape(1, 512)),
        "subg": np.ascontiguousarray(subln_g[layer].reshape(1, 256)),
        "sinks4": np.ascontiguousarray(sinks[layer, 4 * h:4 * h + 4].reshape(1, 4)),
    }
    m.update(attn_consts(h))
    return m


def lam_init_of(layer):
    return 0.8 - 0.6 * math.exp(-0.3 * layer)


def build_attn_program(S, layer):
    nc = bass.Bass("TRN2", target_bir_lowering=False)
    io = attn_decl(nc, S)
    kb = KB(nc)
    outs = emit_attn(nc, kb, S, lam_init_of(layer), io)
    kb.final_wait("sp", outs)
    kb.emit()
    return nc


def kb_barrier(kb):
    toks = []
    for e in ("pe", "act", "dve", "pool"):
        if kb.ecnt[e] > 0:
            toks.append((id(kb.esem[e]), kb.esem[e], kb.ecnt[e], e))
    for key, (sem, val) in kb.dsem.items():
        if val > 0:
            toks.append((id(sem), sem, val, "dma"))
    for eng in ENGS:
        waits = []
        for sid, sem, val, teng in toks:
            if teng == eng:
                continue
            if kb.known[eng].get(sid, 0) >= val:
                continue
            waits.append([sid, sem, val])
            kb.known[eng][sid] = val
        kb.ops[eng].append((waits, None, None, 0))


def tok_consts(C):
    t = np.arange(128)
    return {
        "c_ident": np.eye(128, dtype=np.float32),
        "c_upper": (t[:, None] < t[None, :]).astype(np.float32),
        "c_ecap": np.broadcast_to((np.arange(32, dtype=np.float32) * C)[None, :], (128, 32)).copy(),
    }


def emit_tok(nc, kb, T, C, io, pfx="t"):
    import contextlib
    NT = T // 128
    CT = C // 128
    NH = (C + 511) // 512
    CH = C // NH
    NSLOT = 32 * C
    alpha = float(DEEPNORM_ALPHA)
    op, dma = kb.op, kb.dma
    K = lambda s: pfx + s
    psb = [nc.alloc_psum_tensor(pfx + "ps%d" % i, [128, 512], F32) for i in range(8)]
    pstate = {"i": 0}

    def pbank():
        i = pstate["i"] % 8
        pstate["i"] += 1
        return psb[i], K("ps%d" % i)

    def sbp(name, shape, dt):
        return nc.alloc_sbuf_tensor(pfx + name, shape, dt)

    identbf = sbp("identbf", [128, 128], BF16)
    ident32 = sbp("ident32", [128, 128], F32)
    upper = sbp("upper", [128, 128], BF16)
    onesbf = sbp("onesbf", [128, 128], BF16)
    ecap = sbp("ecap", [128, 32], F32)
    lng = sbp("lng", [128, 2, 1024], F32)
    lnb = sbp("lnb", [128, 2, 1024], F32)
    brt = sbp("brt", [128, 32], F32)
    wr32 = sbp("wr32", [128, 8, 32], F32)
    base = sbp("base", [128, 32], F32)
    desti = sbp("desti", [128, NT, 4], I32)
    gkall = sbp("gkall", [128, NT, 4], F32)
    epst = sbp("epst", [128, 1], F32)
    bgT = sbp("bgT", [128, 256], F32)
    buT = sbp("buT", [128, 256], F32)

    dma("pool", lambda e: e.dma_start(out=identbf[:], in_=io["c_ident"]), writes=[K("identbf")])
    dma("sp", lambda e: e.dma_start(out=ident32[:], in_=io["c_ident"]), writes=[K("ident32")])
    dma("pool", lambda e: e.dma_start(out=upper[:], in_=io["c_upper"]), writes=[K("upper")])
    dma("sp", lambda e: e.dma_start(out=ecap[:], in_=io["c_ecap"]), writes=[K("ecap")])
    for j in range(2):
        dma("sp", lambda e, j=j: e.dma_start(out=lng[:, j, :], in_=io["ln_g"][j:j + 1, :].partition_broadcast(128)),
            writes=[K("lng%d" % j)])
        dma("sp", lambda e, j=j: e.dma_start(out=lnb[:, j, :], in_=io["ln_b"][j:j + 1, :].partition_broadcast(128)),
            writes=[K("lnb%d" % j)])
    dma("sp", lambda e: e.dma_start(out=brt[:], in_=io["b_router"].partition_broadcast(128)), writes=[K("brt")])
    dma("sp", lambda e: e.dma_start(out=wr32[:], in_=io["w_router"].rearrange("(kc p) n -> p kc n", p=128)),
        writes=[K("wr32")])
    op("dve", lambda e: e.memset(onesbf[:], 1.0), writes=[K("onesbf")])
    op("dve", lambda e: e.memset(base[:], 0.0), writes=[K("base")])
    op("dve", lambda e: e.memset(epst[:], LN_EPS), writes=[K("epst")])

    def layer_norm(pre, sums, j, out, tmp, tagk, small):
        negmean, ssq, lnv, rstd, nb = small
        kp, ko, kt = K(tagk + "pre"), K(tagk + "out"), K(tagk + "tmp")
        ks = K(tagk + "small")
        op("dve", lambda e: e.tensor_tensor(out=negmean[:], in0=sums[:, 0:1], in1=sums[:, 1:2], op=ALU.add),
           reads=[K(tagk + "sums")], writes=[ks + "nm"])
        op("dve", lambda e: e.tensor_scalar(out=negmean[:], in0=negmean[:], scalar1=-1.0 / 1024.0, scalar2=None,
                                            op0=ALU.mult), reads=[ks + "nm"], writes=[ks + "nm"])
        op("act", lambda e: e.activation(out=tmp[:], in_=pre[:], func=AF.Square, bias=negmean[:, 0:1], scale=1.0,
                                         accum_out=ssq[:]), reads=[kp, ks + "nm"], writes=[kt, ks + "ssq"])
        op("act", lambda e: e.activation(out=lnv[:], in_=ssq[:], func=AF.Ln, bias=epst[:, 0:1], scale=1.0 / 1024.0),
           reads=[ks + "ssq", K("epst")], writes=[ks + "lnv"])
        op("act", lambda e: e.activation(out=rstd[:], in_=lnv[:], func=AF.Exp, scale=-0.5),
           reads=[ks + "lnv"], writes=[ks + "rstd"])
        op("dve", lambda e: e.tensor_tensor(out=nb[:], in0=negmean[:], in1=rstd[:], op=ALU.mult),
           reads=[ks + "nm", ks + "rstd"], writes=[ks + "nb"])
        op("act", lambda e: e.activation(out=tmp[:], in_=pre[:], func=AF.Identity, bias=nb[:, 0:1], scale=rstd[:, 0:1]),
           reads=[kp, ks + "nb", ks + "rstd"], writes=[kt])
        op("dve", lambda e: e.tensor_tensor(out=tmp[:], in0=tmp[:], in1=lng[:, j, :], op=ALU.mult),
           reads=[kt, K("lng%d" % j)], writes=[kt])
        op("dve", lambda e: e.tensor_tensor(out=out[:], in0=tmp[:], in1=lnb[:, j, :], op=ALU.add),
           reads=[kt, K("lnb%d" % j)], writes=[ko])

    with contextlib.ExitStack() as es:
        def sb(name, shape, dt):
            return es.enter_context(nc.sbuf_tensor(pfx + name, shape, dt))
        wg = sb("wg", [128, 8, 2048], BF16)
        wo = sb("wo", [128, 8, 1024], BF16)
        xt_ = [sb("x%d" % i, [128, 1024], F32) for i in range(2)]
        oat = [sb("oa%d" % i, [128, 1024], F32) for i in range(2)]
        obt = [sb("ob%d" % i, [128, 1024], F32) for i in range(2)]
        xTt = [sb("xT%d" % i, [128, 8, 128], BF16) for i in range(2)]
        sig = sb("sig", [128, 2048], F32)
        m1 = sb("m1", [128, 1024], F32)
        m2 = sb("m2", [128, 1024], F32)
        mg = sb("mg", [128, 1024], BF16)
        mT = sb("mT", [128, 8, 128], BF16)
        pre = sb("pre", [128, 1024], F32)
        tmp = sb("tmp", [128, 1024], F32)
        hh = [sb("h%d" % i, [128, 1024], F32) for i in range(2)]
        hbf = [sb("hbf%d" % i, [128, 1024], BF16) for i in range(2)]
        hT32 = sb("hT32", [128, 8, 128], F32)
        sums = sb("sums", [128, 2], F32)
        small = [sb("sm%d" % i, [128, 1], F32) for i in range(5)]
        logit = sb("logit", [128, 32], F32)
        m8 = sb("m8", [128, 8], F32)
        negm0 = sb("negm0", [128, 1], F32)
        exl = sb("exl", [128, 32], F32)
        mask = sb("mask", [128, 32], F32)
        maskbf = sb("maskbf", [128, 32], BF16)
        gun = sb("gun", [128, 32], F32)
        den = sb("den", [128, 1], F32)
        gd = sb("gd", [128, 32], F32)
        slotf = sb("slotf", [128, 32], F32)
        oh = sb("oh", [128, 32], F32)
        junk32 = sb("junk32", [128, 32], F32)
        destf = sb("destf", [128, 4], F32)

        dma("pool", lambda e: e.dma_start(out=wg[:, 0:4], in_=io["wg"].rearrange("(kc p) n -> p kc n", p=128)[:, 0:4]),
            writes=[K("wg_a")])
        dma("pool", lambda e: e.dma_start(out=wg[:, 4:8], in_=io["wg"].rearrange("(kc p) n -> p kc n", p=128)[:, 4:8]),
            writes=[K("wg_b")])
        dma("pool", lambda e: e.dma_start(out=wo[:], in_=io["w_o"].rearrange("(kc p) n -> p kc n", p=128)),
            writes=[K("wo")])
        xT_v = io["xT"].rearrange("(kc p) t -> p kc t", p=128)
        zt = sb("zt", [128, CT, 1024], BF16)
        op("pool", lambda e: e.memset(zt[:], 0.0), writes=[K("zt")])
        for ex in range(N_EXPERTS):
            dma("sp", lambda e: e.dma_start(out=io["xbuf"][ex * C:(ex + 1) * C, :].rearrange("(t p) d -> p t d", p=128),
                                            in_=zt[:]), reads=[K("zt")], writes=[K("xz%d" % ex)], semkey=K("xz"))

        def loads(i):
            b = i % 2
            r = slice(i * 128, (i + 1) * 128)
            dma("pool", lambda e: e.dma_start(out=xTt[b][:], in_=xT_v[:, :, r]), writes=[K("xT%d" % b)])
            dma("sp", lambda e: e.dma_start(out=xt_[b][:], in_=io["x"][r, :]), writes=[K("x%d" % b)])
            dma("sp", lambda e: e.dma_start(out=oat[b][:], in_=io["oa"][r, :]), writes=[K("oa%d" % b)])
            dma("sp", lambda e: e.dma_start(out=obt[b][:], in_=io["ob"][r, :]), writes=[K("ob%d" % b)])

        def tile1(i):
            b = i % 2
            for cb in range(4):
                bank, bk = pbank()
                for kc in range(8):
                    op("pe", lambda e, kc=kc, bank=bank: e.matmul(bank[:], lhsT=xTt[b][:, kc, :],
                                                                 rhs=wg[:, kc, cb * 512:(cb + 1) * 512],
                                                                 start=(kc == 0), stop=(kc == 7)),
                       reads=[K("xT%d" % b), K("wg_a"), K("wg_b")], writes=[bk])
                op("act", lambda e, bank=bank: e.activation(out=sig[:, cb * 512:(cb + 1) * 512], in_=bank[:],
                                                            func=AF.Sigmoid), reads=[bk], writes=[K("sig%d" % cb)])
            op("dve", lambda e: e.tensor_tensor(out=m1[:], in0=sig[:, 0:1024], in1=oat[b][:], op=ALU.mult),
               reads=[K("sig0"), K("sig1"), K("oa%d" % b)], writes=[K("m1")])
            op("dve", lambda e: e.tensor_tensor(out=m2[:], in0=sig[:, 1024:2048], in1=obt[b][:], op=ALU.mult),
               reads=[K("sig2"), K("sig3"), K("ob%d" % b)], writes=[K("m2")])
            op("dve", lambda e: e.tensor_tensor(out=mg[:], in0=m1[:], in1=m2[:], op=ALU.add),
               reads=[K("m1"), K("m2")], writes=[K("mg")])
            bank, bk = pbank()
            bankbf = bank[:].bitcast(BF16)
            for kc in range(8):
                op("pe", lambda e, kc=kc: e.transpose(out=bankbf[:, kc * 128:(kc + 1) * 128],
                                                      in_=mg[:, kc * 128:(kc + 1) * 128], identity=identbf[:]),
                   reads=[K("mg"), K("identbf")], writes=[bk])
            op("dve", lambda e: e.tensor_copy(out=mT[:].rearrange("p k t -> p (k t)"), in_=bankbf),
               reads=[bk], writes=[K("mT")])
            for dh in range(2):
                bank, bk = pbank()
                for kc in range(8):
                    op("pe", lambda e, kc=kc, bank=bank: e.matmul(bank[:], lhsT=mT[:, kc, :],
                                                                 rhs=wo[:, kc, dh * 512:(dh + 1) * 512],
                                                                 start=(kc == 0), stop=(kc == 7)),
                       reads=[K("mT"), K("wo")], writes=[bk])
                op("dve", lambda e, bank=bank: e.scalar_tensor_tensor(
                    out=pre[:, dh * 512:(dh + 1) * 512], in0=xt_[b][:, dh * 512:(dh + 1) * 512], scalar=alpha,
                    in1=bank[:], op0=ALU.mult, op1=ALU.add, accum_out=sums[:, dh:dh + 1]),
                   reads=[bk, K("x%d" % b)], writes=[K("l1pre"), K("l1sums")])
            h = hh[b]
            layer_norm(pre, sums, 0, h, tmp, "l1", small)
            dma("sp", lambda e: e.dma_start(out=io["hbuf"][i * 128:(i + 1) * 128, :], in_=h[:]),
                reads=[K("l1out")], writes=[K("hbuf%d" % i)], semkey=K("hst%d" % b))
            op("act", lambda e: e.activation(out=hbf[b][:], in_=h[:], func=AF.Copy), reads=[K("l1out")],
               writes=[K("hbf%d" % b)])
            for half in range(2):
                bank, bk = pbank()
                for q in range(4):
                    kc = half * 4 + q
                    op("pe", lambda e, kc=kc, q=q, bank=bank: e.transpose(out=bank[:, q * 128:(q + 1) * 128],
                                                                         in_=h[:, kc * 128:(kc + 1) * 128],
                                                                         identity=ident32[:]),
                       reads=[K("l1out"), K("ident32")], writes=[bk])
                op("dve", lambda e, bank=bank: e.tensor_copy(
                    out=hT32[:, half * 4:(half + 1) * 4, :].rearrange("p k t -> p (k t)"), in_=bank[:]),
                   reads=[bk], writes=[K("hT32_%d" % half)])
            bank, bk = pbank()
            for kc in range(8):
                op("pe", lambda e, kc=kc, bank=bank: e.matmul(bank[:, 0:32], lhsT=hT32[:, kc, :], rhs=wr32[:, kc, :],
                                                             start=(kc == 0), stop=(kc == 7)),
                   reads=[K("hT32_0"), K("hT32_1"), K("wr32")], writes=[bk])
            op("dve", lambda e, bank=bank: e.tensor_tensor(out=logit[:], in0=bank[:, 0:32], in1=brt[:], op=ALU.add),
               reads=[bk, K("brt")], writes=[K("logit")])
            op("dve", lambda e: e.max(out=m8[:], in_=logit[:]), reads=[K("logit")], writes=[K("m8")])
            op("dve", lambda e: e.tensor_scalar(out=negm0[:], in0=m8[:, 0:1], scalar1=-1.0, scalar2=None, op0=ALU.mult),
               reads=[K("m8")], writes=[K("negm0")])
            op("act", lambda e: e.activation(out=exl[:], in_=logit[:], func=AF.Exp, bias=negm0[:, 0:1], scale=1.0),
               reads=[K("logit"), K("negm0")], writes=[K("exl")])
            op("dve", lambda e: e.tensor_scalar(out=mask[:], in0=logit[:], scalar1=m8[:, 3:4], scalar2=None,
                                                op0=ALU.is_ge), reads=[K("logit"), K("m8")], writes=[K("mask")])
            op("dve", lambda e: e.scalar_tensor_tensor(out=gun[:], in0=exl[:], scalar=1.0, in1=mask[:], op0=ALU.mult,
                                                       op1=ALU.mult, accum_out=den[:]),
               reads=[K("exl"), K("mask")], writes=[K("gun"), K("den")])
            op("dve", lambda e: e.reciprocal(out=den[:], in_=den[:]), reads=[K("den")], writes=[K("den")])
            op("dve", lambda e: e.tensor_scalar(out=gd[:], in0=gun[:], scalar1=den[:, 0:1], scalar2=None, op0=ALU.mult),
               reads=[K("gun"), K("den")], writes=[K("gd")])
            op("dve", lambda e: e.tensor_copy(out=maskbf[:], in_=mask[:]), reads=[K("mask")], writes=[K("maskbf")])
            bank, bk = pbank()
            op("pe", lambda e, bank=bank: e.matmul(bank[:, 0:32], lhsT=upper[:], rhs=maskbf[:], start=True, stop=True),
               reads=[K("upper"), K("maskbf")], writes=[bk])
            op("pe", lambda e, bank=bank: e.matmul(bank[:, 32:64], lhsT=onesbf[:], rhs=maskbf[:], start=True, stop=True),
               reads=[K("onesbf"), K("maskbf")], writes=[bk])
            op("dve", lambda e, bank=bank: e.tensor_tensor(out=slotf[:], in0=bank[:, 0:32], in1=base[:], op=ALU.add),
               reads=[bk, K("base")], writes=[K("slotf")])
            op("dve", lambda e: e.scalar_tensor_tensor(out=slotf[:], in0=slotf[:], scalar=float(C - 1), in1=ecap[:],
                                                       op0=ALU.min, op1=ALU.add),
               reads=[K("slotf"), K("ecap")], writes=[K("slotf")])
            op("dve", lambda e, bank=bank: e.tensor_tensor(out=base[:], in0=bank[:, 32:64], in1=base[:], op=ALU.add),
               reads=[bk, K("base")], writes=[K("base")])
            for k in range(4):
                op("dve", lambda e, k=k: e.tensor_scalar(out=oh[:], in0=logit[:], scalar1=m8[:, k:k + 1], scalar2=None,
                                                         op0=ALU.is_equal), reads=[K("logit"), K("m8")], writes=[K("oh")])
                op("dve", lambda e, k=k: e.scalar_tensor_tensor(out=junk32[:], in0=oh[:], scalar=1.0, in1=slotf[:],
                                                                op0=ALU.mult, op1=ALU.mult, accum_out=destf[:, k:k + 1]),
                   reads=[K("oh"), K("slotf")], writes=[K("junk32"), K("destf")])
                op("dve", lambda e, k=k: e.scalar_tensor_tensor(out=junk32[:], in0=oh[:], scalar=1.0, in1=gd[:],
                                                                op0=ALU.mult, op1=ALU.mult,
                                                                accum_out=gkall[:, i, k:k + 1]),
                   reads=[K("oh"), K("gd")], writes=[K("junk32"), K("gk%d" % i)])
            op("dve", lambda e: e.tensor_copy(out=desti[:, i, :], in_=destf[:]), reads=[K("destf")],
               writes=[K("desti%d" % i)])
            for k in range(4):
                dma("pool", lambda e, k=k: e.indirect_dma_start(
                    out=io["xbuf"][:, :], out_offset=bass.IndirectOffsetOnAxis(ap=desti[:, i, k:k + 1], axis=0),
                    in_=hbf[b][:, :], in_offset=None),
                    reads=[K("hbf%d" % b), K("desti%d" % i), K("xz%d" % (N_EXPERTS - 1))],
                    writes=[K("xbuf_%d_%d" % (i, k))], semkey=K("hsc%d" % b))

        loads(0)
        for i in range(NT):
            if i + 1 < NT:
                loads(i + 1)
            tile1(i)
        kb_barrier(kb)

    with contextlib.ExitStack() as es:
        def sb(name, shape, dt):
            return es.enter_context(nc.sbuf_tensor(pfx + name, shape, dt))
        wup = [sb("wup%d" % i, [128, 8, 2048], BF16) for i in range(2)]
        wdn = [sb("wdn%d" % i, [128, 8, 1024], BF16) for i in range(2)]
        bdn = [sb("bdn%d" % i, [128, 1024], F32) for i in range(2)]
        xrows = [sb("xrows%d" % i, [128, CT, 1024], BF16) for i in range(2)]
        xTe = sb("xTe", [128, 8, C], BF16)
        actT = sb("actT", [128, 8, C], BF16)
        gsb = [sb("gsb%d" % i, [128, CH], F32) for i in range(2)]
        sgb = [sb("sgb%d" % i, [128, CH], F32) for i in range(2)]
        usb = [sb("usb%d" % i, [128, CH], F32) for i in range(2)]
        ysb = [sb("ysb%d" % i, [128, 1024], F32) for i in range(2)]
        bupr = [sb("bupr%d" % i, [128, 256], F32) for i in range(2)]

        bup_v = io["b_up"].rearrange("e (fo n) -> (e fo) n", n=256)
        for j in range(2):
            dma("sp", lambda e, j=j: e.dma_start(out=bupr[j][:], in_=bup_v[j * 128:(j + 1) * 128, :]),
                writes=[K("bupr%d" % j)])
        for gu, dst in ((0, bgT), (1, buT)):
            bank, bk = pbank()
            for j in range(2):
                src = bupr[j][:].rearrange("p (f two) -> p f two", two=2)[:, :, gu]
                op("pe", lambda e, j=j, src=src, bank=bank: e.transpose(out=bank[:, j * 128:(j + 1) * 128], in_=src,
                                                                       identity=ident32[:]),
                   reads=[K("bupr%d" % j), K("ident32")], writes=[bk])
            op("dve", lambda e, bank=bank, dst=dst: e.tensor_copy(out=dst[:], in_=bank[:, 0:256]), reads=[bk],
               writes=[K("bgu%d" % gu)])
        b7g = sb("b7g", [128, 256], F32)
        bu1 = sb("bu1", [128, 256], F32)
        c7 = sb("c7", [128, 1], F32)
        op("dve", lambda e: e.tensor_scalar(out=b7g[:], in0=bgT[:], scalar1=-1.0, scalar2=7.0, op0=ALU.mult, op1=ALU.add),
           reads=[K("bgu0")], writes=[K("b7g")])
        op("dve", lambda e: e.tensor_scalar(out=bu1[:], in0=buT[:], scalar1=1.0, scalar2=None, op0=ALU.add),
           reads=[K("bgu1")], writes=[K("bu1")])
        op("dve", lambda e: e.memset(c7[:], 7.0 * 1.702), writes=[K("c7")])

        def wloads(ex):
            wb = ex % 2
            upv = io["w_up"][ex].rearrange("(kc p) n -> p kc n", p=128)
            for q in range(4):
                dma("pool", lambda e, q=q: e.dma_start(out=wup[wb][:, 2 * q:2 * q + 2], in_=upv[:, 2 * q:2 * q + 2]),
                    writes=[K("wup%d_%d" % (wb, q))])
            dnv = io["w_down"][ex].rearrange("(kc p) n -> p kc n", p=128)
            for q in range(2):
                dma("pool", lambda e, q=q: e.dma_start(out=wdn[wb][:, 4 * q:4 * q + 4], in_=dnv[:, 4 * q:4 * q + 4]),
                    writes=[K("wdn%d_%d" % (wb, q))])
            dma("sp", lambda e: e.dma_start(out=bdn[wb][:], in_=io["b_down"][ex:ex + 1, :].partition_broadcast(128)),
                writes=[K("bdn%d" % wb)])
            dma("sp", lambda e: e.dma_start(out=xrows[wb][:],
                                            in_=io["xbuf"][ex * C:(ex + 1) * C, :].rearrange("(t p) d -> p t d", p=128)),
                writes=[K("xrows%d" % wb)])

        tstate = {"i": 0}

        def expert(ex):
            wb = ex % 2
            wupv = wup[wb][:].rearrange("p k (f two) -> p k f two", two=2)
            for t in range(CT):
                bank, bk = pbank()
                bankbf = bank[:].bitcast(BF16)
                for kc in range(8):
                    op("pe", lambda e, kc=kc, t=t, bankbf=bankbf: e.transpose(
                        out=bankbf[:, kc * 128:(kc + 1) * 128], in_=xrows[wb][:, t, kc * 128:(kc + 1) * 128],
                        identity=identbf[:]), reads=[K("xrows%d" % wb), K("identbf")], writes=[bk])
                op("dve", lambda e, t=t, bankbf=bankbf: e.tensor_copy(
                    out=xTe[:, :, t * 128:(t + 1) * 128], in_=bankbf.rearrange("p (k q) -> p k q", k=8)),
                   reads=[bk], writes=[K("xTe%d" % t)])
            xkeys = [K("xTe%d" % t) for t in range(CT)]
            for hf in range(NH):
                cs = slice(hf * CH, (hf + 1) * CH)
                for fo in range(8):
                    banks = []
                    for gu in range(2):
                        bank, bk = pbank()
                        banks.append((bank, bk))
                        for kc in range(8):
                            op("pe", lambda e, kc=kc, gu=gu, bank=bank: e.matmul(
                                bank[:, 0:CH], lhsT=wupv[:, kc, fo * 128:(fo + 1) * 128, gu], rhs=xTe[:, kc, cs],
                                start=(kc == 0), stop=(kc == 7)),
                               reads=xkeys + [K("wup%d_%d" % (wb, kc // 2))], writes=[bk])
                    tb = tstate["i"] % 2
                    tstate["i"] += 1
                    g_, s_, u_ = gsb[tb], sgb[tb], usb[tb]
                    col = ex * 8 + fo
                    (bg_, bgk), (bu_, buk) = banks
                    op("act", lambda e: e.activation(out=g_[:], in_=bg_[:, 0:CH], func=AF.Relu,
                                                     bias=b7g[:, col:col + 1], scale=-1.0),
                       reads=[bgk, K("b7g")], writes=[K("g%d" % tb)])
                    op("act", lambda e: e.activation(out=s_[:], in_=g_[:], func=AF.Silu, bias=c7[:, 0:1], scale=-1.702),
                       reads=[K("g%d" % tb), K("c7")], writes=[K("s%d" % tb)])
                    op("dve", lambda e: e.tensor_scalar(out=u_[:], in0=bu_[:, 0:CH], scalar1=bu1[:, col:col + 1],
                                                        scalar2=8.0, op0=ALU.add, op1=ALU.min),
                       reads=[buk, K("bu1")], writes=[K("u%d" % tb)])
                    op("dve", lambda e: e.scalar_tensor_tensor(out=actT[:, fo, cs], in0=u_[:], scalar=-6.0, in1=s_[:],
                                                               op0=ALU.max, op1=ALU.mult),
                       reads=[K("u%d" % tb), K("s%d" % tb)], writes=[K("actT%d_%d" % (hf, fo))])
            akeys = [K("actT%d_%d" % (hf, fo)) for hf in range(NH) for fo in range(8)]
            for t in range(CT):
                yb = (ex * CT + t) % 2
                y_ = ysb[yb]
                for dh in range(2):
                    bank, bk = pbank()
                    for fo in range(8):
                        op("pe", lambda e, fo=fo, bank=bank: e.matmul(
                            bank[:], lhsT=actT[:, fo, t * 128:(t + 1) * 128], rhs=wdn[wb][:, fo, dh * 512:(dh + 1) * 512],
                            start=(fo == 0), stop=(fo == 7)),
                           reads=akeys + [K("wdn%d_%d" % (wb, fo // 4))], writes=[bk])
                    op("dve", lambda e, bank=bank, y_=y_: e.scalar_tensor_tensor(
                        out=y_[:, dh * 512:(dh + 1) * 512], in0=bank[:], scalar=1.0 / 1.702,
                        in1=bdn[wb][:, dh * 512:(dh + 1) * 512], op0=ALU.mult, op1=ALU.add),
                       reads=[bk, K("bdn%d" % wb)], writes=[K("ysb%d_%d" % (yb, dh))])
                r0 = ex * C + t * 128
                dma("sp", lambda e, y_=y_, r0=r0: e.dma_start(out=io["ybuf"][r0:r0 + 128, :], in_=y_[:]),
                    reads=[K("ysb%d_0" % yb), K("ysb%d_1" % yb)], writes=[K("ybuf_%d" % r0)], semkey=K("yst%d" % yb))

        wloads(0)
        for ex in range(N_EXPERTS):
            if ex + 1 < N_EXPERTS:
                wloads(ex + 1)
            expert(ex)
        kb_barrier(kb)

    with contextlib.ExitStack() as es:
        def sb(name, shape, dt):
            return es.enter_context(nc.sbuf_tensor(pfx + name, shape, dt))
        hb = [sb("hb%d" % i, [128, 1024], F32) for i in range(2)]
        yk = [[sb("yk%d_%d" % (i, k), [128, 1024], F32) for k in range(4)] for i in range(2)]
        ff = sb("ff", [128, 1024], F32)
        pre = sb("pre2", [128, 1024], F32)
        tmp = sb("tmp2", [128, 1024], F32)
        outt = [sb("outt%d" % i, [128, 1024], F32) for i in range(2)]
        sums = sb("sums2", [128, 2], F32)
        small = [sb("sm2_%d" % i, [128, 1], F32) for i in range(5)]

        def loads3(i):
            b = i % 2
            dma("sp", lambda e: e.dma_start(out=hb[b][:], in_=io["hbuf"][i * 128:(i + 1) * 128, :]), writes=[K("hb%d" % b)])
            for k in range(4):
                dma("pool", lambda e, k=k: e.indirect_dma_start(
                    out=yk[b][k][:, :], out_offset=None, in_=io["ybuf"][:, :],
                    in_offset=bass.IndirectOffsetOnAxis(ap=desti[:, i, k:k + 1], axis=0)),
                    reads=[K("desti%d" % i)], writes=[K("yk%d_%d" % (b, k))])

        def tile3(i):
            b = i % 2
            for k in range(4):
                if k == 0:
                    op("dve", lambda e: e.tensor_scalar(out=ff[:], in0=yk[b][0][:], scalar1=gkall[:, i, 0:1], scalar2=None,
                                                        op0=ALU.mult), reads=[K("yk%d_0" % b), K("gk%d" % i)],
                       writes=[K("ff")])
                else:
                    op("dve", lambda e, k=k: e.scalar_tensor_tensor(out=ff[:], in0=yk[b][k][:], scalar=gkall[:, i, k:k + 1],
                                                                    in1=ff[:], op0=ALU.mult, op1=ALU.add),
                       reads=[K("yk%d_%d" % (b, k)), K("gk%d" % i), K("ff")], writes=[K("ff")])
            for dh in range(2):
                op("dve", lambda e, dh=dh: e.scalar_tensor_tensor(
                    out=pre[:, dh * 512:(dh + 1) * 512], in0=hb[b][:, dh * 512:(dh + 1) * 512], scalar=alpha,
                    in1=ff[:, dh * 512:(dh + 1) * 512], op0=ALU.mult, op1=ALU.add, accum_out=sums[:, dh:dh + 1]),
                   reads=[K("hb%d" % b), K("ff")], writes=[K("l2pre"), K("l2sums")])
            o_ = outt[b]
            layer_norm(pre, sums, 1, o_, tmp, "l2", small)
            dma("sp", lambda e: e.dma_start(out=io["xo"][i * 128:(i + 1) * 128, :], in_=o_[:]),
                reads=[K("l2out")], writes=[K("xo_out")], semkey=K("ost%d" % b))

        loads3(0)
        for i in range(NT):
            if i + 1 < NT:
                loads3(i + 1)
            tile3(i)
    return [K("xo_out")]


def tok_decl(nc, T, C, pfx=""):
    io = {}
    def di(name, shape):
        io[name] = nc.dram_tensor(pfx + name, shape, F32, kind="ExternalInput").ap()
    di("x", [T, 1024]); di("xT", [1024, T]); di("oa", [T, 1024]); di("ob", [T, 1024])
    di("wg", [1024, 2048]); di("w_o", [1024, 1024]); di("ln_g", [2, 1024]); di("ln_b", [2, 1024])
    di("w_router", [1024, 32]); di("b_router", [1, 32])
    di("w_up", [32, 1024, 2048]); di("b_up", [32, 2048]); di("w_down", [32, 1024, 1024]); di("b_down", [32, 1024])
    di("c_ident", [128, 128]); di("c_upper", [128, 128]); di("c_ecap", [128, 32])
    io["xo"] = nc.dram_tensor(pfx + "xo", [T, 1024], F32, kind="ExternalOutput").ap()
    io["hbuf"] = nc.dram_tensor(pfx + "hbuf", [T, 1024], F32, kind="Internal").ap()
    io["xbuf"] = nc.dram_tensor(pfx + "xbuf", [32 * C, 1024], BF16, kind="Internal").ap()
    io["ybuf"] = nc.dram_tensor(pfx + "ybuf", [32 * C, 1024], F32, kind="Internal").ap()
    return io


def build_tok_program(T, C):
    nc = bass.Bass("TRN2", target_bir_lowering=False)
    io = tok_decl(nc, T, C)
    kb = KB(nc)
    outs = emit_tok(nc, kb, T, C, io)
    kb.final_wait("sp", outs)
    kb.emit()
    return nc


def tok_inputs(layer, x_c, xT_c, oa_c, ob_c, inp, C):
    m = {
        "x": x_c, "xT": xT_c, "oa": oa_c, "ob": ob_c,
        "wg": np.ascontiguousarray(inp["w_in"][layer][:, 4352:6400]),
        "w_o": inp["w_o"][layer], "ln_g": inp["ln_g"][layer], "ln_b": inp["ln_b"][layer],
        "w_router": inp["w_router"][layer], "b_router": np.ascontiguousarray(inp["b_router"][layer].reshape(1, 32)),
        "w_up": inp["w_up"][layer], "b_up": inp["b_up"][layer], "w_down": inp["w_down"][layer],
        "b_down": inp["b_down"][layer],
    }
    m.update(tok_consts(C))
    return m


CAP = 768
_PROG_CACHE = {}


def _attn_prog(layer):
    key = ("attn", layer)
    if key not in _PROG_CACHE:
        _PROG_CACHE[key] = build_attn_program(SEQ, layer)
    return _PROG_CACHE[key]


def _tok_prog():
    key = ("tok",)
    if key not in _PROG_CACHE:
        _PROG_CACHE[key] = build_tok_program(BATCH * SEQ // 8, CAP)
    return _PROG_CACHE[key]


def kernel(x, w_in, w_o, lambda_qk, subln_g, sinks, ln_g, ln_b, w_router, b_router, w_up, b_up, w_down, b_down):
    f32 = lambda a: np.ascontiguousarray(np.asarray(a, dtype=np.float32))
    inp = {"w_in": f32(w_in), "w_o": f32(w_o), "ln_g": f32(ln_g), "ln_b": f32(ln_b), "w_router": f32(w_router),
           "b_router": f32(b_router), "w_up": f32(w_up), "b_up": f32(b_up), "w_down": f32(w_down),
           "b_down": f32(b_down)}
    lambda_qk, subln_g, sinks = f32(lambda_qk), f32(subln_g), f32(sinks)
    xcur = f32(x)
    T = BATCH * SEQ // 8
    for layer in range(DEPTH):
        xTs = [np.ascontiguousarray(xcur[b].T) for b in range(BATCH)]
        in_maps = [attn_inputs(layer, c % 4, xTs[c // 4], inp["w_in"], lambda_qk, subln_g, sinks) for c in range(8)]
        res = run_bass_kernel_spmd(_attn_prog(layer), in_maps, core_ids=list(range(8)))
        oa = np.empty((BATCH, SEQ, D_MODEL), np.float32)
        ob = np.empty((BATCH, SEQ, D_MODEL), np.float32)
        for c in range(8):
            b, h = c // 4, c % 4
            oa[b, :, h * 256:(h + 1) * 256] = res.results[c]["oa"]
            ob[b, :, h * 256:(h + 1) * 256] = res.results[c]["ob"]
        del res, in_maps, xTs
        xf = xcur.reshape(-1, D_MODEL)
        oaf = oa.reshape(-1, D_MODEL)
        obf = ob.reshape(-1, D_MODEL)
        in_maps = []
        for c in range(8):
            sl = slice(c * T, (c + 1) * T)
            in_maps.append(tok_inputs(layer, np.ascontiguousarray(xf[sl]), np.ascontiguousarray(xf[sl].T),
                                      np.ascontiguousarray(oaf[sl]), np.ascontiguousarray(obf[sl]), inp, CAP))
        res = run_bass_kernel_spmd(_tok_prog(), in_maps, core_ids=list(range(8)))
        xcur = np.concatenate([res.results[c]["xo"] for c in range(8)], axis=0).reshape(BATCH, SEQ, D_MODEL)
        del res, in_maps
    return xcur
```

```python
import math
import numpy as np
import concourse.bass as bass
import concourse.mybir as mybir
from concourse.bass_utils import run_bass_kernel_spmd

F32 = mybir.dt.float32
BF16 = mybir.dt.bfloat16
U32 = mybir.dt.uint32
I32 = mybir.dt.int32
ALU = mybir.AluOpType
AF = mybir.ActivationFunctionType
AX = mybir.AxisListType

D_MODEL = 1024
BATCH = 2
SEQ = 16384
DEPTH = 4
N_EXPERTS = 32
TOP_K = 4
LN_EPS = 1e-5
DEEPNORM_ALPHA = (2.0 * DEPTH) ** 0.25
NEG_BIG = -30000.0

ENGS = ("pe", "act", "dve", "pool", "sp")


def _freeze(fn):
    import types
    if fn is None or fn.__closure__ is None:
        return fn
    cells = []
    for c in fn.__closure__:
        try:
            cells.append(types.CellType(c.cell_contents))
        except ValueError:
            cells.append(c)
    return types.FunctionType(fn.__code__, fn.__globals__, fn.__name__, fn.__defaults__, tuple(cells))


class KB:
    def __init__(self, nc, same_engine_sync=True):
        self.nc = nc
        self.same_engine_sync = same_engine_sync
        self.ops = {e: [] for e in ENGS}
        self.esem = {e: nc.alloc_semaphore("es_" + e) for e in ("pe", "act", "dve", "pool")}
        self.ecnt = {e: 0 for e in ("pe", "act", "dve", "pool")}
        self.dsem = {}
        self.last_w = {}
        self.readers = {}
        self.known = {e: {} for e in ENGS}
        self.n_dma_sems = 0

    def _need(self, eng, tok, waits):
        if tok is None:
            return
        sid, sem, val, teng = tok
        if teng == eng and eng == "pe":
            return
        if teng == eng and not self.same_engine_sync:
            return
        if self.known[eng].get(sid, 0) >= val:
            return
        for w in waits:
            if w[0] == sid:
                if w[2] < val:
                    w[2] = val
                return
        waits.append([sid, sem, val])

    def _deps(self, eng, reads, writes):
        waits = []
        for k in reads:
            self._need(eng, self.last_w.get(k), waits)
        for k in writes:
            self._need(eng, self.last_w.get(k), waits)
            for t in self.readers.get(k, ()):
                self._need(eng, t, waits)
        for sid, sem, val in waits:
            self.known[eng][sid] = val
        return waits

    def _commit(self, tok, reads, writes):
        for k in reads:
            lst = self.readers.setdefault(k, [])
            for i, t in enumerate(lst):
                if t[0] == tok[0]:
                    lst[i] = tok
                    break
            else:
                lst.append(tok)
        for k in writes:
            self.last_w[k] = tok
            self.readers[k] = []

    def op(self, eng, fn, reads=(), writes=()):
        fn = _freeze(fn)
        waits = self._deps(eng, reads, writes)
        self.ecnt[eng] += 1
        val = self.ecnt[eng]
        sem = self.esem[eng]
        tok = (id(sem), sem, val, eng)
        self.ops[eng].append((waits, fn, sem, 1))
        self._commit(tok, reads, writes)
        return tok

    def dma(self, eng, fn, reads=(), writes=(), semkey=None):
        fn = _freeze(fn)
        waits = self._deps(eng, reads, writes)
        if semkey is None:
            semkey = writes[0] if writes else reads[0]
        ent = self.dsem.get(semkey)
        if ent is None:
            ent = [self.nc.alloc_semaphore("ds%d" % self.n_dma_sems), 0]
            self.n_dma_sems += 1
            self.dsem[semkey] = ent
        ent[1] += 16
        sem, val = ent
        tok = (id(sem), sem, val, "dma")
        self.ops[eng].append((waits, fn, sem, 16))
        self._commit(tok, reads, writes)
        return tok

    def final_wait(self, eng, keys):
        waits = []
        for k in keys:
            self._need(eng, self.last_w.get(k), waits)
            for t in self.readers.get(k, ()):
                self._need(eng, t, waits)
        for sid, sem, val in waits:
            self.known[eng][sid] = val
        self.ops[eng].append((waits, None, None, 0))

    def emit(self):
        nc = self.nc
        ops = self.ops
        with nc.Block() as block:
            def run(e, engobj):
                for waits, fn, sem, inc in ops[e]:
                    for sid, wsem, val in waits:
                        engobj.wait_ge(wsem, val)
                    if fn is not None:
                        fn(engobj).then_inc(sem, inc)

            @block.tensor
            def _(eng):
                run("pe", eng)

            @block.scalar
            def _(eng):
                run("act", eng)

            @block.vector
            def _(eng):
                run("dve", eng)

            @block.gpsimd
            def _(eng):
                run("pool", eng)

            @block.sync
            def _(eng):
                run("sp", eng)


def _bf16_round(a):
    a = np.ascontiguousarray(a, dtype=np.float32)
    u = a.view(np.uint32).astype(np.uint64)
    r = ((u + 0x7FFF + ((u >> 16) & 1)) >> 16) << 16
    return r.astype(np.uint32).view(np.float32)


def attn_consts(head):
    kp = np.arange(128, dtype=np.float64)
    ident = np.eye(128, dtype=np.float32)
    maskA = np.zeros((128, 2, 2, 2, 128), np.float32)
    tri = (kp[:, None] > kp[None, :]).astype(np.float32) * NEG_BIG
    maskA[:, 0, :, 0, :] = tri[:, None, :]
    maskA[:, 1, :, 0, :] = NEG_BIG
    maskA[:, 1, :, 1, :] = tri[:, None, :]
    slope_a = 2.0 ** (-8.0 * (head + 1) / 4.0)
    m = np.arange(130, dtype=np.float64)
    biasA = (slope_a * (kp[:, None] + 128.0 * (m[None, :] - 128.0))).astype(np.float32)
    biasB = np.zeros((128, 2, 2, 4, 128), np.float32)
    ql = kp
    for g in range(4):
        slope = 2.0 ** (-8.0 * (4 * head + g + 1) / 16.0)
        dist_prev = ql[None, :] + 128.0 - kp[:, None]
        vprev = np.where(kp[:, None] > ql[None, :], -8.0 * slope * dist_prev, NEG_BIG)
        dist_cur = ql[None, :] - kp[:, None]
        vcur = np.where(kp[:, None] <= ql[None, :], -8.0 * slope * dist_cur, NEG_BIG)
        for t, v in enumerate((vprev, vcur)):
            v32 = v.astype(np.float32)
            hi = _bf16_round(v32)
            lo = (v32 - hi).astype(np.float32)
            biasB[:, t, 0, g, :] = hi
            biasB[:, t, 1, g, :] = lo
    return {
        "c_ident": ident,
        "c_maskA": maskA.reshape(128, 2 * 512),
        "c_biasA": biasA,
        "c_biasB": biasB.reshape(128, 4 * 512),
    }


def emit_attn(nc, kb, S, lam_init, io, pfx="a"):
    NSB = S // 256
    NCH = S // 128
    scaleA = 128.0 ** -0.5

    def sb(name, shape, dt):
        return nc.alloc_sbuf_tensor(pfx + name, shape, dt)

    def ps(name):
        return nc.alloc_psum_tensor(pfx + name, [128, 512], F32)

    K = lambda s: pfx + s

    ident = sb("ident", [128, 128], BF16)
    maskA = sb("maskA", [128, 2, 512], BF16)
    biasA = sb("biasA", [128, 130], F32)
    biasB = sb("biasB", [128, 2, 2, 512], BF16)
    wq = sb("wq", [128, 8, 256], BF16)
    wk = sb("wk", [128, 8, 256], BF16)
    wv = sb("wv", [128, 8, 256], BF16)
    wqb = sb("wqb", [128, 8, 256], BF16)
    wkb2 = sb("wkb2", [128, 8, 128], BF16)
    wvb = sb("wvb", [128, 8, 64], BF16)
    lq = sb("lq", [128, 512], F32)
    subg = sb("subg", [128, 256], F32)
    gsub = sb("gsub", [128, 256], F32)
    sinks = sb("sinks", [128, 4], F32)
    expsink = sb("expsink", [128, 4], F32)
    s12 = sb("s12", [128, 2], F32)
    e12 = sb("e12", [128, 2], F32)
    lamt = sb("lamt", [128, 1], F32)
    neglam = sb("neglam", [128, 1], F32)
    junk = sb("junk", [128, 256], F32)

    kT = sb("kT", [128, 2, S], BF16)
    va = sb("va", [128, NCH, 257], BF16)
    xt = [sb("xt%d" % i, [128, 8, 256], BF16) for i in range(2)]
    qT = sb("qT", [128, 2, 2, 128], BF16)
    qbz = sb("qbz", [128, 2, 4, 128], BF16)
    kbT = sb("kbT", [128, 4, 128], BF16)
    vb = sb("vb", [128, 4, 65], BF16)
    pt = [sb("pt%d" % i, [128, 512], BF16) for i in range(4)]
    t1 = sb("t1", [128, 256], F32)
    osb = sb("osb", [128, 256], F32)
    oa_sb = [sb("oa_sb%d" % i, [128, 256], F32) for i in range(2)]
    ob_sb = [sb("ob_sb%d" % i, [128, 256], F32) for i in range(2)]
    r0 = sb("r0", [128, 1], F32)
    r1 = sb("r1", [128, 1], F32)
    ss = sb("ss", [128, 1], F32)
    lnv = sb("lnv", [128, 1], F32)
    rstd = sb("rstd", [128, 1], F32)
    lt = sb("lt", [128, 4], F32)
    rl = sb("rl", [128, 4], F32)

    acc = [[ps("acc%d%d" % (c, s)) for s in range(2)] for c in range(2)]
    gbank = [ps("g%d" % i) for i in range(4)]
    gstate = {"i": 0}

    def gnext():
        i = gstate["i"] % 4
        gstate["i"] += 1
        return gbank[i], K("g%d" % i), i

    op, dma = kb.op, kb.dma
    import os
    dbg = int(os.environ.get("ATT_DBG", "9"))

    def cast_load(dst, dst_key, src_ap):
        dma("pool", lambda e: e.dma_start(out=dst, in_=src_ap), writes=[dst_key])

    cast_load(ident[:], K("ident"), io["c_ident"])
    cast_load(maskA[:], K("maskA"), io["c_maskA"].rearrange("p (d n) -> p d n", d=2))
    dma("sp", lambda e: e.dma_start(out=biasA[:], in_=io["c_biasA"]), writes=[K("biasA")])
    cast_load(biasB[:], K("biasB"), io["c_biasB"].rearrange("p (t h n) -> p t h n", t=2, h=2))
    for wt, nm in ((wq, "wq"), (wk, "wk"), (wv, "wv"), (wqb, "wqb"), (wkb2, "wkb2"), (wvb, "wvb")):
        cast_load(wt[:], K(nm), io[nm].rearrange("(kc p) n -> p kc n", p=128))
    dma("sp", lambda e: e.dma_start(out=lq[:], in_=io["lamqk"].partition_broadcast(128)), writes=[K("lq")])
    dma("sp", lambda e: e.dma_start(out=subg[:], in_=io["subg"].partition_broadcast(128)), writes=[K("subg")])
    dma("sp", lambda e: e.dma_start(out=sinks[:], in_=io["sinks4"].partition_broadcast(128)), writes=[K("sinks")])

    for i in range(2):
        op("dve", lambda e, i=i: e.scalar_tensor_tensor(
            out=junk[:, 0:128], in0=lq[:, 256 * i:256 * i + 128], scalar=1.0,
            in1=lq[:, 256 * i + 128:256 * i + 256], op0=ALU.mult, op1=ALU.mult,
            accum_out=s12[:, i:i + 1]), reads=[K("lq")], writes=[K("junk"), K("s12")])
    op("act", lambda e: e.activation(out=e12[:], in_=s12[:], func=AF.Exp), reads=[K("s12")], writes=[K("e12")])
    op("dve", lambda e: e.tensor_tensor(out=lamt[:], in0=e12[:, 1:2], in1=e12[:, 0:1], op=ALU.subtract),
       reads=[K("e12")], writes=[K("lamt")])
    op("dve", lambda e: e.tensor_scalar(out=neglam[:], in0=lamt[:], scalar1=-float(lam_init), scalar2=None,
                                        op0=ALU.add), reads=[K("lamt")], writes=[K("neglam")])
    op("dve", lambda e: e.tensor_scalar(out=gsub[:], in0=subg[:], scalar1=float(1.0 - lam_init), scalar2=None,
                                        op0=ALU.mult), reads=[K("subg")], writes=[K("gsub")])
    op("act", lambda e: e.activation(out=expsink[:], in_=sinks[:], func=AF.Exp), reads=[K("sinks")],
       writes=[K("expsink")])
    op("dve", lambda e: e.memset(va[:, :, 256:257], 1.0), writes=[K("va_ones")])
    op("dve", lambda e: e.memset(vb[:, :, 64:65], 1.0), writes=[K("vb_ones")])
    op("dve", lambda e: e.memset(qbz[:], 0.0), writes=[K("qbz")])

    xT_v = io["xT"].rearrange("(kc p) s -> p kc s", p=128)

    def load_xt(I):
        b = I % 2
        dma("pool", lambda e: e.dma_start(out=xt[b][:], in_=xT_v[:, :, I * 256:(I + 1) * 256]),
            writes=[K("xt%d" % b)])

    state = {"pp": 0, "st": 0}

    def proj_group(mm_list, evac_dst, evac_keys, ncols, evacs=None):
        bank, bk, _ = gnext()
        dstp = bank[:, 0:ncols]
        n = len(mm_list)
        for i, (l, r, rk) in enumerate(mm_list):
            op("pe", lambda e, l=l, r=r, i=i: e.matmul(dstp, lhsT=l, rhs=r, start=(i == 0), stop=(i == n - 1)),
               reads=rk, writes=[bk])
        if evacs is None:
            evacs = [(evac_dst, dstp)]
        else:
            evacs = [(d_, f_(bank)) for d_, f_ in evacs]
        for d_, s_ in evacs:
            op("dve", lambda e, d_=d_, s_=s_: e.tensor_copy(out=d_, in_=s_), reads=[bk], writes=evac_keys)

    def projections(I):
        b = I % 2
        xk = K("xt%d" % b)
        x = xt[b]
        for c in range(2):
            proj_group([(wq[:, kc, c * 128:(c + 1) * 128], x[:, kc, :], [K("wq"), xk]) for kc in range(8)],
                       qT[:, c, :, :].rearrange("p s q -> p (s q)"), [K("qT%d" % c)], 256)
        for c in range(2):
            proj_group([(wk[:, kc, c * 128:(c + 1) * 128], x[:, kc, :], [K("wk"), xk]) for kc in range(8)],
                       kT[:, c, I * 256:(I + 1) * 256], [K("kT%d_%d" % (c, I))], 256)
        for s in range(2):
            proj_group([(x[:, kc, s * 128:(s + 1) * 128], wv[:, kc, :], [K("wv"), xk]) for kc in range(8)],
                       va[:, 2 * I + s, 0:256], [K("va_%d" % (2 * I + s))], 256)
        for p in range(2):
            evs = []
            for half in range(2):
                g = 2 * p + half
                evs.append((qbz[half * 64:(half + 1) * 64, :, g, :],
                            lambda t, half=half: t[half * 64:(half + 1) * 64, 0:256].rearrange("p (s q) -> p s q", s=2)))
            proj_group([(wqb[:, kc, p * 128:(p + 1) * 128], x[:, kc, :], [K("wqb"), xk]) for kc in range(8)],
                       None, [K("qbz")], 256, evacs=evs)
        s0 = (2 * I) % 4
        proj_group([(wkb2[:, kc, :], x[:, kc, :], [K("wkb2"), xk]) for kc in range(8)],
                   kbT[:, s0:s0 + 2, :].rearrange("p s q -> p (s q)"), [K("kbT%d" % s0), K("kbT%d" % (s0 + 1))], 256)
        for s in range(2):
            proj_group([(x[:, kc, s * 128:(s + 1) * 128], wvb[:, kc, :], [K("wvb"), xk]) for kc in range(8)],
                       vb[:, s0 + s, 0:64], [K("vb%d" % (s0 + s))], 64)

    def diff_attn(I):
        nch = 2 * I + 2

        def qk(j):
            bank, bk, b = gnext()
            stv = bank[:].rearrange("p (c s q) -> p c s q", c=2, s=2)
            d = j - 2 * I
            diag = d >= 0
            for c in range(2):
                op("pe", lambda e, c=c: e.matmul(stv[:, c, :, :], lhsT=kT[:, c, j * 128:(j + 1) * 128],
                                                 rhs=qT[:, c, :, :], start=(c == 0), stop=(c == 1 and not diag)),
                   reads=[K("kT%d_%d" % (c, j // 2)), K("qT%d" % c)], writes=[bk])
            if diag:
                op("pe", lambda e: e.matmul(bank[:], lhsT=ident[:], rhs=maskA[:, d, :], start=False, stop=True),
                   reads=[K("ident"), K("maskA")], writes=[bk])
            return bank, bk, b

        def ex(j, bb):
            bank, bk, b = bb
            m = (j - 2 * I) + 128
            op("act", lambda e: e.activation(out=pt[b][:], in_=bank[:], func=AF.Exp, bias=biasA[:, m:m + 1],
                                             scale=scaleA),
               reads=[bk, K("biasA")], writes=[K("pt%d_0" % b), K("pt%d_1" % b)])

        def av(j, bb):
            bank, bk, b = bb
            ptv = pt[b][:].rearrange("p (c s q) -> p c s q", c=2, s=2)
            for c in range(2):
                for s in range(2):
                    op("pe", lambda e, c=c, s=s: e.matmul(acc[c][s][:, 0:257], lhsT=ptv[:, c, s, :], rhs=va[:, j, :],
                                                          start=(j == 0), stop=(j == nch - 1)),
                       reads=[K("pt%d_%d" % (b, s)), K("va_%d" % j), K("va_ones")], writes=[K("acc%d%d" % (c, s))])

        LA = 2
        bufs = {}
        for j in range(min(LA, nch)):
            bufs[j] = qk(j)
        for j in range(nch):
            if j + LA < nch:
                bufs[j + LA] = qk(j + LA)
            ex(j, bufs[j])
            av(j, bufs[j])
            del bufs[j]

    def diff_final(I):
        for s in range(2):
            diff_final_s(I, s)

    def diff_final_s(I, s):
        if True:
            a0, a1 = acc[0][s], acc[1][s]
            ob_ = oa_sb[s]
            op("dve", lambda e: e.reciprocal(out=r0[:], in_=a0[:, 256:257]), reads=[K("acc0%d" % s)], writes=[K("r0")])
            op("dve", lambda e: e.reciprocal(out=r1[:], in_=a1[:, 256:257]), reads=[K("acc1%d" % s)], writes=[K("r1")])
            op("dve", lambda e: e.tensor_tensor(out=r1[:], in0=r1[:], in1=neglam[:], op=ALU.mult),
               reads=[K("r1"), K("neglam")], writes=[K("r1")])
            op("dve", lambda e: e.tensor_scalar(out=t1[:], in0=a1[:, 0:256], scalar1=r1[:, 0:1], scalar2=None,
                                                op0=ALU.mult), reads=[K("acc1%d" % s), K("r1")], writes=[K("t1")])
            op("dve", lambda e: e.scalar_tensor_tensor(out=osb[:], in0=a0[:, 0:256], scalar=r0[:, 0:1], in1=t1[:],
                                                       op0=ALU.mult, op1=ALU.add),
               reads=[K("acc0%d" % s), K("r0"), K("t1")], writes=[K("osb")])
            op("dve", lambda e: e.scalar_tensor_tensor(out=junk[:], in0=osb[:], scalar=1.0, in1=osb[:],
                                                       op0=ALU.mult, op1=ALU.mult, accum_out=ss[:]),
               reads=[K("osb")], writes=[K("junk"), K("ss")])
            op("act", lambda e: e.activation(out=lnv[:], in_=ss[:], func=AF.Ln, bias=LN_EPS_TILE[0][:, 0:1],
                                             scale=1.0 / 256.0), reads=[K("ss"), K("epst")], writes=[K("lnv")])
            op("act", lambda e: e.activation(out=rstd[:], in_=lnv[:], func=AF.Exp, scale=-0.5),
               reads=[K("lnv")], writes=[K("rstd")])
            op("dve", lambda e, ob_=ob_: e.scalar_tensor_tensor(out=ob_[:], in0=osb[:], scalar=rstd[:, 0:1],
                                                                in1=gsub[:], op0=ALU.mult, op1=ALU.mult),
               reads=[K("osb"), K("rstd"), K("gsub")], writes=[K("oa_sb%d" % s)])
            r0_ = I * 256 + s * 128
            dma("sp", lambda e, ob_=ob_, r0_=r0_: e.dma_start(out=io["oa"][r0_:r0_ + 128, :], in_=ob_[:]),
                reads=[K("oa_sb%d" % s)], writes=[K("oa_out")], semkey=K("oa_sb%d" % s))

    def swa(I):
        for sblk in range(2):
            swa_block(I, sblk)

    def swa_block(I, sblk):
        if True:
            n = 2 * I + sblk
            slot = n % 4
            chunks = []
            if n > 0:
                chunks.append(((n - 1) % 4, 0))
            chunks.append((slot, 1))
            bl = []
            for (cs, t) in chunks:
                bank, bk, b = gnext()
                bl.append((bank, bk, b))
                op("pe", lambda e, cs=cs, bank=bank: e.matmul(
                    bank[:].rearrange("p (g q) -> p g q", g=4), lhsT=kbT[:, cs, :],
                    rhs=qbz[:, sblk, :, :], start=True, stop=False),
                   reads=[K("kbT%d" % cs), K("qbz")], writes=[bk])
                for hl in range(2):
                    op("pe", lambda e, t=t, hl=hl, bank=bank: e.matmul(bank[:], lhsT=ident[:], rhs=biasB[:, t, hl, :],
                                                                      start=False, stop=(hl == 1)),
                       reads=[K("ident"), K("biasB")], writes=[bk])
                op("act", lambda e, b=b, bank=bank: e.activation(out=pt[b][:], in_=bank[:], func=AF.Exp, scale=0.125),
                   reads=[bk], writes=[K("pt%d_0" % b), K("pt%d_1" % b)])
            obps, obk, _ = gnext()
            nchk = len(chunks)
            for g in range(4):
                for ci, (cs, t) in enumerate(chunks):
                    b = bl[ci][2]
                    op("pe", lambda e, g=g, cs=cs, b=b, ci=ci: e.matmul(
                        obps[:, g * 65:(g + 1) * 65], lhsT=pt[b][:, g * 128:(g + 1) * 128], rhs=vb[:, cs, :],
                        start=(ci == 0), stop=(ci == nchk - 1)),
                       reads=[K("pt%d_0" % b), K("pt%d_1" % b), K("vb%d" % cs), K("vb_ones")], writes=[obk])
            obv = obps[:, 0:260].rearrange("p (g e) -> p g e", g=4)
            op("dve", lambda e: e.tensor_tensor(out=lt[:], in0=obv[:, :, 64], in1=expsink[:], op=ALU.add),
               reads=[obk, K("expsink")], writes=[K("lt")])
            op("dve", lambda e: e.reciprocal(out=rl[:], in_=lt[:]), reads=[K("lt")], writes=[K("rl")])
            ob_ = ob_sb[sblk]
            for g in range(4):
                op("dve", lambda e, g=g, ob_=ob_: e.tensor_scalar(out=ob_[:, g * 64:(g + 1) * 64], in0=obv[:, g, 0:64],
                                                                  scalar1=rl[:, g:g + 1], scalar2=None, op0=ALU.mult),
                   reads=[obk, K("rl")], writes=[K("ob_sb%d" % sblk)])
            dma("sp", lambda e, ob_=ob_, n=n: e.dma_start(out=io["ob"][n * 128:(n + 1) * 128, :], in_=ob_[:]),
                reads=[K("ob_sb%d" % sblk)], writes=[K("ob_out")], semkey=K("ob_sb%d" % sblk))

    epst = sb("epst", [128, 1], F32)
    LN_EPS_TILE = [epst]
    op("dve", lambda e: e.memset(epst[:], LN_EPS), writes=[K("epst")])

    load_xt(0)
    if NSB > 1:
        load_xt(1)
    projections(0)
    for I in range(NSB):
        diff_attn(I)
        swa(I)
        if I + 1 < NSB:
            projections(I + 1)
            if I + 2 < NSB:
                load_xt(I + 2)
        diff_final(I)
    return [K("ob_out"), K("oa_out")]


def attn_decl(nc, S, pfx=""):
    io = {}
    def di(name, shape):
        io[name] = nc.dram_tensor(pfx + name, shape, F32, kind="ExternalInput").ap()
    di("xT", [1024, S])
    for nm in ("wq", "wk", "wv", "wqb"):
        di(nm, [1024, 256])
    di("wkb2", [1024, 128])
    di("wvb", [1024, 64])
    di("lamqk", [1, 512])
    di("subg", [1, 256])
    di("sinks4", [1, 4])
    di("c_ident", [128, 128])
    di("c_maskA", [128, 1024])
    di("c_biasA", [128, 130])
    di("c_biasB", [128, 2048])
    io["oa"] = nc.dram_tensor(pfx + "oa", [S, 256], F32, kind="ExternalOutput").ap()
    io["ob"] = nc.dram_tensor(pfx + "ob", [S, 256], F32, kind="ExternalOutput").ap()
    return io


def attn_inputs(layer, head, xT_b, w_in, lambda_qk, subln_g, sinks):
    W = w_in[layer]
    h = head
    kvh = h // 2
    wkb = W[:, 4096 + kvh * 64:4096 + (kvh + 1) * 64]
    m = {
        "xT": xT_b,
        "wq": np.ascontiguousarray(W[:, h * 256:(h + 1) * 256]),
        "wk": np.ascontiguousarray(W[:, 1024 + h * 256:1024 + (h + 1) * 256]),
        "wv": np.ascontiguousarray(W[:, 2048 + h * 256:2048 + (h + 1) * 256]),
        "wqb": np.ascontiguousarray(W[:, 3072 + h * 256:3072 + (h + 1) * 256]),
        "wkb2": np.ascontiguousarray(np.concatenate([wkb, wkb], axis=1)),
        "wvb": np.ascontiguousarray(W[:, 4224 + kvh * 64:4224 + (kvh + 1) * 64]),
        "lamqk": np.ascontiguousarray(lambda_qk[layer].reshape(1, 512)),
        "subg": np.ascontiguousarray(subln_g[layer].reshape(1, 256)),
        "sinks4": np.ascontiguousarray(sinks[layer, 4 * h:4 * h + 4].reshape(1, 4)),
    }
    m.update(attn_consts(h))
    return m


def lam_init_of(layer):
    return 0.8 - 0.6 * math.exp(-0.3 * layer)


def build_attn_program(S, layer):
    nc = bass.Bass("TRN2", target_bir_lowering=False)
    io = attn_decl(nc, S)
    kb = KB(nc)
    outs = emit_attn(nc, kb, S, lam_init_of(layer), io)
    kb.final_wait("sp", outs)
    kb.emit()
    return nc


def kb_barrier(kb):
    toks = []
    for e in ("pe", "act", "dve", "pool"):
        if kb.ecnt[e] > 0:
            toks.append((id(kb.esem[e]), kb.esem[e], kb.ecnt[e], e))
    for key, (sem, val) in kb.dsem.items():
        if val > 0:
            toks.append((id(sem), sem, val, "dma"))
    for eng in ENGS:
        waits = []
        for sid, sem, val, teng in toks:
            if teng == eng:
                continue
            if kb.known[eng].get(sid, 0) >= val:
                continue
            waits.append([sid, sem, val])
            kb.known[eng][sid] = val
        kb.ops[eng].append((waits, None, None, 0))


def tok_consts(C):
    t = np.arange(128)
    return {
        "c_ident": np.eye(128, dtype=np.float32),
        "c_upper": (t[:, None] < t[None, :]).astype(np.float32),
        "c_ecap": np.broadcast_to((np.arange(32, dtype=np.float32) * C)[None, :], (128, 32)).copy(),
    }


def emit_tok(nc, kb, T, C, io, pfx="t"):
    import contextlib
    NT = T // 128
    CT = C // 128
    NH = (C + 511) // 512
    CH = C // NH
    NSLOT = 32 * C
    alpha = float(DEEPNORM_ALPHA)
    op, dma = kb.op, kb.dma
    K = lambda s: pfx + s
    psb = [nc.alloc_psum_tensor(pfx + "ps%d" % i, [128, 512], F32) for i in range(8)]
    pstate = {"i": 0}

    def pbank():
        i = pstate["i"] % 8
        pstate["i"] += 1
        return psb[i], K("ps%d" % i)

    def sbp(name, shape, dt):
        return nc.alloc_sbuf_tensor(pfx + name, shape, dt)

    identbf = sbp("identbf", [128, 128], BF16)
    ident32 = sbp("ident32", [128, 128], F32)
    upper = sbp("upper", [128, 128], BF16)
    onesbf = sbp("onesbf", [128, 128], BF16)
    ecap = sbp("ecap", [128, 32], F32)
    lng = sbp("lng", [128, 2, 1024], F32)
    lnb = sbp("lnb", [128, 2, 1024], F32)
    brt = sbp("brt", [128, 32], F32)
    wr32 = sbp("wr32", [128, 8, 32], F32)
    base = sbp("base", [128, 32], F32)
    desti = sbp("desti", [128, NT, 4], I32)
    gkall = sbp("gkall", [128, NT, 4], F32)
    epst = sbp("epst", [128, 1], F32)
    bgT = sbp("bgT", [128, 256], F32)
    buT = sbp("buT", [128, 256], F32)

    dma("pool", lambda e: e.dma_start(out=identbf[:], in_=io["c_ident"]), writes=[K("identbf")])
    dma("sp", lambda e: e.dma_start(out=ident32[:], in_=io["c_ident"]), writes=[K("ident32")])
    dma("pool", lambda e: e.dma_start(out=upper[:], in_=io["c_upper"]), writes=[K("upper")])
    dma("sp", lambda e: e.dma_start(out=ecap[:], in_=io["c_ecap"]), writes=[K("ecap")])
    for j in range(2):
        dma("sp", lambda e, j=j: e.dma_start(out=lng[:, j, :], in_=io["ln_g"][j:j + 1, :].partition_broadcast(128)),
            writes=[K("lng%d" % j)])
        dma("sp", lambda e, j=j: e.dma_start(out=lnb[:, j, :], in_=io["ln_b"][j:j + 1, :].partition_broadcast(128)),
            writes=[K("lnb%d" % j)])
    dma("sp", lambda e: e.dma_start(out=brt[:], in_=io["b_router"].partition_broadcast(128)), writes=[K("brt")])
    dma("sp", lambda e: e.dma_start(out=wr32[:], in_=io["w_router"].rearrange("(kc p) n -> p kc n", p=128)),
        writes=[K("wr32")])
    op("dve", lambda e: e.memset(onesbf[:], 1.0), writes=[K("onesbf")])
    op("dve", lambda e: e.memset(base[:], 0.0), writes=[K("base")])
    op("dve", lambda e: e.memset(epst[:], LN_EPS), writes=[K("epst")])

    def layer_norm(pre, sums, j, out, tmp, tagk, small, extra_out_keys=()):
        negmean, ssq, lnv, rstd, nb = small
        kp, ko, kt = K(tagk + "pre"), K(tagk + "out"), K(tagk + "tmp")
        ks = K(tagk + "small")
        op("dve", lambda e: e.tensor_tensor(out=negmean[:], in0=sums[:, 0:1], in1=sums[:, 1:2], op=ALU.add),
           reads=[K(tagk + "sums")], writes=[ks + "nm"])
        op("dve", lambda e: e.tensor_scalar(out=negmean[:], in0=negmean[:], scalar1=-1.0 / 1024.0, scalar2=None,
                                            op0=ALU.mult), reads=[ks + "nm"], writes=[ks + "nm"])
        op("act", lambda e: e.activation(out=tmp[:], in_=pre[:], func=AF.Square, bias=negmean[:, 0:1], scale=1.0,
                                         accum_out=ssq[:]), reads=[kp, ks + "nm"], writes=[kt, ks + "ssq"])
        op("act", lambda e: e.activation(out=lnv[:], in_=ssq[:], func=AF.Ln, bias=epst[:, 0:1], scale=1.0 / 1024.0),
           reads=[ks + "ssq", K("epst")], writes=[ks + "lnv"])
        op("act", lambda e: e.activation(out=rstd[:], in_=lnv[:], func=AF.Exp, scale=-0.5),
           reads=[ks + "lnv"], writes=[ks + "rstd"])
        op("dve", lambda e: e.tensor_tensor(out=nb[:], in0=negmean[:], in1=rstd[:], op=ALU.mult),
           reads=[ks + "nm", ks + "rstd"], writes=[ks + "nb"])
        op("act", lambda e: e.activation(out=tmp[:], in_=pre[:], func=AF.Identity, bias=nb[:, 0:1], scale=rstd[:, 0:1]),
           reads=[kp, ks + "nb", ks + "rstd"], writes=[kt])
        op("dve", lambda e: e.tensor_tensor(out=tmp[:], in0=tmp[:], in1=lng[:, j, :], op=ALU.mult),
           reads=[kt, K("lng%d" % j)], writes=[kt])
        op("dve", lambda e: e.tensor_tensor(out=out[:], in0=tmp[:], in1=lnb[:, j, :], op=ALU.add),
           reads=[kt, K("lnb%d" % j)], writes=[ko] + list(extra_out_keys))

    with contextlib.ExitStack() as es:
        def sb(name, shape, dt):
            return es.enter_context(nc.sbuf_tensor(pfx + name, shape, dt))
        wg = sb("wg", [128, 8, 2048], BF16)
        wo = sb("wo", [128, 8, 1024], BF16)
        xt_ = [sb("x%d" % i, [128, 1024], F32) for i in range(3)]
        oat = [sb("oa%d" % i, [128, 1024], F32) for i in range(2)]
        obt = [sb("ob%d" % i, [128, 1024], F32) for i in range(2)]
        xTt = [sb("xT%d" % i, [128, 8, 128], BF16) for i in range(2)]
        sig = sb("sig", [128, 2048], F32)
        m1 = sb("m1", [128, 1024], F32)
        m2 = sb("m2", [128, 1024], F32)
        mg = [sb("mg%d" % i, [128, 1024], BF16) for i in range(2)]
        mT = sb("mT", [128, 8, 128], BF16)
        pre = sb("pre", [128, 1024], F32)
        tmp = sb("tmp", [128, 1024], F32)
        hh = [sb("h%d" % i, [128, 1024], F32) for i in range(2)]
        hbf = [sb("hbf%d" % i, [128, 1024], BF16) for i in range(3)]
        hT32 = sb("hT32", [128, 8, 128], F32)
        sums = sb("sums", [128, 2], F32)
        small = [sb("sm%d" % i, [128, 1], F32) for i in range(5)]
        logit = [sb("logit%d" % i, [128, 32], F32) for i in range(2)]
        m8 = sb("m8", [128, 8], F32)
        negm0 = sb("negm0", [128, 1], F32)
        exl = sb("exl", [128, 32], F32)
        mask = sb("mask", [128, 32], F32)
        maskbf = sb("maskbf", [128, 32], BF16)
        gun = sb("gun", [128, 32], F32)
        den = sb("den", [128, 1], F32)
        gd = sb("gd", [128, 32], F32)
        slotf = sb("slotf", [128, 32], F32)
        oh = sb("oh", [128, 32], F32)
        junk32 = sb("junk32", [128, 32], F32)
        destf = sb("destf", [128, 4], F32)

        dma("pool", lambda e: e.dma_start(out=wg[:, 0:4], in_=io["wg"].rearrange("(kc p) n -> p kc n", p=128)[:, 0:4]),
            writes=[K("wg_a")])
        dma("pool", lambda e: e.dma_start(out=wg[:, 4:8], in_=io["wg"].rearrange("(kc p) n -> p kc n", p=128)[:, 4:8]),
            writes=[K("wg_b")])
        dma("pool", lambda e: e.dma_start(out=wo[:], in_=io["w_o"].rearrange("(kc p) n -> p kc n", p=128)),
            writes=[K("wo")])
        xT_v = io["xT"].rearrange("(kc p) t -> p kc t", p=128)
        zt = sb("zt", [128, CT, 1024], BF16)
        op("pool", lambda e: e.memset(zt[:], 0.0), writes=[K("zt")])
        for ex in range(N_EXPERTS):
            dma("sp", lambda e: e.dma_start(out=io["xbuf"][ex * C:(ex + 1) * C, :].rearrange("(t p) d -> p t d", p=128),
                                            in_=zt[:]), reads=[K("zt")], writes=[K("xz%d" % ex)], semkey=K("xz"))

        def loads(i):
            b = i % 2
            r = slice(i * 128, (i + 1) * 128)
            dma("pool", lambda e: e.dma_start(out=xTt[b][:], in_=xT_v[:, :, r]), writes=[K("xT%d" % b)])
            b3 = i % 3
            dma("sp", lambda e: e.dma_start(out=xt_[b3][:], in_=io["x"][r, :]), writes=[K("x%d" % b3)])
            dma("sp", lambda e: e.dma_start(out=oat[b][:], in_=io["oa"][r, :]), writes=[K("oa%d" % b)])
            dma("sp", lambda e: e.dma_start(out=obt[b][:], in_=io["ob"][r, :]), writes=[K("ob%d" % b)])

        def stage1(i):
            b = i % 2
            for cb in range(4):
                bank, bk = pbank()
                for kc in range(8):
                    op("pe", lambda e: e.matmul(bank[:], lhsT=xTt[b][:, kc, :], rhs=wg[:, kc, cb * 512:(cb + 1) * 512],
                                                start=(kc == 0), stop=(kc == 7)),
                       reads=[K("xT%d" % b), K("wg_a"), K("wg_b")], writes=[bk])
                op("act", lambda e: e.activation(out=sig[:, cb * 512:(cb + 1) * 512], in_=bank[:], func=AF.Sigmoid),
                   reads=[bk], writes=[K("sig%d" % cb)])
            op("dve", lambda e: e.tensor_tensor(out=m1[:], in0=sig[:, 0:1024], in1=oat[b][:], op=ALU.mult),
               reads=[K("sig0"), K("sig1"), K("oa%d" % b)], writes=[K("m1")])
            op("dve", lambda e: e.tensor_tensor(out=m2[:], in0=sig[:, 1024:2048], in1=obt[b][:], op=ALU.mult),
               reads=[K("sig2"), K("sig3"), K("ob%d" % b)], writes=[K("m2")])
            op("dve", lambda e: e.tensor_tensor(out=mg[b][:], in0=m1[:], in1=m2[:], op=ALU.add),
               reads=[K("m1"), K("m2")], writes=[K("mg%d" % b)])

        def stage2(i):
            b = i % 2
            b3 = i % 3
            bank, bk = pbank()
            bankbf = bank[:].bitcast(BF16)
            for kc in range(8):
                op("pe", lambda e: e.transpose(out=bankbf[:, kc * 128:(kc + 1) * 128],
                                               in_=mg[b][:, kc * 128:(kc + 1) * 128], identity=identbf[:]),
                   reads=[K("mg%d" % b), K("identbf")], writes=[bk])
            op("dve", lambda e: e.tensor_copy(out=mT[:].rearrange("p k t -> p (k t)"), in_=bankbf),
               reads=[bk], writes=[K("mT")])
            for dh in range(2):
                bank, bk = pbank()
                for kc in range(8):
                    op("pe", lambda e: e.matmul(bank[:], lhsT=mT[:, kc, :], rhs=wo[:, kc, dh * 512:(dh + 1) * 512],
                                                start=(kc == 0), stop=(kc == 7)),
                       reads=[K("mT"), K("wo")], writes=[bk])
                op("dve", lambda e: e.scalar_tensor_tensor(
                    out=pre[:, dh * 512:(dh + 1) * 512], in0=xt_[b3][:, dh * 512:(dh + 1) * 512], scalar=alpha,
                    in1=bank[:], op0=ALU.mult, op1=ALU.add, accum_out=sums[:, dh:dh + 1]),
                   reads=[bk, K("x%d" % b3)], writes=[K("l1pre"), K("l1sums")])
            h = hh[b]
            layer_norm(pre, sums, 0, h, tmp, "l1", small, extra_out_keys=[K("h%d" % b)])
            dma("sp", lambda e: e.dma_start(out=io["hbuf"][i * 128:(i + 1) * 128, :], in_=h[:]),
                reads=[K("l1out"), K("h%d" % b)], writes=[K("hbuf%d" % i)], semkey=K("hst%d" % b))
            op("act", lambda e: e.activation(out=hbf[b3][:], in_=h[:], func=AF.Copy), reads=[K("l1out"), K("h%d" % b)],
               writes=[K("hbf%d" % b3)])

        def stage3(i):
            b = i % 2
            h = hh[b]
            for half in range(2):
                bank, bk = pbank()
                for q in range(4):
                    kc = half * 4 + q
                    op("pe", lambda e: e.transpose(out=bank[:, q * 128:(q + 1) * 128], in_=h[:, kc * 128:(kc + 1) * 128],
                                                   identity=ident32[:]),
                       reads=[K("h%d" % b), K("ident32")], writes=[bk])
                op("dve", lambda e: e.tensor_copy(
                    out=hT32[:, half * 4:(half + 1) * 4, :].rearrange("p k t -> p (k t)"), in_=bank[:]),
                   reads=[bk], writes=[K("hT32_%d" % half)])
            bank, bk = pbank()
            for kc in range(8):
                op("pe", lambda e: e.matmul(bank[:, 0:32], lhsT=hT32[:, kc, :], rhs=wr32[:, kc, :],
                                            start=(kc == 0), stop=(kc == 7)),
                   reads=[K("hT32_0"), K("hT32_1"), K("wr32")], writes=[bk])
            lg = logit[b]
            op("dve", lambda e: e.tensor_tensor(out=lg[:], in0=bank[:, 0:32], in1=brt[:], op=ALU.add),
               reads=[bk, K("brt")], writes=[K("logit%d" % b)])

        def stage4(i):
            b = i % 2
            b3 = i % 3
            lg = logit[b]
            lk = K("logit%d" % b)
            op("dve", lambda e: e.max(out=m8[:], in_=lg[:]), reads=[lk], writes=[K("m8")])
            op("dve", lambda e: e.tensor_scalar(out=negm0[:], in0=m8[:, 0:1], scalar1=-1.0, scalar2=None, op0=ALU.mult),
               reads=[K("m8")], writes=[K("negm0")])
            op("act", lambda e: e.activation(out=exl[:], in_=lg[:], func=AF.Exp, bias=negm0[:, 0:1], scale=1.0),
               reads=[lk, K("negm0")], writes=[K("exl")])
            op("dve", lambda e: e.tensor_scalar(out=mask[:], in0=lg[:], scalar1=m8[:, 3:4], scalar2=None,
                                                op0=ALU.is_ge), reads=[lk, K("m8")], writes=[K("mask")])
            op("dve", lambda e: e.scalar_tensor_tensor(out=gun[:], in0=exl[:], scalar=1.0, in1=mask[:], op0=ALU.mult,
                                                       op1=ALU.mult, accum_out=den[:]),
               reads=[K("exl"), K("mask")], writes=[K("gun"), K("den")])
            op("dve", lambda e: e.reciprocal(out=den[:], in_=den[:]), reads=[K("den")], writes=[K("den")])
            op("dve", lambda e: e.tensor_scalar(out=gd[:], in0=gun[:], scalar1=den[:, 0:1], scalar2=None, op0=ALU.mult),
               reads=[K("gun"), K("den")], writes=[K("gd")])
            op("dve", lambda e: e.tensor_copy(out=maskbf[:], in_=mask[:]), reads=[K("mask")], writes=[K("maskbf")])
            bank, bk = pbank()
            op("pe", lambda e: e.matmul(bank[:, 0:32], lhsT=upper[:], rhs=maskbf[:], start=True, stop=True),
               reads=[K("upper"), K("maskbf")], writes=[bk])
            op("pe", lambda e: e.matmul(bank[:, 32:64], lhsT=onesbf[:], rhs=maskbf[:], start=True, stop=True),
               reads=[K("onesbf"), K("maskbf")], writes=[bk])
            op("dve", lambda e: e.tensor_tensor(out=slotf[:], in0=bank[:, 0:32], in1=base[:], op=ALU.add),
               reads=[bk, K("base")], writes=[K("slotf")])
            op("dve", lambda e: e.scalar_tensor_tensor(out=slotf[:], in0=slotf[:], scalar=float(C - 1), in1=ecap[:],
                                                       op0=ALU.min, op1=ALU.add),
               reads=[K("slotf"), K("ecap")], writes=[K("slotf")])
            op("dve", lambda e: e.tensor_tensor(out=base[:], in0=bank[:, 32:64], in1=base[:], op=ALU.add),
               reads=[bk, K("base")], writes=[K("base")])
            for k in range(4):
                op("dve", lambda e: e.tensor_scalar(out=oh[:], in0=lg[:], scalar1=m8[:, k:k + 1], scalar2=None,
                                                    op0=ALU.is_equal), reads=[lk, K("m8")], writes=[K("oh")])
                op("dve", lambda e: e.scalar_tensor_tensor(out=junk32[:], in0=oh[:], scalar=1.0, in1=slotf[:],
                                                           op0=ALU.mult, op1=ALU.mult, accum_out=destf[:, k:k + 1]),
                   reads=[K("oh"), K("slotf")], writes=[K("junk32"), K("destf")])
                op("dve", lambda e: e.scalar_tensor_tensor(out=junk32[:], in0=oh[:], scalar=1.0, in1=gd[:],
                                                           op0=ALU.mult, op1=ALU.mult, accum_out=gkall[:, i, k:k + 1]),
                   reads=[K("oh"), K("gd")], writes=[K("junk32"), K("gk%d" % i)])
            op("dve", lambda e: e.tensor_copy(out=desti[:, i, :], in_=destf[:]), reads=[K("destf")],
               writes=[K("desti%d" % i)])
            for k in range(4):
                dma("pool", lambda e: e.indirect_dma_start(
                    out=io["xbuf"][:, :], out_offset=bass.IndirectOffsetOnAxis(ap=desti[:, i, k:k + 1], axis=0),
                    in_=hbf[b3][:, :], in_offset=None),
                    reads=[K("hbf%d" % b3), K("desti%d" % i), K("xz%d" % (N_EXPERTS - 1))],
                    writes=[K("xbuf_%d_%d" % (i, k))], semkey=K("hsc%d" % b3))

        loads(0)
        if NT > 1:
            loads(1)
        for t in range(NT + 3):
            if t < NT:
                stage1(t)
            if 0 <= t - 1 < NT:
                stage2(t - 1)
            if t + 2 < NT:
                loads(t + 2)
            if 0 <= t - 2 < NT:
                stage3(t - 2)
            if 0 <= t - 3 < NT:
                stage4(t - 3)
        kb_barrier(kb)

    with contextlib.ExitStack() as es:
        def sb(name, shape, dt):
            return es.enter_context(nc.sbuf_tensor(pfx + name, shape, dt))
        wup = [sb("wup%d" % i, [128, 8, 2048], BF16) for i in range(2)]
        wdn = [sb("wdn%d" % i, [128, 8, 1024], BF16) for i in range(2)]
        bdn = [sb("bdn%d" % i, [128, 1024], F32) for i in range(2)]
        xrows = [sb("xrows%d" % i, [128, CT, 1024], BF16) for i in range(2)]
        xTe = sb("xTe", [128, 8, C], BF16)
        actT = sb("actT", [128, 8, C], BF16)
        gsb = [sb("gsb%d" % i, [128, CH], F32) for i in range(2)]
        sgb = [sb("sgb%d" % i, [128, CH], F32) for i in range(2)]
        usb = [sb("usb%d" % i, [128, CH], F32) for i in range(2)]
        ysb = [sb("ysb%d" % i, [128, 1024], F32) for i in range(2)]
        bupr = [sb("bupr%d" % i, [128, 256], F32) for i in range(2)]

        bup_v = io["b_up"].rearrange("e (fo n) -> (e fo) n", n=256)
        for j in range(2):
            dma("sp", lambda e, j=j: e.dma_start(out=bupr[j][:], in_=bup_v[j * 128:(j + 1) * 128, :]),
                writes=[K("bupr%d" % j)])
        for gu, dst in ((0, bgT), (1, buT)):
            bank, bk = pbank()
            for j in range(2):
                src = bupr[j][:].rearrange("p (f two) -> p f two", two=2)[:, :, gu]
                op("pe", lambda e, j=j, src=src, bank=bank: e.transpose(out=bank[:, j * 128:(j + 1) * 128], in_=src,
                                                                       identity=ident32[:]),
                   reads=[K("bupr%d" % j), K("ident32")], writes=[bk])
            op("dve", lambda e, bank=bank, dst=dst: e.tensor_copy(out=dst[:], in_=bank[:, 0:256]), reads=[bk],
               writes=[K("bgu%d" % gu)])
        b7g = sb("b7g", [128, 256], F32)
        bu1 = sb("bu1", [128, 256], F32)
        c7 = sb("c7", [128, 1], F32)
        op("dve", lambda e: e.tensor_scalar(out=b7g[:], in0=bgT[:], scalar1=-1.0, scalar2=7.0, op0=ALU.mult, op1=ALU.add),
           reads=[K("bgu0")], writes=[K("b7g")])
        op("dve", lambda e: e.tensor_scalar(out=bu1[:], in0=buT[:], scalar1=1.0, scalar2=None, op0=ALU.add),
           reads=[K("bgu1")], writes=[K("bu1")])
        op("dve", lambda e: e.memset(c7[:], 7.0 * 1.702), writes=[K("c7")])

        def wloads(ex):
            wb = ex % 2
            upv = io["w_up"][ex].rearrange("(kc p) n -> p kc n", p=128)
            for q in range(4):
                dma("pool", lambda e, q=q: e.dma_start(out=wup[wb][:, 2 * q:2 * q + 2], in_=upv[:, 2 * q:2 * q + 2]),
                    writes=[K("wup%d_%d" % (wb, q))])
            dnv = io["w_down"][ex].rearrange("(kc p) n -> p kc n", p=128)
            for q in range(2):
                dma("pool", lambda e, q=q: e.dma_start(out=wdn[wb][:, 4 * q:4 * q + 4], in_=dnv[:, 4 * q:4 * q + 4]),
                    writes=[K("wdn%d_%d" % (wb, q))])
            dma("sp", lambda e: e.dma_start(out=bdn[wb][:], in_=io["b_down"][ex:ex + 1, :].partition_broadcast(128)),
                writes=[K("bdn%d" % wb)])
            dma("sp", lambda e: e.dma_start(out=xrows[wb][:],
                                            in_=io["xbuf"][ex * C:(ex + 1) * C, :].rearrange("(t p) d -> p t d", p=128)),
                writes=[K("xrows%d" % wb)])

        tstate = {"i": 0}

        def expert(ex):
            wb = ex % 2
            wupv = wup[wb][:].rearrange("p k (f two) -> p k f two", two=2)
            for t in range(CT):
                bank, bk = pbank()
                bankbf = bank[:].bitcast(BF16)
                for kc in range(8):
                    op("pe", lambda e, kc=kc, t=t, bankbf=bankbf: e.transpose(
                        out=bankbf[:, kc * 128:(kc + 1) * 128], in_=xrows[wb][:, t, kc * 128:(kc + 1) * 128],
                        identity=identbf[:]), reads=[K("xrows%d" % wb), K("identbf")], writes=[bk])
                op("dve", lambda e, t=t, bankbf=bankbf: e.tensor_copy(
                    out=xTe[:, :, t * 128:(t + 1) * 128], in_=bankbf.rearrange("p (k q) -> p k q", k=8)),
                   reads=[bk], writes=[K("xTe%d" % t)])
            xkeys = [K("xTe%d" % t) for t in range(CT)]
            for hf in range(NH):
                cs = slice(hf * CH, (hf + 1) * CH)
                for fo in range(8):
                    banks = []
                    for gu in range(2):
                        bank, bk = pbank()
                        banks.append((bank, bk))
                        for kc in range(8):
                            op("pe", lambda e, kc=kc, gu=gu, bank=bank: e.matmul(
                                bank[:, 0:CH], lhsT=wupv[:, kc, fo * 128:(fo + 1) * 128, gu], rhs=xTe[:, kc, cs],
                                start=(kc == 0), stop=(kc == 7)),
                               reads=xkeys + [K("wup%d_%d" % (wb, kc // 2))], writes=[bk])
                    tb = tstate["i"] % 2
                    tstate["i"] += 1
                    g_, s_, u_ = gsb[tb], sgb[tb], usb[tb]
                    col = ex * 8 + fo
                    (bg_, bgk), (bu_, buk) = banks
                    op("act", lambda e: e.activation(out=g_[:], in_=bg_[:, 0:CH], func=AF.Relu,
                                                     bias=b7g[:, col:col + 1], scale=-1.0),
                       reads=[bgk, K("b7g")], writes=[K("g%d" % tb)])
                    op("act", lambda e: e.activation(out=s_[:], in_=g_[:], func=AF.Silu, bias=c7[:, 0:1], scale=-1.702),
                       reads=[K("g%d" % tb), K("c7")], writes=[K("s%d" % tb)])
                    op("dve", lambda e: e.tensor_scalar(out=u_[:], in0=bu_[:, 0:CH], scalar1=bu1[:, col:col + 1],
                                                        scalar2=8.0, op0=ALU.add, op1=ALU.min),
                       reads=[buk, K("bu1")], writes=[K("u%d" % tb)])
                    op("dve", lambda e: e.scalar_tensor_tensor(out=actT[:, fo, cs], in0=u_[:], scalar=-6.0, in1=s_[:],
                                                               op0=ALU.max, op1=ALU.mult),
                       reads=[K("u%d" % tb), K("s%d" % tb)], writes=[K("actT%d_%d" % (hf, fo))])
            akeys = [K("actT%d_%d" % (hf, fo)) for hf in range(NH) for fo in range(8)]
            for t in range(CT):
                yb = (ex * CT + t) % 2
                y_ = ysb[yb]
                for dh in range(2):
                    bank, bk = pbank()
                    for fo in range(8):
                        op("pe", lambda e, fo=fo, bank=bank: e.matmul(
                            bank[:], lhsT=actT[:, fo, t * 128:(t + 1) * 128], rhs=wdn[wb][:, fo, dh * 512:(dh + 1) * 512],
                            start=(fo == 0), stop=(fo == 7)),
                           reads=akeys + [K("wdn%d_%d" % (wb, fo // 4))], writes=[bk])
                    op("dve", lambda e, bank=bank, y_=y_: e.scalar_tensor_tensor(
                        out=y_[:, dh * 512:(dh + 1) * 512], in0=bank[:], scalar=1.0 / 1.702,
                        in1=bdn[wb][:, dh * 512:(dh + 1) * 512], op0=ALU.mult, op1=ALU.add),
                       reads=[bk, K("bdn%d" % wb)], writes=[K("ysb%d_%d" % (yb, dh))])
                r0 = ex * C + t * 128
                dma("sp", lambda e, y_=y_, r0=r0: e.dma_start(out=io["ybuf"][r0:r0 + 128, :], in_=y_[:]),
                    reads=[K("ysb%d_0" % yb), K("ysb%d_1" % yb)], writes=[K("ybuf_%d" % r0)], semkey=K("yst%d" % yb))

        wloads(0)
        for ex in range(N_EXPERTS):
            if ex + 1 < N_EXPERTS:
                wloads(ex + 1)
            expert(ex)
        kb_barrier(kb)

    with contextlib.ExitStack() as es:
        def sb(name, shape, dt):
            return es.enter_context(nc.sbuf_tensor(pfx + name, shape, dt))
        hb = [sb("hb%d" % i, [128, 1024], F32) for i in range(2)]
        yk = [[sb("yk%d_%d" % (i, k), [128, 1024], F32) for k in range(4)] for i in range(2)]
        ff = sb("ff", [128, 1024], F32)
        pre = sb("pre2", [128, 1024], F32)
        tmp = sb("tmp2", [128, 1024], F32)
        outt = [sb("outt%d" % i, [128, 1024], F32) for i in range(2)]
        sums = sb("sums2", [128, 2], F32)
        small = [sb("sm2_%d" % i, [128, 1], F32) for i in range(5)]

        def loads3(i):
            b = i % 2
            dma("sp", lambda e: e.dma_start(out=hb[b][:], in_=io["hbuf"][i * 128:(i + 1) * 128, :]), writes=[K("hb%d" % b)])
            for k in range(4):
                dma("pool", lambda e, k=k: e.indirect_dma_start(
                    out=yk[b][k][:, :], out_offset=None, in_=io["ybuf"][:, :],
                    in_offset=bass.IndirectOffsetOnAxis(ap=desti[:, i, k:k + 1], axis=0)),
                    reads=[K("desti%d" % i)], writes=[K("yk%d_%d" % (b, k))])

        def tile3(i):
            b = i % 2
            for k in range(4):
                if k == 0:
                    op("dve", lambda e: e.tensor_scalar(out=ff[:], in0=yk[b][0][:], scalar1=gkall[:, i, 0:1], scalar2=None,
                                                        op0=ALU.mult), reads=[K("yk%d_0" % b), K("gk%d" % i)],
                       writes=[K("ff")])
                else:
                    op("dve", lambda e, k=k: e.scalar_tensor_tensor(out=ff[:], in0=yk[b][k][:], scalar=gkall[:, i, k:k + 1],
                                                                    in1=ff[:], op0=ALU.mult, op1=ALU.add),
                       reads=[K("yk%d_%d" % (b, k)), K("gk%d" % i), K("ff")], writes=[K("ff")])
            for dh in range(2):
                op("dve", lambda e, dh=dh: e.scalar_tensor_tensor(
                    out=pre[:, dh * 512:(dh + 1) * 512], in0=hb[b][:, dh * 512:(dh + 1) * 512], scalar=alpha,
                    in1=ff[:, dh * 512:(dh + 1) * 512], op0=ALU.mult, op1=ALU.add, accum_out=sums[:, dh:dh + 1]),
                   reads=[K("hb%d" % b), K("ff")], writes=[K("l2pre"), K("l2sums")])
            o_ = outt[b]
            layer_norm(pre, sums, 1, o_, tmp, "l2", small)
            dma("sp", lambda e: e.dma_start(out=io["xo"][i * 128:(i + 1) * 128, :], in_=o_[:]),
                reads=[K("l2out")], writes=[K("xo_out")], semkey=K("ost%d" % b))

        loads3(0)
        for i in range(NT):
            if i + 1 < NT:
                loads3(i + 1)
            tile3(i)
    return [K("xo_out")]


def tok_decl(nc, T, C, pfx=""):
    io = {}
    def di(name, shape):
        io[name] = nc.dram_tensor(pfx + name, shape, F32, kind="ExternalInput").ap()
    di("x", [T, 1024]); di("xT", [1024, T]); di("oa", [T, 1024]); di("ob", [T, 1024])
    di("wg", [1024, 2048]); di("w_o", [1024, 1024]); di("ln_g", [2, 1024]); di("ln_b", [2, 1024])
    di("w_router", [1024, 32]); di("b_router", [1, 32])
    di("w_up", [32, 1024, 2048]); di("b_up", [32, 2048]); di("w_down", [32, 1024, 1024]); di("b_down", [32, 1024])
    di("c_ident", [128, 128]); di("c_upper", [128, 128]); di("c_ecap", [128, 32])
    io["xo"] = nc.dram_tensor(pfx + "xo", [T, 1024], F32, kind="ExternalOutput").ap()
    io["hbuf"] = nc.dram_tensor(pfx + "hbuf", [T, 1024], F32, kind="Internal").ap()
    io["xbuf"] = nc.dram_tensor(pfx + "xbuf", [32 * C, 1024], BF16, kind="Internal").ap()
    io["ybuf"] = nc.dram_tensor(pfx + "ybuf", [32 * C, 1024], F32, kind="Internal").ap()
    return io


def build_tok_program(T, C):
    nc = bass.Bass("TRN2", target_bir_lowering=False)
    io = tok_decl(nc, T, C)
    kb = KB(nc)
    outs = emit_tok(nc, kb, T, C, io)
    kb.final_wait("sp", outs)
    kb.emit()
    return nc


def tok_inputs(layer, x_c, xT_c, oa_c, ob_c, inp, C):
    m = {
        "x": x_c, "xT": xT_c, "oa": oa_c, "ob": ob_c,
        "wg": np.ascontiguousarray(inp["w_in"][layer][:, 4352:6400]),
        "w_o": inp["w_o"][layer], "ln_g": inp["ln_g"][layer], "ln_b": inp["ln_b"][layer],
        "w_router": inp["w_router"][layer], "b_router": np.ascontiguousarray(inp["b_router"][layer].reshape(1, 32)),
        "w_up": inp["w_up"][layer], "b_up": inp["b_up"][layer], "w_down": inp["w_down"][layer],
        "b_down": inp["b_down"][layer],
    }
    m.update(tok_consts(C))
    return m


CAP = 768
_PROG_CACHE = {}


def _attn_prog(layer):
    key = ("attn", layer)
    if key not in _PROG_CACHE:
        _PROG_CACHE[key] = build_attn_program(SEQ, layer)
    return _PROG_CACHE[key]


def _tok_prog():
    key = ("tok",)
    if key not in _PROG_CACHE:
        _PROG_CACHE[key] = build_tok_program(BATCH * SEQ // 8, CAP)
    return _PROG_CACHE[key]


def kernel(x, w_in, w_o, lambda_qk, subln_g, sinks, ln_g, ln_b, w_router, b_router, w_up, b_up, w_down, b_down):
    f32 = lambda a: np.ascontiguousarray(np.asarray(a, dtype=np.float32))
    inp = {"w_in": f32(w_in), "w_o": f32(w_o), "ln_g": f32(ln_g), "ln_b": f32(ln_b), "w_router": f32(w_router),
           "b_router": f32(b_router), "w_up": f32(w_up), "b_up": f32(b_up), "w_down": f32(w_down),
           "b_down": f32(b_down)}
    lambda_qk, subln_g, sinks = f32(lambda_qk), f32(subln_g), f32(sinks)
    xcur = f32(x)
    T = BATCH * SEQ // 8
    for layer in range(DEPTH):
        xTs = [np.ascontiguousarray(xcur[b].T) for b in range(BATCH)]
        in_maps = [attn_inputs(layer, c % 4, xTs[c // 4], inp["w_in"], lambda_qk, subln_g, sinks) for c in range(8)]
        res = run_bass_kernel_spmd(_attn_prog(layer), in_maps, core_ids=list(range(8)))
        oa = np.empty((BATCH, SEQ, D_MODEL), np.float32)
        ob = np.empty((BATCH, SEQ, D_MODEL), np.float32)
        for c in range(8):
            b, h = c // 4, c % 4
            oa[b, :, h * 256:(h + 1) * 256] = res.results[c]["oa"]
            ob[b, :, h * 256:(h + 1) * 256] = res.results[c]["ob"]
        del res, in_maps, xTs
        xf = xcur.reshape(-1, D_MODEL)
        oaf = oa.reshape(-1, D_MODEL)
        obf = ob.reshape(-1, D_MODEL)
        in_maps = []
        for c in range(8):
            sl = slice(c * T, (c + 1) * T)
            in_maps.append(tok_inputs(layer, np.ascontiguousarray(xf[sl]), np.ascontiguousarray(xf[sl].T),
                                      np.ascontiguousarray(oaf[sl]), np.ascontiguousarray(obf[sl]), inp, CAP))
        res = run_bass_kernel_spmd(_tok_prog(), in_maps, core_ids=list(range(8)))
        xcur = np.concatenate([res.results[c]["xo"] for c in range(8)], axis=0).reshape(BATCH, SEQ, D_MODEL)
        del res, in_maps
    return xcur
```

```python
import math
import numpy as np
import concourse.bass as bass
import concourse.mybir as mybir
from concourse.bass_utils import run_bass_kernel_spmd

F32 = mybir.dt.float32
BF16 = mybir.dt.bfloat16
U32 = mybir.dt.uint32
I32 = mybir.dt.int32
ALU = mybir.AluOpType
AF = mybir.ActivationFunctionType
AX = mybir.AxisListType

D_MODEL = 1024
BATCH = 2
SEQ = 16384
DEPTH = 4
N_EXPERTS = 32
TOP_K = 4
LN_EPS = 1e-5
DEEPNORM_ALPHA = (2.0 * DEPTH) ** 0.25
NEG_BIG = -30000.0

ENGS = ("pe", "act", "dve", "pool", "sp")


def _freeze(fn):
    import types
    if fn is None or fn.__closure__ is None:
        return fn
    cells = []
    for c in fn.__closure__:
        try:
            cells.append(types.CellType(c.cell_contents))
        except ValueError:
            cells.append(c)
    return types.FunctionType(fn.__code__, fn.__globals__, fn.__name__, fn.__defaults__, tuple(cells))


class KB:
    def __init__(self, nc, same_engine_sync=True):
        self.nc = nc
        self.same_engine_sync = same_engine_sync
        self.ops = {e: [] for e in ENGS}
        self.esem = {e: nc.alloc_semaphore("es_" + e) for e in ("pe", "act", "dve", "pool")}
        self.ecnt = {e: 0 for e in ("pe", "act", "dve", "pool")}
        self.dsem = {}
        self.last_w = {}
        self.readers = {}
        self.known = {e: {} for e in ENGS}
        self.n_dma_sems = 0

    def _need(self, eng, tok, waits):
        if tok is None:
            return
        sid, sem, val, teng = tok
        if teng == eng and eng == "pe":
            return
        if teng == eng and not self.same_engine_sync:
            return
        if self.known[eng].get(sid, 0) >= val:
            return
        for w in waits:
            if w[0] == sid:
                if w[2] < val:
                    w[2] = val
                return
        waits.append([sid, sem, val])

    def _deps(self, eng, reads, writes):
        waits = []
        for k in reads:
            self._need(eng, self.last_w.get(k), waits)
        for k in writes:
            self._need(eng, self.last_w.get(k), waits)
            for t in self.readers.get(k, ()):
                self._need(eng, t, waits)
        for sid, sem, val in waits:
            self.known[eng][sid] = val
        return waits

    def _commit(self, tok, reads, writes):
        for k in reads:
            lst = self.readers.setdefault(k, [])
            for i, t in enumerate(lst):
                if t[0] == tok[0]:
                    lst[i] = tok
                    break
            else:
                lst.append(tok)
        for k in writes:
            self.last_w[k] = tok
            self.readers[k] = []

    def op(self, eng, fn, reads=(), writes=()):
        fn = _freeze(fn)
        waits = self._deps(eng, reads, writes)
        self.ecnt[eng] += 1
        val = self.ecnt[eng]
        sem = self.esem[eng]
        tok = (id(sem), sem, val, eng)
        self.ops[eng].append((waits, fn, sem, 1))
        self._commit(tok, reads, writes)
        return tok

    def dma(self, eng, fn, reads=(), writes=(), semkey=None):
        fn = _freeze(fn)
        waits = self._deps(eng, reads, writes)
        if semkey is None:
            semkey = writes[0] if writes else reads[0]
        ent = self.dsem.get(semkey)
        if ent is None:
            ent = [self.nc.alloc_semaphore("ds%d" % self.n_dma_sems), 0]
            self.n_dma_sems += 1
            self.dsem[semkey] = ent
        ent[1] += 16
        sem, val = ent
        tok = (id(sem), sem, val, "dma")
        self.ops[eng].append((waits, fn, sem, 16))
        self._commit(tok, reads, writes)
        return tok

    def final_wait(self, eng, keys):
        waits = []
        for k in keys:
            self._need(eng, self.last_w.get(k), waits)
            for t in self.readers.get(k, ()):
                self._need(eng, t, waits)
        for sid, sem, val in waits:
            self.known[eng][sid] = val
        self.ops[eng].append((waits, None, None, 0))

    def emit(self):
        nc = self.nc
        ops = self.ops
        with nc.Block() as block:
            def run(e, engobj):
                for waits, fn, sem, inc in ops[e]:
                    for sid, wsem, val in waits:
                        engobj.wait_ge(wsem, val)
                    if fn is not None:
                        fn(engobj).then_inc(sem, inc)

            @block.tensor
            def _(eng):
                run("pe", eng)

            @block.scalar
            def _(eng):
                run("act", eng)

            @block.vector
            def _(eng):
                run("dve", eng)

            @block.gpsimd
            def _(eng):
                run("pool", eng)

            @block.sync
            def _(eng):
                run("sp", eng)


def _bf16_round(a):
    a = np.ascontiguousarray(a, dtype=np.float32)
    u = a.view(np.uint32).astype(np.uint64)
    r = ((u + 0x7FFF + ((u >> 16) & 1)) >> 16) << 16
    return r.astype(np.uint32).view(np.float32)


def attn_consts(head):
    kp = np.arange(128, dtype=np.float64)
    ident = np.eye(128, dtype=np.float32)
    maskA = np.zeros((128, 2, 2, 2, 128), np.float32)
    tri = (kp[:, None] > kp[None, :]).astype(np.float32) * NEG_BIG
    maskA[:, 0, :, 0, :] = tri[:, None, :]
    maskA[:, 1, :, 0, :] = NEG_BIG
    maskA[:, 1, :, 1, :] = tri[:, None, :]
    slope_a = 2.0 ** (-8.0 * (head + 1) / 4.0)
    m = np.arange(130, dtype=np.float64)
    biasA = (slope_a * (kp[:, None] + 128.0 * (m[None, :] - 128.0))).astype(np.float32)
    biasB = np.zeros((128, 2, 2, 4, 128), np.float32)
    ql = kp
    for g in range(4):
        slope = 2.0 ** (-8.0 * (4 * head + g + 1) / 16.0)
        dist_prev = ql[None, :] + 128.0 - kp[:, None]
        vprev = np.where(kp[:, None] > ql[None, :], -8.0 * slope * dist_prev, NEG_BIG)
        dist_cur = ql[None, :] - kp[:, None]
        vcur = np.where(kp[:, None] <= ql[None, :], -8.0 * slope * dist_cur, NEG_BIG)
        for t, v in enumerate((vprev, vcur)):
            v32 = v.astype(np.float32)
            hi = _bf16_round(v32)
            lo = (v32 - hi).astype(np.float32)
            biasB[:, t, 0, g, :] = hi
            biasB[:, t, 1, g, :] = lo
    return {
        "c_ident": ident,
        "c_maskA": maskA.reshape(128, 2 * 512),
        "c_biasA": biasA,
        "c_biasB": biasB.reshape(128, 4 * 512),
    }


def emit_attn(nc, kb, S, lam_init, io, pfx="a"):
    NSB = S // 256
    NCH = S // 128
    scaleA = 128.0 ** -0.5

    def sb(name, shape, dt):
        return nc.alloc_sbuf_tensor(pfx + name, shape, dt)

    def ps(name):
        return nc.alloc_psum_tensor(pfx + name, [128, 512], F32)

    K = lambda s: pfx + s

    ident = sb("ident", [128, 128], BF16)
    maskA = sb("maskA", [128, 2, 512], BF16)
    biasA = sb("biasA", [128, 130], F32)
    biasB = sb("biasB", [128, 2, 2, 512], BF16)
    wq = sb("wq", [128, 8, 256], BF16)
    wk = sb("wk", [128, 8, 256], BF16)
    wv = sb("wv", [128, 8, 256], BF16)
    wqb = sb("wqb", [128, 8, 256], BF16)
    wkb2 = sb("wkb2", [128, 8, 128], BF16)
    wvb = sb("wvb", [128, 8, 64], BF16)
    lq = sb("lq", [128, 512], F32)
    subg = sb("subg", [128, 256], F32)
    gsub = sb("gsub", [128, 256], F32)
    sinks = sb("sinks", [128, 4], F32)
    expsink = sb("expsink", [128, 4], F32)
    s12 = sb("s12", [128, 2], F32)
    e12 = sb("e12", [128, 2], F32)
    lamt = sb("lamt", [128, 1], F32)
    neglam = sb("neglam", [128, 1], F32)
    junk = sb("junk", [128, 256], F32)

    kT = sb("kT", [128, 2, S], BF16)
    va = sb("va", [128, NCH, 257], BF16)
    xt = [sb("xt%d" % i, [128, 8, 256], BF16) for i in range(2)]
    qT = sb("qT", [128, 2, 2, 128], BF16)
    qbz = sb("qbz", [128, 2, 4, 128], BF16)
    kbT = sb("kbT", [128, 4, 128], BF16)
    vb = sb("vb", [128, 4, 65], BF16)
    pt = [sb("pt%d" % i, [128, 512], BF16) for i in range(4)]
    t1 = sb("t1", [128, 256], F32)
    osb = sb("osb", [128, 256], F32)
    oa_sb = [sb("oa_sb%d" % i, [128, 256], F32) for i in range(2)]
    ob_sb = [sb("ob_sb%d" % i, [128, 256], F32) for i in range(2)]
    r0 = sb("r0", [128, 1], F32)
    r1 = sb("r1", [128, 1], F32)
    ss = sb("ss", [128, 1], F32)
    lnv = sb("lnv", [128, 1], F32)
    rstd = sb("rstd", [128, 1], F32)
    lt = sb("lt", [128, 4], F32)
    rl = sb("rl", [128, 4], F32)

    acc = [[ps("acc%d%d" % (c, s)) for s in range(2)] for c in range(2)]
    gbank = [ps("g%d" % i) for i in range(4)]
    gstate = {"i": 0}

    def gnext():
        i = gstate["i"] % 4
        gstate["i"] += 1
        return gbank[i], K("g%d" % i), i

    op, dma = kb.op, kb.dma
    import os
    dbg = int(os.environ.get("ATT_DBG", "9"))

    def cast_load(dst, dst_key, src_ap):
        dma("pool", lambda e: e.dma_start(out=dst, in_=src_ap), writes=[dst_key])

    cast_load(ident[:], K("ident"), io["c_ident"])
    cast_load(maskA[:], K("maskA"), io["c_maskA"].rearrange("p (d n) -> p d n", d=2))
    dma("sp", lambda e: e.dma_start(out=biasA[:], in_=io["c_biasA"]), writes=[K("biasA")])
    cast_load(biasB[:], K("biasB"), io["c_biasB"].rearrange("p (t h n) -> p t h n", t=2, h=2))
    for wt, nm in ((wq, "wq"), (wk, "wk"), (wv, "wv"), (wqb, "wqb"), (wkb2, "wkb2"), (wvb, "wvb")):
        cast_load(wt[:], K(nm), io[nm].rearrange("(kc p) n -> p kc n", p=128))
    dma("sp", lambda e: e.dma_start(out=lq[:], in_=io["lamqk"].partition_broadcast(128)), writes=[K("lq")])
    dma("sp", lambda e: e.dma_start(out=subg[:], in_=io["subg"].partition_broadcast(128)), writes=[K("subg")])
    dma("sp", lambda e: e.dma_start(out=sinks[:], in_=io["sinks4"].partition_broadcast(128)), writes=[K("sinks")])

    for i in range(2):
        op("dve", lambda e, i=i: e.scalar_tensor_tensor(
            out=junk[:, 0:128], in0=lq[:, 256 * i:256 * i + 128], scalar=1.0,
            in1=lq[:, 256 * i + 128:256 * i + 256], op0=ALU.mult, op1=ALU.mult,
            accum_out=s12[:, i:i + 1]), reads=[K("lq")], writes=[K("junk"), K("s12")])
    op("act", lambda e: e.activation(out=e12[:], in_=s12[:], func=AF.Exp), reads=[K("s12")], writes=[K("e12")])
    op("dve", lambda e: e.tensor_tensor(out=lamt[:], in0=e12[:, 1:2], in1=e12[:, 0:1], op=ALU.subtract),
       reads=[K("e12")], writes=[K("lamt")])
    op("dve", lambda e: e.tensor_scalar(out=neglam[:], in0=lamt[:], scalar1=-float(lam_init), scalar2=None,
                                        op0=ALU.add), reads=[K("lamt")], writes=[K("neglam")])
    op("dve", lambda e: e.tensor_scalar(out=gsub[:], in0=subg[:], scalar1=float(1.0 - lam_init), scalar2=None,
                                        op0=ALU.mult), reads=[K("subg")], writes=[K("gsub")])
    op("act", lambda e: e.activation(out=expsink[:], in_=sinks[:], func=AF.Exp), reads=[K("sinks")],
       writes=[K("expsink")])
    op("dve", lambda e: e.memset(va[:, :, 256:257], 1.0), writes=[K("va_ones")])
    op("dve", lambda e: e.memset(vb[:, :, 64:65], 1.0), writes=[K("vb_ones")])
    op("dve", lambda e: e.memset(qbz[:], 0.0), writes=[K("qbz")])

    xT_v = io["xT"].rearrange("(kc p) s -> p kc s", p=128)

    def load_xt(I):
        b = I % 2
        dma("pool", lambda e: e.dma_start(out=xt[b][:], in_=xT_v[:, :, I * 256:(I + 1) * 256]),
            writes=[K("xt%d" % b)])

    state = {"pp": 0, "st": 0}

    def proj_group(mm_list, evac_dst, evac_keys, ncols, evacs=None):
        bank, bk, _ = gnext()
        dstp = bank[:, 0:ncols]
        n = len(mm_list)
        for i, (l, r, rk) in enumerate(mm_list):
            op("pe", lambda e, l=l, r=r, i=i: e.matmul(dstp, lhsT=l, rhs=r, start=(i == 0), stop=(i == n - 1)),
               reads=rk, writes=[bk])
        if evacs is None:
            evacs = [(evac_dst, dstp)]
        else:
            evacs = [(d_, f_(bank)) for d_, f_ in evacs]
        for d_, s_ in evacs:
            op("dve", lambda e, d_=d_, s_=s_: e.tensor_copy(out=d_, in_=s_), reads=[bk], writes=evac_keys)

    def projections(I):
        b = I % 2
        xk = K("xt%d" % b)
        x = xt[b]
        for c in range(2):
            proj_group([(wq[:, kc, c * 128:(c + 1) * 128], x[:, kc, :], [K("wq"), xk]) for kc in range(8)],
                       qT[:, c, :, :].rearrange("p s q -> p (s q)"), [K("qT%d" % c)], 256)
        for c in range(2):
            proj_group([(wk[:, kc, c * 128:(c + 1) * 128], x[:, kc, :], [K("wk"), xk]) for kc in range(8)],
                       kT[:, c, I * 256:(I + 1) * 256], [K("kT%d_%d" % (c, I))], 256)
        for s in range(2):
            proj_group([(x[:, kc, s * 128:(s + 1) * 128], wv[:, kc, :], [K("wv"), xk]) for kc in range(8)],
                       va[:, 2 * I + s, 0:256], [K("va_%d" % (2 * I + s))], 256)
        for p in range(2):
            evs = []
            for half in range(2):
                g = 2 * p + half
                evs.append((qbz[half * 64:(half + 1) * 64, :, g, :],
                            lambda t, half=half: t[half * 64:(half + 1) * 64, 0:256].rearrange("p (s q) -> p s q", s=2)))
            proj_group([(wqb[:, kc, p * 128:(p + 1) * 128], x[:, kc, :], [K("wqb"), xk]) for kc in range(8)],
                       None, [K("qbz")], 256, evacs=evs)
        s0 = (2 * I) % 4
        proj_group([(wkb2[:, kc, :], x[:, kc, :], [K("wkb2"), xk]) for kc in range(8)],
                   kbT[:, s0:s0 + 2, :].rearrange("p s q -> p (s q)"), [K("kbT%d" % s0), K("kbT%d" % (s0 + 1))], 256)
        for s in range(2):
            proj_group([(x[:, kc, s * 128:(s + 1) * 128], wvb[:, kc, :], [K("wvb"), xk]) for kc in range(8)],
                       vb[:, s0 + s, 0:64], [K("vb%d" % (s0 + s))], 64)

    def diff_attn(I):
        nch = 2 * I + 2

        def qk(j):
            bank, bk, b = gnext()
            stv = bank[:].rearrange("p (c s q) -> p c s q", c=2, s=2)
            d = j - 2 * I
            diag = d >= 0
            for c in range(2):
                op("pe", lambda e, c=c: e.matmul(stv[:, c, :, :], lhsT=kT[:, c, j * 128:(j + 1) * 128],
                                                 rhs=qT[:, c, :, :], start=(c == 0), stop=(c == 1 and not diag)),
                   reads=[K("kT%d_%d" % (c, j // 2)), K("qT%d" % c)], writes=[bk])
            if diag:
                op("pe", lambda e: e.matmul(bank[:], lhsT=ident[:], rhs=maskA[:, d, :], start=False, stop=True),
                   reads=[K("ident"), K("maskA")], writes=[bk])
            return bank, bk, b

        def ex(j, bb):
            bank, bk, b = bb
            m = (j - 2 * I) + 128
            op("act", lambda e: e.activation(out=pt[b][:], in_=bank[:], func=AF.Exp, bias=biasA[:, m:m + 1],
                                             scale=scaleA),
               reads=[bk, K("biasA")], writes=[K("pt%d_0" % b), K("pt%d_1" % b)])

        def av(j, bb):
            bank, bk, b = bb
            ptv = pt[b][:].rearrange("p (c s q) -> p c s q", c=2, s=2)
            for c in range(2):
                for s in range(2):
                    op("pe", lambda e, c=c, s=s: e.matmul(acc[c][s][:, 0:257], lhsT=ptv[:, c, s, :], rhs=va[:, j, :],
                                                          start=(j == 0), stop=(j == nch - 1)),
                       reads=[K("pt%d_%d" % (b, s)), K("va_%d" % j), K("va_ones")], writes=[K("acc%d%d" % (c, s))])

        LA = 2
        bufs = {}
        for j in range(min(LA, nch)):
            bufs[j] = qk(j)
        for j in range(nch):
            if j + LA < nch:
                bufs[j + LA] = qk(j + LA)
            ex(j, bufs[j])
            av(j, bufs[j])
            del bufs[j]

    def diff_final(I):
        for s in range(2):
            diff_final_s(I, s)

    def diff_final_s(I, s):
        if True:
            a0, a1 = acc[0][s], acc[1][s]
            ob_ = oa_sb[s]
            op("dve", lambda e: e.reciprocal(out=r0[:], in_=a0[:, 256:257]), reads=[K("acc0%d" % s)], writes=[K("r0")])
            op("dve", lambda e: e.reciprocal(out=r1[:], in_=a1[:, 256:257]), reads=[K("acc1%d" % s)], writes=[K("r1")])
            op("dve", lambda e: e.tensor_tensor(out=r1[:], in0=r1[:], in1=neglam[:], op=ALU.mult),
               reads=[K("r1"), K("neglam")], writes=[K("r1")])
            op("dve", lambda e: e.tensor_scalar(out=t1[:], in0=a1[:, 0:256], scalar1=r1[:, 0:1], scalar2=None,
                                                op0=ALU.mult), reads=[K("acc1%d" % s), K("r1")], writes=[K("t1")])
            op("dve", lambda e: e.scalar_tensor_tensor(out=osb[:], in0=a0[:, 0:256], scalar=r0[:, 0:1], in1=t1[:],
                                                       op0=ALU.mult, op1=ALU.add),
               reads=[K("acc0%d" % s), K("r0"), K("t1")], writes=[K("osb")])
            op("dve", lambda e: e.scalar_tensor_tensor(out=junk[:], in0=osb[:], scalar=1.0, in1=osb[:],
                                                       op0=ALU.mult, op1=ALU.mult, accum_out=ss[:]),
               reads=[K("osb")], writes=[K("junk"), K("ss")])
            op("act", lambda e: e.activation(out=lnv[:], in_=ss[:], func=AF.Ln, bias=LN_EPS_TILE[0][:, 0:1],
                                             scale=1.0 / 256.0), reads=[K("ss"), K("epst")], writes=[K("lnv")])
            op("act", lambda e: e.activation(out=rstd[:], in_=lnv[:], func=AF.Exp, scale=-0.5),
               reads=[K("lnv")], writes=[K("rstd")])
            op("dve", lambda e, ob_=ob_: e.scalar_tensor_tensor(out=ob_[:], in0=osb[:], scalar=rstd[:, 0:1],
                                                                in1=gsub[:], op0=ALU.mult, op1=ALU.mult),
               reads=[K("osb"), K("rstd"), K("gsub")], writes=[K("oa_sb%d" % s)])
            r0_ = I * 256 + s * 128
            dma("sp", lambda e, ob_=ob_, r0_=r0_: e.dma_start(out=io["oa"][r0_:r0_ + 128, :], in_=ob_[:]),
                reads=[K("oa_sb%d" % s)], writes=[K("oa_out")], semkey=K("oa_sb%d" % s))

    def swa(I):
        for sblk in range(2):
            swa_block(I, sblk)

    def swa_block(I, sblk):
        if True:
            n = 2 * I + sblk
            slot = n % 4
            chunks = []
            if n > 0:
                chunks.append(((n - 1) % 4, 0))
            chunks.append((slot, 1))
            bl = []
            for (cs, t) in chunks:
                bank, bk, b = gnext()
                bl.append((bank, bk, b))
                op("pe", lambda e, cs=cs, bank=bank: e.matmul(
                    bank[:].rearrange("p (g q) -> p g q", g=4), lhsT=kbT[:, cs, :],
                    rhs=qbz[:, sblk, :, :], start=True, stop=False),
                   reads=[K("kbT%d" % cs), K("qbz")], writes=[bk])
                for hl in range(2):
                    op("pe", lambda e, t=t, hl=hl, bank=bank: e.matmul(bank[:], lhsT=ident[:], rhs=biasB[:, t, hl, :],
                                                                      start=False, stop=(hl == 1)),
                       reads=[K("ident"), K("biasB")], writes=[bk])
                op("act", lambda e, b=b, bank=bank: e.activation(out=pt[b][:], in_=bank[:], func=AF.Exp, scale=0.125),
                   reads=[bk], writes=[K("pt%d_0" % b), K("pt%d_1" % b)])
            obps, obk, _ = gnext()
            nchk = len(chunks)
            for g in range(4):
                for ci, (cs, t) in enumerate(chunks):
                    b = bl[ci][2]
                    op("pe", lambda e, g=g, cs=cs, b=b, ci=ci: e.matmul(
                        obps[:, g * 65:(g + 1) * 65], lhsT=pt[b][:, g * 128:(g + 1) * 128], rhs=vb[:, cs, :],
                        start=(ci == 0), stop=(ci == nchk - 1)),
                       reads=[K("pt%d_0" % b), K("pt%d_1" % b), K("vb%d" % cs), K("vb_ones")], writes=[obk])
            obv = obps[:, 0:260].rearrange("p (g e) -> p g e", g=4)
            op("dve", lambda e: e.tensor_tensor(out=lt[:], in0=obv[:, :, 64], in1=expsink[:], op=ALU.add),
               reads=[obk, K("expsink")], writes=[K("lt")])
            op("dve", lambda e: e.reciprocal(out=rl[:], in_=lt[:]), reads=[K("lt")], writes=[K("rl")])
            ob_ = ob_sb[sblk]
            for g in range(4):
                op("dve", lambda e, g=g, ob_=ob_: e.tensor_scalar(out=ob_[:, g * 64:(g + 1) * 64], in0=obv[:, g, 0:64],
                                                                  scalar1=rl[:, g:g + 1], scalar2=None, op0=ALU.mult),
                   reads=[obk, K("rl")], writes=[K("ob_sb%d" % sblk)])
            dma("sp", lambda e, ob_=ob_, n=n: e.dma_start(out=io["ob"][n * 128:(n + 1) * 128, :], in_=ob_[:]),
                reads=[K("ob_sb%d" % sblk)], writes=[K("ob_out")], semkey=K("ob_sb%d" % sblk))

    epst = sb("epst", [128, 1], F32)
    LN_EPS_TILE = [epst]
    op("dve", lambda e: e.memset(epst[:], LN_EPS), writes=[K("epst")])

    load_xt(0)
    if NSB > 1:
        load_xt(1)
    projections(0)
    for I in range(NSB):
        diff_attn(I)
        swa(I)
        if I + 1 < NSB:
            projections(I + 1)
            if I + 2 < NSB:
                load_xt(I + 2)
        diff_final(I)
    return [K("ob_out"), K("oa_out")]


def attn_decl(nc, S, pfx=""):
    io = {}
    def di(name, shape):
        io[name] = nc.dram_tensor(pfx + name, shape, F32, kind="ExternalInput").ap()
    di("xT", [1024, S])
    for nm in ("wq", "wk", "wv", "wqb"):
        di(nm, [1024, 256])
    di("wkb2", [1024, 128])
    di("wvb", [1024, 64])
    di("lamqk", [1, 512])
    di("subg", [1, 256])
    di("sinks4", [1, 4])
    di("c_ident", [128, 128])
    di("c_maskA", [128, 1024])
    di("c_biasA", [128, 130])
    di("c_biasB", [128, 2048])
    io["oa"] = nc.dram_tensor(pfx + "oa", [S, 256], F32, kind="ExternalOutput").ap()
    io["ob"] = nc.dram_tensor(pfx + "ob", [S, 256], F32, kind="ExternalOutput").ap()
    return io


def attn_inputs(layer, head, xT_b, w_in, lambda_qk, subln_g, sinks):
    W = w_in[layer]
    h = head
    kvh = h // 2
    wkb = W[:, 4096 + kvh * 64:4096 + (kvh + 1) * 64]
    m = {
        "xT": xT_b,
        "wq": np.ascontiguousarray(W[:, h * 256:(h + 1) * 256]),
        "wk": np.ascontiguousarray(W[:, 1024 + h * 256:1024 + (h + 1) * 256]),
        "wv": np.ascontiguousarray(W[:, 2048 + h * 256:2048 + (h + 1) * 256]),
        "wqb": np.ascontiguousarray(W[:, 3072 + h * 256:3072 + (h + 1) * 256]),
        "wkb2": np.ascontiguousarray(np.concatenate([wkb, wkb], axis=1)),
        "wvb": np.ascontiguousarray(W[:, 4224 + kvh * 64:4224 + (kvh + 1) * 64]),
        "lamqk": np.ascontiguousarray(lambda_qk[layer].reshape(1, 512)),
        "subg": np.ascontiguousarray(subln_g[layer].reshape(1, 256)),
        "sinks4": np.ascontiguousarray(sinks[layer, 4 * h:4 * h + 4].reshape(1, 4)),
    }
    m.update(attn_consts(h))
    return m


def lam_init_of(layer):
    return 0.8 - 0.6 * math.exp(-0.3 * layer)


def build_attn_program(S, layer):
    nc = bass.Bass("TRN2", target_bir_lowering=False)
    io = attn_decl(nc, S)
    kb = KB(nc)
    outs = emit_attn(nc, kb, S, lam_init_of(layer), io)
    kb.final_wait("sp", outs)
    kb.emit()
    return nc


def kb_barrier(kb):
    toks = []
    for e in ("pe", "act", "dve", "pool"):
        if kb.ecnt[e] > 0:
            toks.append((id(kb.esem[e]), kb.esem[e], kb.ecnt[e], e))
    for key, (sem, val) in kb.dsem.items():
        if val > 0:
            toks.append((id(sem), sem, val, "dma"))
    for eng in ENGS:
        waits = []
        for sid, sem, val, teng in toks:
            if teng == eng:
                continue
            if kb.known[eng].get(sid, 0) >= val:
                continue
            waits.append([sid, sem, val])
            kb.known[eng][sid] = val
        kb.ops[eng].append((waits, None, None, 0))


def tok_consts(C):
    t = np.arange(128)
    return {
        "c_ident": np.eye(128, dtype=np.float32),
        "c_upper": (t[:, None] < t[None, :]).astype(np.float32),
        "c_ecap": np.broadcast_to((np.arange(32, dtype=np.float32) * C)[None, :], (128, 32)).copy(),
        "c_iota": np.broadcast_to(np.arange(32, dtype=np.float32)[None, :], (128, 32)).copy(),
    }


def emit_tok(nc, kb, T, C, io, pfx="t"):
    import contextlib
    NT = T // 128
    CT = C // 128
    NH = (C + 511) // 512
    CH = C // NH
    NSLOT = 32 * C
    alpha = float(DEEPNORM_ALPHA)
    op, dma = kb.op, kb.dma
    K = lambda s: pfx + s
    psb = [nc.alloc_psum_tensor(pfx + "ps%d" % i, [128, 512], F32) for i in range(8)]
    pstate = {"i": 0}

    def pbank():
        i = pstate["i"] % 8
        pstate["i"] += 1
        return psb[i], K("ps%d" % i)

    def sbp(name, shape, dt):
        return nc.alloc_sbuf_tensor(pfx + name, shape, dt)

    identbf = sbp("identbf", [128, 128], BF16)
    ident32 = sbp("ident32", [128, 128], F32)
    upper = sbp("upper", [128, 128], BF16)
    onesbf = sbp("onesbf", [128, 128], BF16)
    ecap = sbp("ecap", [128, 32], F32)
    lng = sbp("lng", [128, 2, 1024], F32)
    lnb = sbp("lnb", [128, 2, 1024], F32)
    brt = sbp("brt", [128, 32], F32)
    wr32 = sbp("wr32", [128, 8, 32], F32)
    base = sbp("base", [128, 32], F32)
    desti = sbp("desti", [128, NT, 4], I32)
    gkall = sbp("gkall", [128, NT, 4], F32)
    epst = sbp("epst", [128, 1], F32)
    bgT = sbp("bgT", [128, 256], F32)
    buT = sbp("buT", [128, 256], F32)

    dma("pool", lambda e: e.dma_start(out=identbf[:], in_=io["c_ident"]), writes=[K("identbf")])
    dma("sp", lambda e: e.dma_start(out=ident32[:], in_=io["c_ident"]), writes=[K("ident32")])
    dma("pool", lambda e: e.dma_start(out=upper[:], in_=io["c_upper"]), writes=[K("upper")])
    dma("sp", lambda e: e.dma_start(out=ecap[:], in_=io["c_ecap"]), writes=[K("ecap")])
    iota = sbp("iota", [128, 32], F32)
    dma("sp", lambda e: e.dma_start(out=iota[:], in_=io["c_iota"]), writes=[K("iota")])
    for j in range(2):
        dma("sp", lambda e, j=j: e.dma_start(out=lng[:, j, :], in_=io["ln_g"][j:j + 1, :].partition_broadcast(128)),
            writes=[K("lng%d" % j)])
        dma("sp", lambda e, j=j: e.dma_start(out=lnb[:, j, :], in_=io["ln_b"][j:j + 1, :].partition_broadcast(128)),
            writes=[K("lnb%d" % j)])
    dma("sp", lambda e: e.dma_start(out=brt[:], in_=io["b_router"].partition_broadcast(128)), writes=[K("brt")])
    dma("sp", lambda e: e.dma_start(out=wr32[:], in_=io["w_router"].rearrange("(kc p) n -> p kc n", p=128)),
        writes=[K("wr32")])
    op("dve", lambda e: e.memset(onesbf[:], 1.0), writes=[K("onesbf")])
    op("dve", lambda e: e.memset(base[:], 0.0), writes=[K("base")])
    op("dve", lambda e: e.memset(epst[:], LN_EPS), writes=[K("epst")])

    def layer_norm(pre, sums, j, out, tmp, tagk, small, extra_out_keys=()):
        negmean, ssq, lnv, rstd, nb = small
        kp, ko, kt = K(tagk + "pre"), K(tagk + "out"), K(tagk + "tmp")
        ks = K(tagk + "small")
        op("dve", lambda e: e.tensor_tensor(out=negmean[:], in0=sums[:, 0:1], in1=sums[:, 1:2], op=ALU.add),
           reads=[K(tagk + "sums")], writes=[ks + "nm"])
        op("dve", lambda e: e.tensor_scalar(out=negmean[:], in0=negmean[:], scalar1=-1.0 / 1024.0, scalar2=None,
                                            op0=ALU.mult), reads=[ks + "nm"], writes=[ks + "nm"])
        op("act", lambda e: e.activation(out=tmp[:], in_=pre[:], func=AF.Square, bias=negmean[:, 0:1], scale=1.0,
                                         accum_out=ssq[:]), reads=[kp, ks + "nm"], writes=[kt, ks + "ssq"])
        op("act", lambda e: e.activation(out=lnv[:], in_=ssq[:], func=AF.Ln, bias=epst[:, 0:1], scale=1.0 / 1024.0),
           reads=[ks + "ssq", K("epst")], writes=[ks + "lnv"])
        op("act", lambda e: e.activation(out=rstd[:], in_=lnv[:], func=AF.Exp, scale=-0.5),
           reads=[ks + "lnv"], writes=[ks + "rstd"])
        op("dve", lambda e: e.tensor_tensor(out=nb[:], in0=negmean[:], in1=rstd[:], op=ALU.mult),
           reads=[ks + "nm", ks + "rstd"], writes=[ks + "nb"])
        op("act", lambda e: e.activation(out=tmp[:], in_=pre[:], func=AF.Identity, bias=nb[:, 0:1], scale=rstd[:, 0:1]),
           reads=[kp, ks + "nb", ks + "rstd"], writes=[kt])
        op("dve", lambda e: e.tensor_tensor(out=tmp[:], in0=tmp[:], in1=lng[:, j, :], op=ALU.mult),
           reads=[kt, K("lng%d" % j)], writes=[kt])
        op("dve", lambda e: e.tensor_tensor(out=out[:], in0=tmp[:], in1=lnb[:, j, :], op=ALU.add),
           reads=[kt, K("lnb%d" % j)], writes=[ko] + list(extra_out_keys))

    with contextlib.ExitStack() as es:
        def sb(name, shape, dt):
            return es.enter_context(nc.sbuf_tensor(pfx + name, shape, dt))
        wg = sb("wg", [128, 8, 2048], BF16)
        wo = sb("wo", [128, 8, 1024], BF16)
        xt_ = [sb("x%d" % i, [128, 1024], F32) for i in range(3)]
        oat = [sb("oa%d" % i, [128, 1024], F32) for i in range(2)]
        obt = [sb("ob%d" % i, [128, 1024], F32) for i in range(2)]
        xTt = [sb("xT%d" % i, [128, 8, 128], BF16) for i in range(2)]
        sig = sb("sig", [128, 2048], F32)
        m1 = sb("m1", [128, 1024], F32)
        m2 = sb("m2", [128, 1024], F32)
        mg = [sb("mg%d" % i, [128, 1024], BF16) for i in range(2)]
        mT = sb("mT", [128, 8, 128], BF16)
        pre = sb("pre", [128, 1024], F32)
        tmp = sb("tmp", [128, 1024], F32)
        hh = [sb("h%d" % i, [128, 1024], F32) for i in range(2)]
        hbf = [sb("hbf%d" % i, [128, 1024], BF16) for i in range(3)]
        hT32 = sb("hT32", [128, 8, 128], F32)
        sums = sb("sums", [128, 2], F32)
        small = [sb("sm%d" % i, [128, 1], F32) for i in range(5)]
        logit = [sb("logit%d" % i, [128, 32], F32) for i in range(2)]
        m8 = sb("m8", [128, 8], F32)
        negm0 = sb("negm0", [128, 1], F32)
        exl = sb("exl", [128, 32], F32)
        mask = sb("mask", [128, 32], F32)
        maskbf = sb("maskbf", [128, 32], BF16)
        gun = sb("gun", [128, 32], F32)
        den = sb("den", [128, 1], F32)
        gd = sb("gd", [128, 32], F32)
        slotf = sb("slotf", [128, 32], F32)
        oh4 = sb("oh4", [128, 4, 32], F32)
        i8u = sb("i8u", [128, 8], U32)
        idxf = sb("idxf", [128, 8], F32)
        junk32 = sb("junk32", [128, 32], F32)
        destf = sb("destf", [128, 4], F32)

        dma("pool", lambda e: e.dma_start(out=wg[:, 0:4], in_=io["wg"].rearrange("(kc p) n -> p kc n", p=128)[:, 0:4]),
            writes=[K("wg_a")])
        dma("pool", lambda e: e.dma_start(out=wg[:, 4:8], in_=io["wg"].rearrange("(kc p) n -> p kc n", p=128)[:, 4:8]),
            writes=[K("wg_b")])
        dma("pool", lambda e: e.dma_start(out=wo[:], in_=io["w_o"].rearrange("(kc p) n -> p kc n", p=128)),
            writes=[K("wo")])
        xT_v = io["xT"].rearrange("(kc p) t -> p kc t", p=128)
        zt = sb("zt", [128, CT, 1024], BF16)
        op("pool", lambda e: e.memset(zt[:], 0.0), writes=[K("zt")])
        for ex in range(N_EXPERTS):
            dma("sp", lambda e: e.dma_start(out=io["xbuf"][ex * C:(ex + 1) * C, :].rearrange("(t p) d -> p t d", p=128),
                                            in_=zt[:]), reads=[K("zt")], writes=[K("xz%d" % ex)], semkey=K("xz"))

        def loads(i):
            b = i % 2
            r = slice(i * 128, (i + 1) * 128)
            dma("pool", lambda e: e.dma_start(out=xTt[b][:], in_=xT_v[:, :, r]), writes=[K("xT%d" % b)])
            b3 = i % 3
            dma("sp", lambda e: e.dma_start(out=xt_[b3][:], in_=io["x"][r, :]), writes=[K("x%d" % b3)])
            dma("sp", lambda e: e.dma_start(out=oat[b][:], in_=io["oa"][r, :]), writes=[K("oa%d" % b)])
            dma("sp", lambda e: e.dma_start(out=obt[b][:], in_=io["ob"][r, :]), writes=[K("ob%d" % b)])

        def stage1(i):
            b = i % 2
            for cb in range(4):
                bank, bk = pbank()
                for kc in range(8):
                    op("pe", lambda e: e.matmul(bank[:], lhsT=xTt[b][:, kc, :], rhs=wg[:, kc, cb * 512:(cb + 1) * 512],
                                                start=(kc == 0), stop=(kc == 7)),
                       reads=[K("xT%d" % b), K("wg_a"), K("wg_b")], writes=[bk])
                op("act", lambda e: e.activation(out=sig[:, cb * 512:(cb + 1) * 512], in_=bank[:], func=AF.Sigmoid),
                   reads=[bk], writes=[K("sig%d" % cb)])
            op("dve", lambda e: e.tensor_tensor(out=m1[:], in0=sig[:, 0:1024], in1=oat[b][:], op=ALU.mult),
               reads=[K("sig0"), K("sig1"), K("oa%d" % b)], writes=[K("m1")])
            op("dve", lambda e: e.tensor_tensor(out=m2[:], in0=sig[:, 1024:2048], in1=obt[b][:], op=ALU.mult),
               reads=[K("sig2"), K("sig3"), K("ob%d" % b)], writes=[K("m2")])
            op("dve", lambda e: e.tensor_tensor(out=mg[b][:], in0=m1[:], in1=m2[:], op=ALU.add),
               reads=[K("m1"), K("m2")], writes=[K("mg%d" % b)])

        def stage2(i):
            b = i % 2
            b3 = i % 3
            bank, bk = pbank()
            bankbf = bank[:].bitcast(BF16)
            for kc in range(8):
                op("pe", lambda e: e.transpose(out=bankbf[:, kc * 128:(kc + 1) * 128],
                                               in_=mg[b][:, kc * 128:(kc + 1) * 128], identity=identbf[:]),
                   reads=[K("mg%d" % b), K("identbf")], writes=[bk])
            op("dve", lambda e: e.tensor_copy(out=mT[:].rearrange("p k t -> p (k t)"), in_=bankbf),
               reads=[bk], writes=[K("mT")])
            for dh in range(2):
                bank, bk = pbank()
                for kc in range(8):
                    op("pe", lambda e: e.matmul(bank[:], lhsT=mT[:, kc, :], rhs=wo[:, kc, dh * 512:(dh + 1) * 512],
                                                start=(kc == 0), stop=(kc == 7)),
                       reads=[K("mT"), K("wo")], writes=[bk])
                op("dve", lambda e: e.scalar_tensor_tensor(
                    out=pre[:, dh * 512:(dh + 1) * 512], in0=xt_[b3][:, dh * 512:(dh + 1) * 512], scalar=alpha,
                    in1=bank[:], op0=ALU.mult, op1=ALU.add, accum_out=sums[:, dh:dh + 1]),
                   reads=[bk, K("x%d" % b3)], writes=[K("l1pre"), K("l1sums")])
            h = hh[b]
            layer_norm(pre, sums, 0, h, tmp, "l1", small, extra_out_keys=[K("h%d" % b)])
            dma("sp", lambda e: e.dma_start(out=io["hbuf"][i * 128:(i + 1) * 128, :], in_=h[:]),
                reads=[K("l1out"), K("h%d" % b)], writes=[K("hbuf%d" % i)], semkey=K("hst%d" % b))
            op("act", lambda e: e.activation(out=hbf[b3][:], in_=h[:], func=AF.Copy), reads=[K("l1out"), K("h%d" % b)],
               writes=[K("hbf%d" % b3)])

        def stage3(i):
            b = i % 2
            h = hh[b]
            for half in range(2):
                bank, bk = pbank()
                for q in range(4):
                    kc = half * 4 + q
                    op("pe", lambda e: e.transpose(out=bank[:, q * 128:(q + 1) * 128], in_=h[:, kc * 128:(kc + 1) * 128],
                                                   identity=ident32[:]),
                       reads=[K("h%d" % b), K("ident32")], writes=[bk])
                op("dve", lambda e: e.tensor_copy(
                    out=hT32[:, half * 4:(half + 1) * 4, :].rearrange("p k t -> p (k t)"), in_=bank[:]),
                   reads=[bk], writes=[K("hT32_%d" % half)])
            bank, bk = pbank()
            for kc in range(8):
                op("pe", lambda e: e.matmul(bank[:, 0:32], lhsT=hT32[:, kc, :], rhs=wr32[:, kc, :],
                                            start=(kc == 0), stop=(kc == 7)),
                   reads=[K("hT32_0"), K("hT32_1"), K("wr32")], writes=[bk])
            lg = logit[b]
            op("dve", lambda e: e.tensor_tensor(out=lg[:], in0=bank[:, 0:32], in1=brt[:], op=ALU.add),
               reads=[bk, K("brt")], writes=[K("logit%d" % b)])

        def stage4(i):
            b = i % 2
            b3 = i % 3
            lg = logit[b]
            lk = K("logit%d" % b)
            op("dve", lambda e: e.max(out=m8[:], in_=lg[:]), reads=[lk], writes=[K("m8")])
            op("dve", lambda e: e.tensor_scalar(out=negm0[:], in0=m8[:, 0:1], scalar1=-1.0, scalar2=None, op0=ALU.mult),
               reads=[K("m8")], writes=[K("negm0")])
            op("act", lambda e: e.activation(out=exl[:], in_=lg[:], func=AF.Exp, bias=negm0[:, 0:1], scale=1.0),
               reads=[lk, K("negm0")], writes=[K("exl")])
            op("dve", lambda e: e.max_index(out=i8u[:], in_max=m8[:], in_values=lg[:]), reads=[lk, K("m8")],
               writes=[K("i8u")])
            op("dve", lambda e: e.tensor_copy(out=idxf[:], in_=i8u[:]), reads=[K("i8u")], writes=[K("idxf")])
            for k in range(4):
                op("dve", lambda e: e.tensor_scalar(out=oh4[:, k, :], in0=iota[:], scalar1=idxf[:, k:k + 1], scalar2=None,
                                                    op0=ALU.is_equal), reads=[K("iota"), K("idxf")], writes=[K("oh4_%d" % k)])
            op("dve", lambda e: e.tensor_tensor(out=mask[:], in0=oh4[:, 0, :], in1=oh4[:, 1, :], op=ALU.add),
               reads=[K("oh4_0"), K("oh4_1")], writes=[K("mask")])
            op("dve", lambda e: e.tensor_tensor(out=mask[:], in0=mask[:], in1=oh4[:, 2, :], op=ALU.add),
               reads=[K("mask"), K("oh4_2")], writes=[K("mask")])
            op("dve", lambda e: e.tensor_tensor(out=mask[:], in0=mask[:], in1=oh4[:, 3, :], op=ALU.add),
               reads=[K("mask"), K("oh4_3")], writes=[K("mask")])
            op("dve", lambda e: e.scalar_tensor_tensor(out=gun[:], in0=exl[:], scalar=1.0, in1=mask[:], op0=ALU.mult,
                                                       op1=ALU.mult, accum_out=den[:]),
               reads=[K("exl"), K("mask")], writes=[K("gun"), K("den")])
            op("dve", lambda e: e.reciprocal(out=den[:], in_=den[:]), reads=[K("den")], writes=[K("den")])
            op("dve", lambda e: e.tensor_scalar(out=gd[:], in0=gun[:], scalar1=den[:, 0:1], scalar2=None, op0=ALU.mult),
               reads=[K("gun"), K("den")], writes=[K("gd")])
            op("dve", lambda e: e.tensor_copy(out=maskbf[:], in_=mask[:]), reads=[K("mask")], writes=[K("maskbf")])
            bank, bk = pbank()
            op("pe", lambda e: e.matmul(bank[:, 0:32], lhsT=upper[:], rhs=maskbf[:], start=True, stop=True),
               reads=[K("upper"), K("maskbf")], writes=[bk])
            op("pe", lambda e: e.matmul(bank[:, 32:64], lhsT=onesbf[:], rhs=maskbf[:], start=True, stop=True),
               reads=[K("onesbf"), K("maskbf")], writes=[bk])
            op("dve", lambda e: e.tensor_tensor(out=slotf[:], in0=bank[:, 0:32], in1=base[:], op=ALU.add),
               reads=[bk, K("base")], writes=[K("slotf")])
            op("dve", lambda e: e.scalar_tensor_tensor(out=slotf[:], in0=slotf[:], scalar=float(C - 1), in1=ecap[:],
                                                       op0=ALU.min, op1=ALU.add),
               reads=[K("slotf"), K("ecap")], writes=[K("slotf")])
            op("dve", lambda e: e.tensor_tensor(out=base[:], in0=bank[:, 32:64], in1=base[:], op=ALU.add),
               reads=[bk, K("base")], writes=[K("base")])
            for k in range(4):
                op("dve", lambda e: e.scalar_tensor_tensor(out=junk32[:], in0=oh4[:, k, :], scalar=1.0, in1=slotf[:],
                                                           op0=ALU.mult, op1=ALU.mult, accum_out=destf[:, k:k + 1]),
                   reads=[K("oh4_%d" % k), K("slotf")], writes=[K("junk32"), K("destf")])
                op("dve", lambda e: e.scalar_tensor_tensor(out=junk32[:], in0=oh4[:, k, :], scalar=1.0, in1=gd[:],
                                                           op0=ALU.mult, op1=ALU.mult, accum_out=gkall[:, i, k:k + 1]),
                   reads=[K("oh4_%d" % k), K("gd")], writes=[K("junk32"), K("gk%d" % i)])
            op("dve", lambda e: e.tensor_copy(out=desti[:, i, :], in_=destf[:]), reads=[K("destf")],
               writes=[K("desti%d" % i)])
            for k in range(4):
                dma("pool", lambda e: e.indirect_dma_start(
                    out=io["xbuf"][:, :], out_offset=bass.IndirectOffsetOnAxis(ap=desti[:, i, k:k + 1], axis=0),
                    in_=hbf[b3][:, :], in_offset=None),
                    reads=[K("hbf%d" % b3), K("desti%d" % i), K("xz%d" % (N_EXPERTS - 1))],
                    writes=[K("xbuf_%d_%d" % (i, k))], semkey=K("hsc%d" % b3))

        loads(0)
        if NT > 1:
            loads(1)
        for t in range(NT + 3):
            if t < NT:
                stage1(t)
            if 0 <= t - 1 < NT:
                stage2(t - 1)
            if t + 2 < NT:
                loads(t + 2)
            if 0 <= t - 2 < NT:
                stage3(t - 2)
            if 0 <= t - 3 < NT:
                stage4(t - 3)
        kb_barrier(kb)

    with contextlib.ExitStack() as es:
        def sb(name, shape, dt):
            return es.enter_context(nc.sbuf_tensor(pfx + name, shape, dt))
        wup = [sb("wup%d" % i, [128, 8, 2048], BF16) for i in range(2)]
        wdn = [sb("wdn%d" % i, [128, 8, 1024], BF16) for i in range(2)]
        bdn = [sb("bdn%d" % i, [128, 1024], F32) for i in range(2)]
        xrows = [sb("xrows%d" % i, [128, CT, 1024], BF16) for i in range(2)]
        xTe = sb("xTe", [128, 8, C], BF16)
        actT = sb("actT", [128, 8, C], BF16)
        gsb = [sb("gsb%d" % i, [128, CH], F32) for i in range(2)]
        sgb = [sb("sgb%d" % i, [128, CH], F32) for i in range(2)]
        usb = [sb("usb%d" % i, [128, CH], F32) for i in range(2)]
        ysb = [sb("ysb%d" % i, [128, 1024], F32) for i in range(2)]
        bupr = [sb("bupr%d" % i, [128, 256], F32) for i in range(2)]

        bup_v = io["b_up"].rearrange("e (fo n) -> (e fo) n", n=256)
        for j in range(2):
            dma("sp", lambda e, j=j: e.dma_start(out=bupr[j][:], in_=bup_v[j * 128:(j + 1) * 128, :]),
                writes=[K("bupr%d" % j)])
        for gu, dst in ((0, bgT), (1, buT)):
            bank, bk = pbank()
            for j in range(2):
                src = bupr[j][:].rearrange("p (f two) -> p f two", two=2)[:, :, gu]
                op("pe", lambda e, j=j, src=src, bank=bank: e.transpose(out=bank[:, j * 128:(j + 1) * 128], in_=src,
                                                                       identity=ident32[:]),
                   reads=[K("bupr%d" % j), K("ident32")], writes=[bk])
            op("dve", lambda e, bank=bank, dst=dst: e.tensor_copy(out=dst[:], in_=bank[:, 0:256]), reads=[bk],
               writes=[K("bgu%d" % gu)])
        b7g = sb("b7g", [128, 256], F32)
        bu1 = sb("bu1", [128, 256], F32)
        c7 = sb("c7", [128, 1], F32)
        op("dve", lambda e: e.tensor_scalar(out=b7g[:], in0=bgT[:], scalar1=-1.0, scalar2=7.0, op0=ALU.mult, op1=ALU.add),
           reads=[K("bgu0")], writes=[K("b7g")])
        op("dve", lambda e: e.tensor_scalar(out=bu1[:], in0=buT[:], scalar1=1.0, scalar2=None, op0=ALU.add),
           reads=[K("bgu1")], writes=[K("bu1")])
        op("dve", lambda e: e.memset(c7[:], 7.0 * 1.702), writes=[K("c7")])

        def wloads(ex):
            wb = ex % 2
            upv = io["w_up"][ex].rearrange("(kc p) n -> p kc n", p=128)
            for q in range(4):
                dma("pool", lambda e, q=q: e.dma_start(out=wup[wb][:, 2 * q:2 * q + 2], in_=upv[:, 2 * q:2 * q + 2]),
                    writes=[K("wup%d_%d" % (wb, q))])
            dnv = io["w_down"][ex].rearrange("(kc p) n -> p kc n", p=128)
            for q in range(2):
                dma("pool", lambda e, q=q: e.dma_start(out=wdn[wb][:, 4 * q:4 * q + 4], in_=dnv[:, 4 * q:4 * q + 4]),
                    writes=[K("wdn%d_%d" % (wb, q))])
            dma("sp", lambda e: e.dma_start(out=bdn[wb][:], in_=io["b_down"][ex:ex + 1, :].partition_broadcast(128)),
                writes=[K("bdn%d" % wb)])
            dma("sp", lambda e: e.dma_start(out=xrows[wb][:],
                                            in_=io["xbuf"][ex * C:(ex + 1) * C, :].rearrange("(t p) d -> p t d", p=128)),
                writes=[K("xrows%d" % wb)])

        tstate = {"i": 0}

        def expert(ex):
            wb = ex % 2
            wupv = wup[wb][:].rearrange("p k (f two) -> p k f two", two=2)
            for t in range(CT):
                bank, bk = pbank()
                bankbf = bank[:].bitcast(BF16)
                for kc in range(8):
                    op("pe", lambda e, kc=kc, t=t, bankbf=bankbf: e.transpose(
                        out=bankbf[:, kc * 128:(kc + 1) * 128], in_=xrows[wb][:, t, kc * 128:(kc + 1) * 128],
                        identity=identbf[:]), reads=[K("xrows%d" % wb), K("identbf")], writes=[bk])
                op("dve", lambda e, t=t, bankbf=bankbf: e.tensor_copy(
                    out=xTe[:, :, t * 128:(t + 1) * 128], in_=bankbf.rearrange("p (k q) -> p k q", k=8)),
                   reads=[bk], writes=[K("xTe%d" % t)])
            xkeys = [K("xTe%d" % t) for t in range(CT)]
            for hf in range(NH):
                cs = slice(hf * CH, (hf + 1) * CH)
                for fo in range(8):
                    banks = []
                    for gu in range(2):
                        bank, bk = pbank()
                        banks.append((bank, bk))
                        for kc in range(8):
                            op("pe", lambda e, kc=kc, gu=gu, bank=bank: e.matmul(
                                bank[:, 0:CH], lhsT=wupv[:, kc, fo * 128:(fo + 1) * 128, gu], rhs=xTe[:, kc, cs],
                                start=(kc == 0), stop=(kc == 7)),
                               reads=xkeys + [K("wup%d_%d" % (wb, kc // 2))], writes=[bk])
                    tb = tstate["i"] % 2
                    tstate["i"] += 1
                    g_, s_, u_ = gsb[tb], sgb[tb], usb[tb]
                    col = ex * 8 + fo
                    (bg_, bgk), (bu_, buk) = banks
                    op("act", lambda e: e.activation(out=g_[:], in_=bg_[:, 0:CH], func=AF.Relu,
                                                     bias=b7g[:, col:col + 1], scale=-1.0),
                       reads=[bgk, K("b7g")], writes=[K("g%d" % tb)])
                    op("act", lambda e: e.activation(out=s_[:], in_=g_[:], func=AF.Silu, bias=c7[:, 0:1], scale=-1.702),
                       reads=[K("g%d" % tb), K("c7")], writes=[K("s%d" % tb)])
                    op("dve", lambda e: e.tensor_scalar(out=u_[:], in0=bu_[:, 0:CH], scalar1=bu1[:, col:col + 1],
                                                        scalar2=8.0, op0=ALU.add, op1=ALU.min),
                       reads=[buk, K("bu1")], writes=[K("u%d" % tb)])
                    op("dve", lambda e: e.scalar_tensor_tensor(out=actT[:, fo, cs], in0=u_[:], scalar=-6.0, in1=s_[:],
                                                               op0=ALU.max, op1=ALU.mult),
                       reads=[K("u%d" % tb), K("s%d" % tb)], writes=[K("actT%d_%d" % (hf, fo))])
            akeys = [K("actT%d_%d" % (hf, fo)) for hf in range(NH) for fo in range(8)]
            for t in range(CT):
                yb = (ex * CT + t) % 2
                y_ = ysb[yb]
                for dh in range(2):
                    bank, bk = pbank()
                    for fo in range(8):
                        op("pe", lambda e, fo=fo, bank=bank: e.matmul(
                            bank[:], lhsT=actT[:, fo, t * 128:(t + 1) * 128], rhs=wdn[wb][:, fo, dh * 512:(dh + 1) * 512],
                            start=(fo == 0), stop=(fo == 7)),
                           reads=akeys + [K("wdn%d_%d" % (wb, fo // 4))], writes=[bk])
                    op("dve", lambda e, bank=bank, y_=y_: e.scalar_tensor_tensor(
                        out=y_[:, dh * 512:(dh + 1) * 512], in0=bank[:], scalar=1.0 / 1.702,
                        in1=bdn[wb][:, dh * 512:(dh + 1) * 512], op0=ALU.mult, op1=ALU.add),
                       reads=[bk, K("bdn%d" % wb)], writes=[K("ysb%d_%d" % (yb, dh))])
                r0 = ex * C + t * 128
                dma("sp", lambda e, y_=y_, r0=r0: e.dma_start(out=io["ybuf"][r0:r0 + 128, :], in_=y_[:]),
                    reads=[K("ysb%d_0" % yb), K("ysb%d_1" % yb)], writes=[K("ybuf_%d" % r0)], semkey=K("yst%d" % yb))

        wloads(0)
        for ex in range(N_EXPERTS):
            if ex + 1 < N_EXPERTS:
                wloads(ex + 1)
            expert(ex)
        kb_barrier(kb)

    with contextlib.ExitStack() as es:
        def sb(name, shape, dt):
            return es.enter_context(nc.sbuf_tensor(pfx + name, shape, dt))
        hb = [sb("hb%d" % i, [128, 1024], F32) for i in range(2)]
        yk = [[sb("yk%d_%d" % (i, k), [128, 1024], F32) for k in range(4)] for i in range(2)]
        ff = sb("ff", [128, 1024], F32)
        pre = sb("pre2", [128, 1024], F32)
        tmp = sb("tmp2", [128, 1024], F32)
        outt = [sb("outt%d" % i, [128, 1024], F32) for i in range(2)]
        sums = sb("sums2", [128, 2], F32)
        small = [sb("sm2_%d" % i, [128, 1], F32) for i in range(5)]

        def loads3(i):
            b = i % 2
            dma("sp", lambda e: e.dma_start(out=hb[b][:], in_=io["hbuf"][i * 128:(i + 1) * 128, :]), writes=[K("hb%d" % b)])
            for k in range(4):
                dma("pool", lambda e, k=k: e.indirect_dma_start(
                    out=yk[b][k][:, :], out_offset=None, in_=io["ybuf"][:, :],
                    in_offset=bass.IndirectOffsetOnAxis(ap=desti[:, i, k:k + 1], axis=0)),
                    reads=[K("desti%d" % i)], writes=[K("yk%d_%d" % (b, k))])

        def tile3(i):
            b = i % 2
            for k in range(4):
                if k == 0:
                    op("dve", lambda e: e.tensor_scalar(out=ff[:], in0=yk[b][0][:], scalar1=gkall[:, i, 0:1], scalar2=None,
                                                        op0=ALU.mult), reads=[K("yk%d_0" % b), K("gk%d" % i)],
                       writes=[K("ff")])
                else:
                    op("dve", lambda e, k=k: e.scalar_tensor_tensor(out=ff[:], in0=yk[b][k][:], scalar=gkall[:, i, k:k + 1],
                                                                    in1=ff[:], op0=ALU.mult, op1=ALU.add),
                       reads=[K("yk%d_%d" % (b, k)), K("gk%d" % i), K("ff")], writes=[K("ff")])
            for dh in range(2):
                op("dve", lambda e, dh=dh: e.scalar_tensor_tensor(
                    out=pre[:, dh * 512:(dh + 1) * 512], in0=hb[b][:, dh * 512:(dh + 1) * 512], scalar=alpha,
                    in1=ff[:, dh * 512:(dh + 1) * 512], op0=ALU.mult, op1=ALU.add, accum_out=sums[:, dh:dh + 1]),
                   reads=[K("hb%d" % b), K("ff")], writes=[K("l2pre"), K("l2sums")])
            o_ = outt[b]
            layer_norm(pre, sums, 1, o_, tmp, "l2", small)
            dma("sp", lambda e: e.dma_start(out=io["xo"][i * 128:(i + 1) * 128, :], in_=o_[:]),
                reads=[K("l2out")], writes=[K("xo_out")], semkey=K("ost%d" % b))

        loads3(0)
        for i in range(NT):
            if i + 1 < NT:
                loads3(i + 1)
            tile3(i)
    return [K("xo_out")]


def tok_decl(nc, T, C, pfx=""):
    io = {}
    def di(name, shape):
        io[name] = nc.dram_tensor(pfx + name, shape, F32, kind="ExternalInput").ap()
    di("x", [T, 1024]); di("xT", [1024, T]); di("oa", [T, 1024]); di("ob", [T, 1024])
    di("wg", [1024, 2048]); di("w_o", [1024, 1024]); di("ln_g", [2, 1024]); di("ln_b", [2, 1024])
    di("w_router", [1024, 32]); di("b_router", [1, 32])
    di("w_up", [32, 1024, 2048]); di("b_up", [32, 2048]); di("w_down", [32, 1024, 1024]); di("b_down", [32, 1024])
    di("c_ident", [128, 128]); di("c_upper", [128, 128]); di("c_ecap", [128, 32]); di("c_iota", [128, 32])
    io["xo"] = nc.dram_tensor(pfx + "xo", [T, 1024], F32, kind="ExternalOutput").ap()
    io["hbuf"] = nc.dram_tensor(pfx + "hbuf", [T, 1024], F32, kind="Internal").ap()
    io["xbuf"] = nc.dram_tensor(pfx + "xbuf", [32 * C, 1024], BF16, kind="Internal").ap()
    io["ybuf"] = nc.dram_tensor(pfx + "ybuf", [32 * C, 1024], F32, kind="Internal").ap()
    return io


def build_tok_program(T, C):
    nc = bass.Bass("TRN2", target_bir_lowering=False)
    io = tok_decl(nc, T, C)
    kb = KB(nc)
    outs = emit_tok(nc, kb, T, C, io)
    kb.final_wait("sp", outs)
    kb.emit()
    return nc


def tok_inputs(layer, x_c, xT_c, oa_c, ob_c, inp, C):
    m = {
        "x": x_c, "xT": xT_c, "oa": oa_c, "ob": ob_c,
        "wg": np.ascontiguousarray(inp["w_in"][layer][:, 4352:6400]),
        "w_o": inp["w_o"][layer], "ln_g": inp["ln_g"][layer], "ln_b": inp["ln_b"][layer],
        "w_router": inp["w_router"][layer], "b_router": np.ascontiguousarray(inp["b_router"][layer].reshape(1, 32)),
        "w_up": inp["w_up"][layer], "b_up": inp["b_up"][layer], "w_down": inp["w_down"][layer],
        "b_down": inp["b_down"][layer],
    }
    m.update(tok_consts(C))
    return m


CAP = 768
_PROG_CACHE = {}


def _attn_prog(layer):
    key = ("attn", layer)
    if key not in _PROG_CACHE:
        _PROG_CACHE[key] = build_attn_program(SEQ, layer)
    return _PROG_CACHE[key]


def _tok_prog():
    key = ("tok",)
    if key not in _PROG_CACHE:
        _PROG_CACHE[key] = build_tok_program(BATCH * SEQ // 8, CAP)
    return _PROG_CACHE[key]


def kernel(x, w_in, w_o, lambda_qk, subln_g, sinks, ln_g, ln_b, w_router, b_router, w_up, b_up, w_down, b_down):
    f32 = lambda a: np.ascontiguousarray(np.asarray(a, dtype=np.float32))
    inp = {"w_in": f32(w_in), "w_o": f32(w_o), "ln_g": f32(ln_g), "ln_b": f32(ln_b), "w_router": f32(w_router),
           "b_router": f32(b_router), "w_up": f32(w_up), "b_up": f32(b_up), "w_down": f32(w_down),
           "b_down": f32(b_down)}
    lambda_qk, subln_g, sinks = f32(lambda_qk), f32(subln_g), f32(sinks)
    xcur = f32(x)
    T = BATCH * SEQ // 8
    for layer in range(DEPTH):
        xTs = [np.ascontiguousarray(xcur[b].T) for b in range(BATCH)]
        in_maps = [attn_inputs(layer, c % 4, xTs[c // 4], inp["w_in"], lambda_qk, subln_g, sinks) for c in range(8)]
        res = run_bass_kernel_spmd(_attn_prog(layer), in_maps, core_ids=list(range(8)))
        oa = np.empty((BATCH, SEQ, D_MODEL), np.float32)
        ob = np.empty((BATCH, SEQ, D_MODEL), np.float32)
        for c in range(8):
            b, h = c // 4, c % 4
            oa[b, :, h * 256:(h + 1) * 256] = res.results[c]["oa"]
            ob[b, :, h * 256:(h + 1) * 256] = res.results[c]["ob"]
        del res, in_maps, xTs
        xf = xcur.reshape(-1, D_MODEL)
        oaf = oa.reshape(-1, D_MODEL)
        obf = ob.reshape(-1, D_MODEL)
        in_maps = []
        for c in range(8):
            sl = slice(c * T, (c + 1) * T)
            in_maps.append(tok_inputs(layer, np.ascontiguousarray(xf[sl]), np.ascontiguousarray(xf[sl].T),
                                      np.ascontiguousarray(oaf[sl]), np.ascontiguousarray(obf[sl]), inp, CAP))
        res = run_bass_kernel_spmd(_tok_prog(), in_maps, core_ids=list(range(8)))
        xcur = np.concatenate([res.results[c]["xo"] for c in range(8)], axis=0).reshape(BATCH, SEQ, D_MODEL)
        del res, in_maps
    return xcur
```
